# Optimizing a Trainium2 kernel written in Bass

```python
import math
import jax
import jax.numpy as jnp
from jax import lax
import numpy as np

D_MODEL = 1024
BATCH = 1
SEQ = 16384
DEPTH = 4

GRID_W = 64
CTX_LEN = 256
D_FF = 2816
N_MOD = 9
N_BRANCH = 3
FOURIER_GROUPS = 4
FOURIER_CH = 64
RET_HEADS = 6
RET_QK = 32
RET_V = 64
RET_CHUNK = 128
DIFF_HEADS = 6
DIFF_QK = 32
DIFF_V = 64
Q_BLOCK = 128
ROPE_BASE = 10000.0
EPS = 1e-6
F_W = FOURIER_GROUPS * FOURIER_CH
RET_QW = RET_HEADS * RET_QK
RET_VW = RET_HEADS * RET_V
DIFF_QW = DIFF_HEADS * 2 * DIFF_QK
DIFF_VW = DIFF_HEADS * DIFF_V
D_IN = F_W + 2 * RET_QW + 2 * RET_VW + 2 * DIFF_QW + DIFF_VW + N_BRANCH * D_MODEL

kernel_name = "hybrid_fourier_retention_diffattn_dit"


def split_points():
    sizes = (F_W, RET_QW, RET_QW, RET_VW, RET_VW, DIFF_QW, DIFF_QW, DIFF_VW)
    out, acc = [], 0
    for s in sizes:
        acc += s
        out.append(acc)
    return out


def rms_norm(x, g=None):
    xf = x.astype(jnp.float32)
    y = xf * lax.rsqrt(jnp.mean(xf * xf, axis=-1, keepdims=True) + EPS)
    if g is not None:
        y = y * g.astype(jnp.float32)
    return y.astype(x.dtype)


def modulate(h, shift, scale):
    return h * (1.0 + scale) + shift


def swiglu(h, w1, w3, w2):
    return (jax.nn.silu(h @ w1) * (h @ w3)) @ w2


def rope_angles(pos, dim):
    inv = ROPE_BASE ** (-jnp.arange(0, dim, 2, dtype=jnp.float32) / dim)
    return pos[:, None] * inv[None, :]


def apply_rope(x, ang):
    L, k = ang.shape
    shape = (1, L) + (1,) * (x.ndim - 3) + (k,)
    cos = jnp.cos(ang).reshape(shape)
    sin = jnp.sin(ang).reshape(shape)
    xf = x.astype(jnp.float32)
    x1, x2 = xf[..., :k], xf[..., k:]
    return jnp.concatenate([x1 * cos - x2 * sin, x1 * sin + x2 * cos], axis=-1).astype(x.dtype)


def axial_rope(x, ang_row, ang_col):
    half = x.shape[-1] // 2
    return jnp.concatenate([apply_rope(x[..., :half], ang_row), apply_rope(x[..., half:], ang_col)], axis=-1)


def fourier_mix(u):
    B, L, _ = u.shape
    ug = u.astype(jnp.float32).reshape(B, L, FOURIER_GROUPS, FOURIER_CH)
    y = jnp.fft.fft2(ug, axes=(1, 3), norm="ortho").real
    return y.reshape(B, L, F_W).astype(u.dtype)


def retention_scan(q, k, v, log_g, s0):
    B, L, H, dk = q.shape
    dv = v.shape[-1]
    C = RET_CHUNK
    n = L // C
    qc = q.astype(jnp.float32).reshape(B, n, C, H, dk).transpose(1, 0, 3, 2, 4)
    kc = k.astype(jnp.float32).reshape(B, n, C, H, dk).transpose(1, 0, 3, 2, 4)
    vc = v.astype(jnp.float32).reshape(B, n, C, H, dv).transpose(1, 0, 3, 2, 4)
    idx = jnp.arange(C, dtype=jnp.float32)
    diff = idx[:, None] - idx[None, :]
    mask = diff >= 0
    dmat = jnp.where(mask, jnp.exp(log_g[:, None, None] * jnp.where(mask, diff, 0.0)), 0.0)
    q_dec = jnp.exp(log_g[:, None] * (idx + 1.0))[None, :, :, None]
    k_dec = jnp.exp(log_g[:, None] * (C - 1.0 - idx))[None, :, :, None]
    chunk_dec = jnp.exp(log_g * C)[None, :, None, None]

    def step(s, inp):
        qi, ki, vi = inp
        inner = jnp.einsum('bhid,bhjd->bhij', qi, ki) * dmat
        o = jnp.einsum('bhij,bhjv->bhiv', inner, vi) + jnp.einsum('bhid,bhdv->bhiv', qi, s) * q_dec
        s = s * chunk_dec + jnp.einsum('bhjd,bhjv->bhdv', ki * k_dec, vi)
        return s, o

    _, o = lax.scan(step, s0.astype(jnp.float32), (qc, kc, vc))
    return o.transpose(1, 0, 3, 2, 4).reshape(B, L, H, dv)


def bidir_retention(q, k, v, log_g2, s0_f, s0_b):
    fwd = retention_scan(q, k, v, log_g2[0], s0_f)
    bwd = retention_scan(q[:, ::-1], k[:, ::-1], v[:, ::-1], log_g2[1], s0_b)[:, ::-1]
    return fwd + bwd


def context_states(k, v, log_g2):
    Lc = k.shape[1]
    j = jnp.arange(Lc, dtype=jnp.float32)
    w_f = jnp.exp(log_g2[0][:, None] * (Lc - 1.0 - j))
    w_b = jnp.exp(log_g2[1][:, None] * j)
    kf = k.astype(jnp.float32)
    vf = v.astype(jnp.float32)
    s_f = jnp.einsum('bjhd,hj,bjhv->bhdv', kf, w_f, vf)
    s_b = jnp.einsum('bjhd,hj,bjhv->bhdv', kf, w_b, vf)
    return s_f, s_b


def retention_readout(r, g):
    B, L, H, dv = r.shape
    gf = g.astype(jnp.float32).reshape(B, L, H, dv)
    y = rms_norm(r) * jax.nn.silu(gf)
    return y.reshape(B, L, H * dv).astype(g.dtype)


def diff_attend(q, k, v, lam):
    s = jnp.einsum('bqhmd,bkhmd->bhmqk', q, k).astype(jnp.float32) * (DIFF_QK ** -0.5)
    p = jax.nn.softmax(s, axis=-1)
    a = p[:, :, 0] - lam * p[:, :, 1]
    return jnp.einsum('bhqk,bkhv->bqhv', a.astype(v.dtype), v)


def diff_attend_blocked(q, k, v, lam):
    B, L = q.shape[:2]
    nb = L // Q_BLOCK
    qb = q.reshape((B, nb, Q_BLOCK) + q.shape[2:]).swapaxes(0, 1)
    o = lax.map(lambda blk: diff_attend(blk, k, v, lam), qb)
    return o.swapaxes(0, 1).reshape((B, L) + o.shape[3:])


def diff_readout(o, subln_w, lam_init):
    B, L, H, dv = o.shape
    y = rms_norm(o, subln_w) * (1.0 - lam_init)
    return y.reshape(B, L, H * dv)


def merge_branches(gate_logits, f_out, r_out, d_out, wbf, wbr, wbd, wo):
    g = jax.nn.sigmoid(gate_logits.astype(jnp.float32)).astype(f_out.dtype)
    gf, gr, gd = jnp.split(g, N_BRANCH, axis=-1)
    mixed = gf * (f_out @ wbf) + gr * (r_out @ wbr) + gd * (d_out @ wbd)
    return mixed @ wo


def token_mixer(h, hc, w_in_l, dec_logit, lam_vec, subln_w, wbf, wbr, wbd, wo, lam_init,
                ang_row, ang_col, ang_ret, with_ctx_out):
    B, L, _ = h.shape
    Lc = hc.shape[1]
    cuts = split_points()
    wf, wrq, wrk, wrv, wrg, wdq, wdk, wdv, wgt = jnp.split(w_in_l, cuts, axis=1)
    log_g2 = jax.nn.log_sigmoid(dec_logit.astype(jnp.float32))
    lv = lam_vec.astype(jnp.float32)
    lam = jnp.exp(jnp.sum(lv[0] * lv[1])) - jnp.exp(jnp.sum(lv[2] * lv[3])) + lam_init

    rk_c = (hc @ wrk).reshape(B, Lc, RET_HEADS, RET_QK) * (RET_QK ** -0.5)
    rv_c = (hc @ wrv).reshape(B, Lc, RET_HEADS, RET_V)
    s_f, s_b = context_states(rk_c, rv_c, log_g2)
    dk_c = (hc @ wdk).reshape(B, Lc, DIFF_HEADS, 2, DIFF_QK)
    dv_c = (hc @ wdv).reshape(B, Lc, DIFF_HEADS, DIFF_V)

    p = h @ w_in_l
    u_f, rq, rk, rv, rg, dq, dk, dv, gl = jnp.split(p, cuts, axis=-1)
    f_out = fourier_mix(u_f)
    rq = apply_rope(rq.reshape(B, L, RET_HEADS, RET_QK), ang_ret)
    rk = apply_rope(rk.reshape(B, L, RET_HEADS, RET_QK), ang_ret) * (RET_QK ** -0.5)
    rv = rv.reshape(B, L, RET_HEADS, RET_V)
    r_out = retention_readout(bidir_retention(rq, rk, rv, log_g2, s_f, s_b), rg)
    dq = axial_rope(dq.reshape(B, L, DIFF_HEADS, 2, DIFF_QK), ang_row, ang_col)
    dk = axial_rope(dk.reshape(B, L, DIFF_HEADS, 2, DIFF_QK), ang_row, ang_col)
    dv = dv.reshape(B, L, DIFF_HEADS, DIFF_V)
    k_all = jnp.concatenate([dk_c, dk], axis=1)
    v_all = jnp.concatenate([dv_c, dv], axis=1)
    d_out = diff_readout(diff_attend_blocked(dq, k_all, v_all, lam), subln_w, lam_init)
    y = merge_branches(gl, f_out, r_out, d_out, wbf, wbr, wbd, wo)

    yc = None
    if with_ctx_out:
        f_c = fourier_mix(hc @ wf)
        rq_c = (hc @ wrq).reshape(B, Lc, RET_HEADS, RET_QK)
        zeros = jnp.zeros((B, RET_HEADS, RET_QK, RET_V), jnp.float32)
        r_c = retention_readout(bidir_retention(rq_c, rk_c, rv_c, log_g2, zeros, zeros), hc @ wrg)
        dq_c = (hc @ wdq).reshape(B, Lc, DIFF_HEADS, 2, DIFF_QK)
        d_c = diff_readout(diff_attend(dq_c, dk_c, dv_c, lam), subln_w, lam_init)
        yc = merge_branches(hc @ wgt, f_c, r_c, d_c, wbf, wbr, wbd, wo)
    return y, yc


def setup_inputs(seed: int = 0) -> dict:
    key = jax.random.key(seed)
    ks = jax.random.split(key, 20)
    f32 = jnp.float32

    def nrm(k, shape, scale):
        return jax.random.normal(k, shape, f32) * scale

    base_logit = jnp.log(2.0 ** (5.0 + jnp.arange(RET_HEADS, dtype=f32)) - 1.0)
    return {
        "x": nrm(ks[0], (BATCH, SEQ, D_MODEL), 1.0),
        "c": nrm(ks[1], (BATCH, D_MODEL), 1.0),
        "ctx": nrm(ks[2], (BATCH, CTX_LEN, D_MODEL), 1.0),
        "c_ctx": nrm(ks[3], (D_MODEL,), 1.0),
        "w_ada": nrm(ks[4], (DEPTH, D_MODEL, N_MOD * D_MODEL), 0.5 * D_MODEL ** -0.5),
        "b_ada": nrm(ks[5], (DEPTH, N_MOD * D_MODEL), 0.02),
        "norm_g": 1.0 + nrm(ks[6], (DEPTH, 3, D_MODEL), 0.02),
        "ffn_w1": nrm(ks[7], (DEPTH, 2, D_MODEL, D_FF), D_MODEL ** -0.5),
        "ffn_w3": nrm(ks[8], (DEPTH, 2, D_MODEL, D_FF), D_MODEL ** -0.5),
        "ffn_w2": nrm(ks[9], (DEPTH, 2, D_FF, D_MODEL), D_FF ** -0.5),
        "w_in": nrm(ks[10], (DEPTH, D_MODEL, D_IN), D_MODEL ** -0.5),
        "ret_decay_logit": base_logit + nrm(ks[11], (DEPTH, 2, RET_HEADS), 0.01),
        "diff_lambda": nrm(ks[12], (DEPTH, 4, DIFF_QK), 0.1),
        "diff_subln": 1.0 + nrm(ks[13], (DEPTH, DIFF_V), 0.02),
        "w_branch_f": nrm(ks[14], (DEPTH, F_W, D_MODEL), F_W ** -0.5),
        "w_branch_r": nrm(ks[15], (DEPTH, RET_VW, D_MODEL), RET_VW ** -0.5),
        "w_branch_d": nrm(ks[16], (DEPTH, DIFF_VW, D_MODEL), DIFF_VW ** -0.5),
        "w_out": nrm(ks[17], (DEPTH, D_MODEL, D_MODEL), D_MODEL ** -0.5),
        "final_g": 1.0 + nrm(ks[18], (D_MODEL,), 0.02),
    }


def reference(x, c, ctx, c_ctx, w_ada, b_ada, norm_g, ffn_w1, ffn_w3, ffn_w2, w_in, ret_decay_logit,
              diff_lambda, diff_subln, w_branch_f, w_branch_r, w_branch_d, w_out, final_g):
    B, L, D = x.shape
    n_rows = L // GRID_W
    rows = jnp.repeat(jnp.arange(n_rows, dtype=jnp.float32), GRID_W)
    cols = jnp.tile(jnp.arange(GRID_W, dtype=jnp.float32), n_rows)
    ang_row = rope_angles(rows, DIFF_QK // 2)
    ang_col = rope_angles(cols, DIFF_QK // 2)
    ang_ret = rope_angles(jnp.arange(L, dtype=jnp.float32), RET_QK)
    sc = jax.nn.silu(c)
    scc = jax.nn.silu(c_ctx)
    xc = ctx
    for l in range(DEPTH):
        last = l == DEPTH - 1
        lam_init = 0.8 - 0.6 * math.exp(-0.3 * l)
        m = (sc @ w_ada[l] + b_ada[l]).reshape(B, N_MOD, 1, D)
        mc = (scc @ w_ada[l] + b_ada[l]).reshape(N_MOD, D)
        x = x + 0.5 * m[:, 2] * swiglu(modulate(rms_norm(x, norm_g[l, 0]), m[:, 0], m[:, 1]),
                                       ffn_w1[l, 0], ffn_w3[l, 0], ffn_w2[l, 0])
        xc = xc + 0.5 * mc[2] * swiglu(modulate(rms_norm(xc, norm_g[l, 0]), mc[0], mc[1]),
                                       ffn_w1[l, 0], ffn_w3[l, 0], ffn_w2[l, 0])
        h = modulate(rms_norm(x, norm_g[l, 1]), m[:, 3], m[:, 4])
        hc = modulate(rms_norm(xc, norm_g[l, 1]), mc[3], mc[4])
        y, yc = token_mixer(h, hc, w_in[l], ret_decay_logit[l], diff_lambda[l], diff_subln[l],
                            w_branch_f[l], w_branch_r[l], w_branch_d[l], w_out[l], lam_init,
                            ang_row, ang_col, ang_ret, not last)
        x = x + m[:, 5] * y
        x = x + 0.5 * m[:, 8] * swiglu(modulate(rms_norm(x, norm_g[l, 2]), m[:, 6], m[:, 7]),
                                       ffn_w1[l, 1], ffn_w3[l, 1], ffn_w2[l, 1])
        if not last:
            xc = xc + mc[5] * yc
            xc = xc + 0.5 * mc[8] * swiglu(modulate(rms_norm(xc, norm_g[l, 2]), mc[6], mc[7]),
                                           ffn_w1[l, 1], ffn_w3[l, 1], ffn_w2[l, 1])
    return rms_norm(x, final_g)
```

```python
import math
from contextlib import ExitStack

import numpy as np
import ml_dtypes

import concourse.bass as bass
import concourse.mybir as mybir
from concourse.bass_utils import run_bass_kernel_spmd

F32 = mybir.dt.float32
BF16 = mybir.dt.bfloat16
AF = mybir.ActivationFunctionType
ALU = mybir.AluOpType
AX = mybir.AxisListType

NCORES = 8
D = 1024
SEQ = 16384
DEPTH = 4
GRID_W = 64
CTX = 256
DFF = 2816
NJ = DFF // 128
OWN = SEQ // NCORES
TOK = OWN + CTX
NT = TOK // 128
NTO = OWN // 128
EPS = 1e-6
F_W = 256
RQW = 192
RVW = 384
DQW = 384
DVW = 384
C_UF = 0
C_RQ = 256
C_RK = 448
C_RV = 640
C_RG = 1024
C_DQ = 1408
C_DK = 1792
C_DV = 2176
C_GL = 2560
D_IN = 5632
NKT = (SEQ + CTX) // 128

KD = 6
ENGS = ["pe", "act", "dve", "pool", "sp"]


class Prog:
    def __init__(self, nc, es):
        self.nc = nc
        self.es = es
        self.ops = []
        self.sem = {}
        for e in ["pe", "act", "dve", "pool"]:
            self.sem[("eng", e)] = es.enter_context(nc.semaphore("s_" + e))
        for q in ["sp", "pool", "act"]:
            for s in range(KD):
                self.sem[("dma", q, s)] = es.enter_context(nc.semaphore("d_%s%d" % (q, s)))
        self.cnt = {e: 0 for e in ENGS}
        self.dcnt = {q: 0 for q in ["sp", "pool", "act"]}
        self.dlast = {q: [] for q in ["sp", "pool", "act"]}
        self.last_tok = {}
        self.nphase = 0

    def op(self, eng, fn, r=(), w=()):
        self.ops.append(dict(eng=eng, fn=fn, r=tuple(r), w=tuple(w), dma=False))

    def dma(self, q, out, in_, r=(), w=(), **kw):
        self.ops.append(dict(eng=q, fn=(lambda e: e.dma_start(out=out, in_=in_, **kw)),
                             r=tuple(r), w=tuple(w), dma=True))

    def emit(self, scratch):
        nc = self.nc
        ops = self.ops
        sc = scratch
        self.op("pe", lambda e: e.matmul(sc["ps"][0:1, 0:1], sc["one_bf"][0:1, 0:1], sc["one_bf"][0:1, 0:1],
                                         start=True, stop=True), r=[], w=[("bar", "pe")] + sc["pskey"])
        self.op("act", lambda e: e.activation(out=sc["s_act"][0:1, 0:1], in_=sc["one_f"][0:1, 0:1], func=AF.Copy),
                w=[("bar", "act")])
        self.op("dve", lambda e: e.memset(sc["s_dve"][0:1, 0:1], 0.0), w=[("bar", "dve")])
        self.op("pool", lambda e: e.memset(sc["s_pool"][0:1, 0:1], 0.0), w=[("bar", "pool")])
        n = len(ops)
        last_w = {}
        rd_eng = {}
        rd_dma = {}
        deps = [None] * n
        for i, o in enumerate(ops):
            d = set()
            for k in o["r"]:
                if k in last_w:
                    d.add(last_w[k])
            for k in o["w"]:
                if k in last_w:
                    d.add(last_w[k])
                for j in rd_eng.get(k, {}).values():
                    d.add(j)
                for j in rd_dma.get(k, ()):
                    d.add(j)
            d.discard(i)
            deps[i] = d
            for k in o["w"]:
                last_w[k] = i
                rd_eng[k] = {}
                rd_dma[k] = []
            for k in o["r"]:
                if k in o["w"]:
                    continue
                if o["dma"]:
                    rd_dma.setdefault(k, []).append(i)
                else:
                    rd_eng.setdefault(k, {})[o["eng"]] = i
        qhist = {q: [] for q in self.dcnt}
        dma_prev_tok = [None] * n
        for i, o in enumerate(ops):
            if o["dma"]:
                q = o["eng"]
                k = self.dcnt[q] + len(qhist[q])
                o["didx"] = k
                if len(qhist[q]) >= KD:
                    deps[i].add(qhist[q][-KD])
                elif k >= KD:
                    dma_prev_tok[i] = (("dma", q, k % KD), 16 * (k // KD))
                qhist[q].append(i)
        need = [False] * n
        for i, o in enumerate(ops):
            for j in deps[i]:
                pj = ops[j]
                if pj["dma"]:
                    continue
                if pj["eng"] == "pe" and o["eng"] == "pe" and not o["dma"]:
                    continue
                need[j] = True
        for i in range(n - 4, n):
            need[i] = True
        token = [None] * n
        for i, o in enumerate(ops):
            if o["dma"]:
                k = o["didx"]
                token[i] = (("dma", o["eng"], k % KD), 16 * (k // KD + 1))
            elif need[i]:
                self.cnt[o["eng"]] += 1
                token[i] = (("eng", o["eng"]), self.cnt[o["eng"]])
        for q in self.dcnt:
            self.dcnt[q] += len(qhist[q])
        final_toks = [token[i] for i in range(n - 4, n)]
        for q in qhist:
            for i in qhist[q][-KD:]:
                final_toks.append(token[i])
        sem = self.sem
        stream = {e: [] for e in ENGS}
        for i, o in enumerate(ops):
            stream[o["eng"]].append(i)

        trace = {en: [] for en in ENGS}

        def run(e, ename):
            seen = {}
            tr = trace[ename]

            def wait(tk):
                if tk is None:
                    return
                key, val = tk
                if seen.get(key, 0) >= val:
                    return
                e.wait_ge(sem[key], val)
                tr.append(("w", key, val))
                seen[key] = val

            for i in stream[ename]:
                o = ops[i]
                wait(dma_prev_tok[i])
                for j in sorted(deps[i]):
                    pj = ops[j]
                    if (not pj["dma"]) and pj["eng"] == "pe" and ename == "pe" and not o["dma"]:
                        continue
                    wait(token[j])
                ins = o["fn"](e)
                if token[i] is not None:
                    ins.then_inc(sem[token[i][0]], 16 if o["dma"] else 1)
                    tr.append(("i", token[i][0], 16 if o["dma"] else 1))
            for tk in final_toks:
                wait(tk)

        with nc.Block() as block:
            @block.tensor
            def _(e):
                run(e, "pe")

            @block.scalar
            def _(e):
                run(e, "act")

            @block.vector
            def _(e):
                run(e, "dve")

            @block.gpsimd
            def _(e):
                run(e, "pool")

            @block.sync
            def _(e):
                run(e, "sp")
        self.check(trace)
        self.ops = []
        self.nphase += 1

    def check(self, trace):
        if not hasattr(self, "simsem"):
            self.simsem = {}
        ss = self.simsem
        ptr = {en: 0 for en in ENGS}
        prog = True
        while prog:
            prog = False
            for en in ENGS:
                tr = trace[en]
                while ptr[en] < len(tr):
                    kind, key, val = tr[ptr[en]]
                    if kind == "i":
                        ss[key] = ss.get(key, 0) + val
                    elif ss.get(key, 0) < val:
                        break
                    ptr[en] += 1
                    prog = True
        for en in ENGS:
            if ptr[en] < len(trace[en]):
                raise RuntimeError("DEADLOCK phase %d engine %s at %d/%d: %s (sem=%s)" % (
                    self.nphase, en, ptr[en], len(trace[en]), trace[en][ptr[en]], ss.get(trace[en][ptr[en]][1])))


def bf(a):
    return np.ascontiguousarray(a).astype(ml_dtypes.bfloat16)


class Ctx:
    pass


def setup_common(nc, es):
    cx = Ctx()
    cx.nc = nc
    cx.es = es
    cx.pg = Prog(nc, es)
    cx.psb = [es.enter_context(nc.psum_tensor("psb%d" % i, [128, 512], F32)) for i in range(8)]
    cx.ident_bf = es.enter_context(nc.sbuf_tensor("ident_bf", [128, 128], BF16))
    cx.ident_f = es.enter_context(nc.sbuf_tensor("ident_f", [128, 128], F32))
    cx.one_bf = es.enter_context(nc.sbuf_tensor("one_bf", [128, 128], BF16))
    cx.one_f = es.enter_context(nc.sbuf_tensor("one_f", [128, 128], F32))
    cx.s_act = es.enter_context(nc.sbuf_tensor("s_act", [128, 8], F32))
    cx.s_dve = es.enter_context(nc.sbuf_tensor("s_dve", [128, 8], F32))
    cx.s_pool = es.enter_context(nc.sbuf_tensor("s_pool", [128, 8], F32))
    cx.scratch = dict(ps=cx.psb[7], pskey=[("ps", 7)], one_bf=cx.one_bf, one_f=cx.one_f,
                      s_act=cx.s_act, s_dve=cx.s_dve, s_pool=cx.s_pool)
    cx.ident_d = nc.dram_tensor("ident_in", [128, 128], F32, kind="ExternalInput").ap()
    pg = cx.pg
    pg.dma("sp", cx.ident_f[:], cx.ident_d, w=[("ident_f",)])
    pg.dma("pool", cx.ident_bf[:], cx.ident_d, w=[("ident_bf",)])
    pg.op("dve", lambda e: e.memset(cx.one_bf[:], 1.0), w=[("one_bf",)])
    pg.op("dve", lambda e: e.memset(cx.one_f[:], 1.0), w=[("one_f",)])
    return cx


def build_ada():
    nc = bass.Bass("TRN2", target_bir_lowering=False)
    NCOL = 4608
    with ExitStack() as es:
        cx = setup_common(nc, es)
        pg = cx.pg
        cc = nc.dram_tensor("cc", [128, 8, 2], F32, kind="ExternalInput").ap()
        wa = nc.dram_tensor("wa", [1024, NCOL], F32, kind="ExternalInput").ap()
        ba = nc.dram_tensor("ba", [2, NCOL], F32, kind="ExternalInput").ap()
        mo = nc.dram_tensor("mo", [2, NCOL], F32, kind="ExternalOutput").ap()
        ccs = es.enter_context(nc.sbuf_tensor("ccs", [128, 8, 2], F32))
        scs = es.enter_context(nc.sbuf_tensor("scs", [128, 8, 2], F32))
        bas = es.enter_context(nc.sbuf_tensor("bas", [2, NCOL], F32))
        mos = es.enter_context(nc.sbuf_tensor("mos", [2, NCOL], F32))
        wsb = [es.enter_context(nc.sbuf_tensor("wsb%d" % i, [128, 8, 512], F32)) for i in range(2)]
        pg.dma("sp", ccs[:], cc, w=[("ccs",)])
        pg.dma("sp", bas[:], ba, w=[("bas",)])
        pg.op("act", lambda e: e.activation(out=scs[:], in_=ccs[:], func=AF.Silu), r=[("ccs",)], w=[("scs",)])
        for cb in range(NCOL // 512):
            b = cb % 2
            pg.dma("sp", wsb[b][:], wa[:, cb * 512:(cb + 1) * 512].rearrange("(c p) n -> p c n", p=128),
                   w=[("wsb", b)])
            for c in range(8):
                pg.op("pe", (lambda e, c=c, b=b: e.matmul(cx.psb[b][0:2, :], scs[:, c, :], wsb[b][:, c, :],
                                                          start=(c == 0), stop=(c == 7))),
                      r=[("scs",), ("wsb", b)], w=[("ps", b)])
            pg.op("dve", (lambda e, cb=cb, b=b: e.tensor_tensor(out=mos[:, cb * 512:(cb + 1) * 512],
                                                                in0=cx.psb[b][0:2, :],
                                                                in1=bas[:, cb * 512:(cb + 1) * 512], op=ALU.add)),
                  r=[("ps", b), ("bas",)], w=[("mos", cb)])
        pg.dma("sp", mo, mos[:], r=[("mos", cb) for cb in range(NCOL // 512)], w=[("mo",)])
        pg.emit(cx.scratch)
    return nc


def load_mod(cx, es):
    nc, pg = cx.nc, cx.pg
    cx.mcols_d = nc.dram_tensor("mcols", [2, 128, 72], F32, kind="ExternalInput").ap()
    cx.mrows_d = nc.dram_tensor("mrows", [2, 9216], F32, kind="ExternalInput").ap()
    cx.gcols_d = nc.dram_tensor("gcols", [128, 24], F32, kind="ExternalInput").ap()
    cx.mcol = [es.enter_context(nc.sbuf_tensor("mcol%d" % t, [128, 72], F32)) for t in range(2)]
    cx.gcol = es.enter_context(nc.sbuf_tensor("gcol", [128, 24], F32))
    cx.Gc = [es.enter_context(nc.sbuf_tensor("Gc%d" % t, [128, 24], F32)) for t in range(2)]
    for t in range(2):
        pg.dma("sp", cx.mcol[t][:], cx.mcols_d[t], w=[("mcol", t)])
    pg.dma("sp", cx.gcol[:], cx.gcols_d, w=[("gcol",)])
    for t in range(2):
        for s in range(3):
            pg.op("dve", (lambda e, t=t, s=s: e.scalar_tensor_tensor(
                out=cx.Gc[t][:, s * 8:(s + 1) * 8], in0=cx.mcol[t][:, (3 * s + 1) * 8:(3 * s + 2) * 8], scalar=1.0,
                in1=cx.gcol[:, s * 8:(s + 1) * 8], op0=ALU.add, op1=ALU.mult)),
                r=[("mcol", t), ("gcol",)], w=[("Gc", t, s)])


def typ_of(t):
    return 0 if t < NTO else 1


def norm_to_HT(cx, es, xs, s, HT, ntiles=NT, gfinal=None):
    nc, pg = cx.nc, cx.pg
    xt = [es.enter_context(nc.sbuf_tensor("n_xt%d_p%d" % (i, cx.pg.nphase), [128, 1024], F32)) for i in range(2)]
    xn = [es.enter_context(nc.sbuf_tensor("n_xn%d_p%d" % (i, cx.pg.nphase), [128, 1024], BF16)) for i in range(2)]
    junk = es.enter_context(nc.sbuf_tensor("n_junk_p%d" % cx.pg.nphase, [128, 1024], BF16))
    st = [es.enter_context(nc.sbuf_tensor("n_st%d_p%d" % (i, cx.pg.nphase), [128, 4], F32)) for i in range(2)]
    for t in range(ntiles):
        b = t % 2
        ty = typ_of(t)
        pg.dma("sp", xt[b][:], xs[t * 128:(t + 1) * 128, :], r=[("xs", t)], w=[("n_xt", b)])
        pg.op("dve", (lambda e, b=b: e.memset(st[b][:], 0.0)), w=[("n_st", b)])
        pg.op("act", (lambda e, b=b: e.activation(out=junk[:], in_=xt[b][:], func=AF.Square,
                                                  accum_out=st[b][:, 0:1])),
              r=[("n_xt", b)], w=[("n_junk",), ("n_st", b)])
        pg.op("dve", (lambda e, b=b: e.tensor_scalar(st[b][:, 1:2], st[b][:, 0:1], 1.0 / D, EPS,
                                                     op0=ALU.mult, op1=ALU.add)),
              r=[("n_st", b)], w=[("n_st", b)])
        pg.op("act", (lambda e, b=b: e.activation(out=st[b][:, 2:3], in_=st[b][:, 1:2], func=AF.Sqrt)),
              r=[("n_st", b)], w=[("n_st", b)])
        pg.op("dve", (lambda e, b=b: e.reciprocal(st[b][:, 3:4], st[b][:, 2:3])),
              r=[("n_st", b)], w=[("n_st", b)])
        pg.op("dve", (lambda e, b=b: e.tensor_scalar(xn[b][:], xt[b][:], st[b][:, 3:4], None, op0=ALU.mult)),
              r=[("n_st", b), ("n_xt", b)], w=[("n_xn", b)])
        pb = b
        pst = cx.psb[pb][:, :].bitcast(BF16)
        for c in range(8):
            pg.op("pe", (lambda e, c=c, b=b, pst=pst: e.transpose(pst[:, c * 128:(c + 1) * 128],
                                                                 xn[b][:, c * 128:(c + 1) * 128], cx.ident_bf[:])),
                  r=[("n_xn", b), ("ident_bf",)], w=[("ps", pb)])
        for c in range(8):
            eng = "act"
            if eng == "act":
                fn = (lambda e, c=c, t=t, ty=ty, pst=pst: e.activation(
                    out=HT[:, c, t * 128:(t + 1) * 128], in_=pst[:, c * 128:(c + 1) * 128], func=AF.Identity,
                    scale=cx.Gc[ty][:, s * 8 + c:s * 8 + c + 1],
                    bias=cx.mcol[ty][:, 3 * s * 8 + c:3 * s * 8 + c + 1]))
            else:
                fn = (lambda e, c=c, t=t, ty=ty, pst=pst: e.tensor_scalar(
                    HT[:, c, t * 128:(t + 1) * 128], pst[:, c * 128:(c + 1) * 128],
                    cx.Gc[ty][:, s * 8 + c:s * 8 + c + 1],
                    cx.mcol[ty][:, 3 * s * 8 + c:3 * s * 8 + c + 1], op0=ALU.mult, op1=ALU.add))
            pg.op(eng, fn, r=[("ps", pb), ("Gc", ty, s), ("mcol", ty)], w=[("HT", c, t)])


def ffn(cx, xs, w1, w3, w2, s, HT, tag):
    nc, pg = cx.nc, cx.pg
    with ExitStack() as es:
        norm_to_HT(cx, es, xs, s, HT)
        GT = es.enter_context(nc.sbuf_tensor("f_GT_p%d" % cx.pg.nphase, [128, NJ, 1152], BF16))
        W2 = es.enter_context(nc.sbuf_tensor("f_W2_p%d" % cx.pg.nphase, [128, NJ, 1024], BF16))
        W13 = [es.enter_context(nc.sbuf_tensor("f_W13_%d_p%d" % (i, cx.pg.nphase), [128, 2, 8, 128], BF16)) for i in range(2)]
        sil = [es.enter_context(nc.sbuf_tensor("f_sil%d_p%d" % (i, cx.pg.nphase), [128, 512], F32)) for i in range(2)]
        gbc = [es.enter_context(nc.sbuf_tensor("f_gbc%d_p%d" % (i, cx.pg.nphase), [128, 1024], F32)) for i in range(2)]
        xt = [es.enter_context(nc.sbuf_tensor("f_xt%d_p%d" % (i, cx.pg.nphase), [128, 1024], F32)) for i in range(2)]
        yt = [es.enter_context(nc.sbuf_tensor("f_yt%d_p%d" % (i, cx.pg.nphase), [128, 1024], F32)) for i in range(2)]
        for ty in range(2):
            pg.dma("sp", gbc[ty][:], cx.mrows_d[ty, (3 * s + 2) * 1024:(3 * s + 3) * 1024].partition_broadcast(128),
                   w=[("f_gbc", ty)])
            pg.op("act", (lambda e, ty=ty: e.mul(gbc[ty][:], gbc[ty][:], 0.5)), r=[("f_gbc", ty)], w=[("f_gbc", ty)])
        for hj in range(2):
            pg.dma("pool", W2[:, hj * 11:(hj + 1) * 11, :],
                   w2[hj * 1408:(hj + 1) * 1408, :].rearrange("(j p) n -> p j n", p=128), w=[("f_W2", hj)])
        it = 0
        for hf in range(2):
            tok0 = hf * 1152
            groups = [(0, 512), (512, 512), (1024, 128)]
            for j in range(NJ):
                b = (hf * NJ + j) % 2
                pg.dma("pool", W13[b][:, 0], w1[:, j * 128:(j + 1) * 128].rearrange("(c p) n -> p c n", p=128),
                       w=[("f_W13", b, 0)])
                pg.dma("pool", W13[b][:, 1], w3[:, j * 128:(j + 1) * 128].rearrange("(c p) n -> p c n", p=128),
                       w=[("f_W13", b, 1)])
                for (g0, gn) in groups:
                    i2 = it % 2
                    it += 1
                    pa, pb_ = 2 + 2 * i2, 3 + 2 * i2
                    htr = [("HT", c, tt) for c in range(8) for tt in range((tok0 + g0) // 128, (tok0 + g0 + gn) // 128)]
                    for c in range(8):
                        pg.op("pe", (lambda e, c=c, b=b, pa=pa, g0=g0, gn=gn, tok0=tok0: e.matmul(
                            cx.psb[pa][:, 0:gn], W13[b][:, 0, c, :], HT[:, c, tok0 + g0:tok0 + g0 + gn],
                            start=(c == 0), stop=(c == 7))), r=[("f_W13", b, 0)] + (htr if c == 0 else []), w=[("ps", pa)])
                    for c in range(8):
                        pg.op("pe", (lambda e, c=c, b=b, pb_=pb_, g0=g0, gn=gn, tok0=tok0: e.matmul(
                            cx.psb[pb_][:, 0:gn], W13[b][:, 1, c, :], HT[:, c, tok0 + g0:tok0 + g0 + gn],
                            start=(c == 0), stop=(c == 7))), r=[("f_W13", b, 1)], w=[("ps", pb_)])
                    pg.op("act", (lambda e, i2=i2, pa=pa, gn=gn: e.activation(out=sil[i2][:, 0:gn], in_=cx.psb[pa][:, 0:gn],
                                                                              func=AF.Silu)),
                          r=[("ps", pa)], w=[("f_sil", i2)])
                    pg.op("dve", (lambda e, i2=i2, pb_=pb_, g0=g0, gn=gn, j=j: e.tensor_tensor(
                        out=GT[:, j, g0:g0 + gn], in0=sil[i2][:, 0:gn], in1=cx.psb[pb_][:, 0:gn], op=ALU.mult)),
                        r=[("f_sil", i2), ("ps", pb_)], w=[("f_GT", j, g0)])
            for tl in range(9):
                t = hf * 9 + tl
                ty = typ_of(t)
                b = t % 2
                banks = (6, 7) if b == 0 else (0, 1)
                gtr = [("f_GT", j, (tl * 128) // 512 * 512) for j in range(NJ)]
                for hc in range(2):
                    for j in range(NJ):
                        pg.op("pe", (lambda e, hc=hc, j=j, tl=tl, bk=banks[hc]: e.matmul(
                            cx.psb[bk][:, :], GT[:, j, tl * 128:(tl + 1) * 128], W2[:, j, hc * 512:(hc + 1) * 512],
                            start=(j == 0), stop=(j == NJ - 1))),
                            r=(gtr + [("f_W2", 0), ("f_W2", 1)]) if j == 0 else [], w=[("ps", banks[hc])])
                pg.dma("sp", xt[b][:], xs[t * 128:(t + 1) * 128, :], r=[("xs", t)], w=[("f_xt", b)])
                for hc in range(2):
                    pg.op("dve", (lambda e, hc=hc, b=b, ty=ty, bk=banks[hc]: e.tensor_tensor(
                        out=yt[b][:, hc * 512:(hc + 1) * 512], in0=cx.psb[bk][:, :],
                        in1=gbc[ty][:, hc * 512:(hc + 1) * 512], op=ALU.mult)),
                        r=[("ps", banks[hc]), ("f_gbc", ty)], w=[("f_yt", b, hc)])
                pg.op("pool", (lambda e, b=b: e.tensor_tensor(out=xt[b][:], in0=xt[b][:], in1=yt[b][:], op=ALU.add)),
                      r=[("f_yt", b, 0), ("f_yt", b, 1), ("f_xt", b)], w=[("f_xt", b)])
                pg.dma("sp", xs[t * 128:(t + 1) * 128, :], xt[b][:], r=[("f_xt", b)], w=[("xs", t)])
        pg.emit(cx.scratch)


def new_prog(inputs_only=False):
    nc = bass.Bass("TRN2", target_bir_lowering=False)
    return nc


def din(nc, name, shape, dt=F32):
    return nc.dram_tensor(name, list(shape), dt, kind="ExternalInput").ap()


def dout(nc, name, shape, dt=F32):
    return nc.dram_tensor(name, list(shape), dt, kind="ExternalOutput").ap()


NPA = 2560
ROPE_SPANS = [(256, 384, 0), (1408, 768, 384)]
ROPE_REGIONS = [(256, 192, 16, 0), (448, 192, 16, 192), (1408, 384, 8, 384), (1792, 384, 8, 768)]


def build_A():
    nc = new_prog()
    with ExitStack() as es:
        cx = setup_common(nc, es)
        pg = cx.pg
        xin = din(nc, "xin", [TOK, D])
        xs = dout(nc, "xs", [TOK, D])
        w1 = din(nc, "w1", [D, DFF])
        w3 = din(nc, "w3", [D, DFF])
        w2 = din(nc, "w2", [DFF, D])
        win = din(nc, "win", [D, NPA])
        rc = din(nc, "ropec", [TOK, 1152])
        rs = din(nc, "ropes", [TOK, 1152])
        pall = dout(nc, "pall", [TOK, NPA], BF16)
        load_mod(cx, es)
        HT = es.enter_context(nc.sbuf_tensor("HT", [128, 8, TOK], BF16))
        for t in range(NT):
            pg.dma("sp", xs[t * 128:(t + 1) * 128, :], xin[t * 128:(t + 1) * 128, :], w=[("xs", t)])
        ffn(cx, xs, w1, w3, w2, 0, HT, "f1")
        with ExitStack() as e2:
            norm_to_HT(cx, e2, xs, 1, HT)
            Win = e2.enter_context(nc.sbuf_tensor("a_Win", [128, 8, NPA], BF16))
            pt = [e2.enter_context(nc.sbuf_tensor("a_pt%d" % i, [128, NPA], F32)) for i in range(2)]
            psw = [e2.enter_context(nc.sbuf_tensor("a_psw%d" % i, [128, 1152], F32)) for i in range(2)]
            ct = [e2.enter_context(nc.sbuf_tensor("a_ct%d" % i, [128, 1152], F32)) for i in range(2)]
            stb = [e2.enter_context(nc.sbuf_tensor("a_st%d" % i, [128, 1152], F32)) for i in range(2)]
            pb = [e2.enter_context(nc.sbuf_tensor("a_pb%d" % i, [128, NPA], BF16)) for i in range(2)]
            for nb in range(5):
                pg.dma("pool", Win[:, :, nb * 512:(nb + 1) * 512],
                       win[:, nb * 512:(nb + 1) * 512].rearrange("(c p) n -> p c n", p=128), w=[("a_Win", nb)])
            for t in range(NT):
                b = t % 2
                pg.dma("sp", ct[b][:], rc[t * 128:(t + 1) * 128, :], w=[("a_ct", b)])
                pg.dma("sp", stb[b][:], rs[t * 128:(t + 1) * 128, :], w=[("a_st", b)])
                for nb in range(5):
                    bk = 2 + nb
                    for c in range(8):
                        pg.op("pe", (lambda e, c=c, t=t, nb=nb, bk=bk: e.matmul(
                            cx.psb[bk][:, :], HT[:, c, t * 128:(t + 1) * 128], Win[:, c, nb * 512:(nb + 1) * 512],
                            start=(c == 0), stop=(c == 7))),
                            r=[("a_Win", nb)] + ([("HT", cc, t) for cc in range(8)] if c == 0 else []), w=[("ps", bk)])
                    pg.op("act", (lambda e, b=b, nb=nb, bk=bk: e.activation(out=pt[b][:, nb * 512:(nb + 1) * 512],
                                                                           in_=cx.psb[bk][:, :], func=AF.Copy)),
                          r=[("ps", bk)], w=[("a_pt", b, nb)])
                ptk = [("a_pt", b, nb) for nb in range(5)]
                for (c0, w_, hf, toff) in ROPE_REGIONS:
                    g = w_ // (2 * hf)
                    src = pt[b][:, c0:c0 + w_].rearrange("p (g h k) -> p g h k", h=2, k=hf)
                    dst = psw[b][:, toff:toff + w_].rearrange("p (g h k) -> p g h k", h=2, k=hf)
                    for h in range(2):
                        pg.op("pool", (lambda e, src=src, dst=dst, h=h: e.tensor_copy(dst[:, :, h, :], src[:, :, 1 - h, :])),
                              r=ptk, w=[("a_psw", b, toff, h)])
                pswk = [("a_psw", b, toff, h) for (_, _, _, toff) in ROPE_REGIONS for h in range(2)]
                for (c0, w_, toff) in ROPE_SPANS:
                    pg.op("dve", (lambda e, b=b, c0=c0, w_=w_, toff=toff: e.tensor_tensor(
                        out=pt[b][:, c0:c0 + w_], in0=pt[b][:, c0:c0 + w_], in1=ct[b][:, toff:toff + w_], op=ALU.mult)),
                        r=ptk + pswk + [("a_ct", b)], w=ptk)
                    pg.op("pool", (lambda e, b=b, w_=w_, toff=toff: e.tensor_tensor(
                        out=psw[b][:, toff:toff + w_], in0=psw[b][:, toff:toff + w_], in1=stb[b][:, toff:toff + w_],
                        op=ALU.mult)), r=pswk + [("a_st", b)], w=pswk)
                    pg.op("dve", (lambda e, b=b, c0=c0, w_=w_, toff=toff: e.tensor_tensor(
                        out=pt[b][:, c0:c0 + w_], in0=pt[b][:, c0:c0 + w_], in1=psw[b][:, toff:toff + w_], op=ALU.add)),
                        r=ptk + pswk, w=ptk)
                pg.op("act", (lambda e, b=b: e.activation(out=pb[b][:], in_=pt[b][:], func=AF.Copy)),
                      r=ptk, w=[("a_pb", b)])
                pg.dma("sp", pall[t * 128:(t + 1) * 128, :], pb[b][:], r=[("a_pb", b)], w=[("pall", t)])
            pg.emit(cx.scratch)
    return nc


def build_ATT():
    nc = new_prog()
    with ExitStack() as es:
        cx = setup_common(nc, es)
        pg = cx.pg
        qT_d = din(nc, "qT", [6, 64, TOK], BF16)
        kT_d = din(nc, "kT", [6, 64, NKT * 128], BF16)
        v_d = din(nc, "vv", [6, 128, NKT, 64], BF16)
        lamv_d = din(nc, "lamv", [128, 128])
        cst_d = din(nc, "cst", [128, 4])
        sub_d = din(nc, "subln", [128, 64])
        do_d = dout(nc, "dout", [TOK, 384])
        QT = es.enter_context(nc.sbuf_tensor("QT", [64, 6, TOK], BF16))
        KT = es.enter_context(nc.sbuf_tensor("KT", [64, NKT * 128], BF16))
        V = es.enter_context(nc.sbuf_tensor("V", [128, NKT, 65], BF16))
        PT = [es.enter_context(nc.sbuf_tensor("PT%d" % i, [128, 512], BF16)) for i in range(3)]
        osb = [es.enter_context(nc.sbuf_tensor("osb%d" % i, [65, 512], F32)) for i in range(2)]
        on = [es.enter_context(nc.sbuf_tensor("on%d" % i, [128, 4, 64], F32)) for i in range(2)]
        dd = es.enter_context(nc.sbuf_tensor("dd", [128, 64], F32))
        junk = es.enter_context(nc.sbuf_tensor("junk", [128, 64], F32))
        stt = es.enter_context(nc.sbuf_tensor("stt", [128, 8], F32))
        lamv = es.enter_context(nc.sbuf_tensor("lamv_s", [128, 128], F32))
        lt = es.enter_context(nc.sbuf_tensor("lt", [128, 64], F32))
        ls = es.enter_context(nc.sbuf_tensor("ls", [128, 8], F32))
        cst = es.enter_context(nc.sbuf_tensor("cst_s", [128, 4], F32))
        subw = es.enter_context(nc.sbuf_tensor("subw", [128, 64], F32))
        dsb = es.enter_context(nc.sbuf_tensor("dsb", [128, NT, 384], F32))
        pg.dma("sp", lamv[:], lamv_d, w=[("lamv",)])
        pg.dma("sp", cst[:], cst_d, w=[("cst",)])
        pg.dma("sp", subw[:], sub_d, w=[("subw",)])
        for h in range(6):
            pg.dma("sp", QT[:, h, :], qT_d[h], w=[("QT", h)])
        pg.op("dve", lambda e: e.memset(V[:, :, 64:65], 1.0), w=[("Vone",)])
        pg.op("dve", lambda e: e.memset(ls[:], 0.0), w=[("ls",)])
        pg.op("dve", lambda e: e.tensor_tensor(out=lt[:, 0:32], in0=lamv[:, 0:32], in1=lamv[:, 32:64], op=ALU.mult),
              r=[("lamv",)], w=[("lt", 0)])
        pg.op("dve", lambda e: e.tensor_tensor(out=lt[:, 32:64], in0=lamv[:, 64:96], in1=lamv[:, 96:128], op=ALU.mult),
              r=[("lamv",)], w=[("lt", 1)])
        pg.op("dve", lambda e: e.tensor_reduce(out=ls[:, 0:1], in_=lt[:, 0:32], axis=AX.X, op=ALU.add),
              r=[("lt", 0), ("ls",)], w=[("ls",)])
        pg.op("dve", lambda e: e.tensor_reduce(out=ls[:, 1:2], in_=lt[:, 32:64], axis=AX.X, op=ALU.add),
              r=[("lt", 1), ("ls",)], w=[("ls",)])
        pg.op("act", lambda e: e.activation(out=ls[:, 2:4], in_=ls[:, 0:2], func=AF.Exp), r=[("ls",)], w=[("ls",)])
        pg.op("dve", lambda e: e.tensor_tensor(out=ls[:, 4:5], in0=ls[:, 3:4], in1=ls[:, 2:3], op=ALU.subtract),
              r=[("ls",)], w=[("ls",)])
        pg.op("dve", lambda e: e.tensor_tensor(out=ls[:, 5:6], in0=ls[:, 4:5], in1=cst[:, 0:1], op=ALU.subtract),
              r=[("ls",), ("cst",)], w=[("ls",)])
        pg.op("dve", lambda e: e.tensor_scalar(subw[:], subw[:], cst[:, 1:2], None, op0=ALU.mult),
              r=[("subw",), ("cst",)], w=[("subw",)])
        qgroups = [(0, 512), (512, 512), (1024, 512), (1536, 512), (2048, 256)]
        scale = 32.0 ** -0.5
        si = 0
        for hd in range(6):
            pg.dma("sp", KT[:, :], kT_d[hd], w=[("KT",)])
            for nb in range(5):
                pg.dma("sp", V[:, nb * 26:(nb + 1) * 26, 0:64], v_d[hd, :, nb * 26:(nb + 1) * 26, :],
                       r=[("Vone",)], w=[("V", nb)])
            vk = [("V", nb) for nb in range(5)]
            for qi, (q0, qn) in enumerate(qgroups):
                kts = list(range(NKT)) if qi < 4 else [NKT - 2, NKT - 1]
                for mp in range(2):
                    ob = 3 + mp
                    for ki, kt in enumerate(kts):
                        sb_ = si % 3
                        si += 1
                        pg.op("pe", (lambda e, sb_=sb_, mp=mp, kt=kt, hd=hd, q0=q0, qn=qn: e.matmul(
                            cx.psb[sb_][:, 0:qn], KT[32 * mp:32 * mp + 32, kt * 128:(kt + 1) * 128],
                            QT[32 * mp:32 * mp + 32, hd, q0:q0 + qn], start=True, stop=True)),
                            r=[("KT",), ("QT", hd)], w=[("ps", sb_)])
                        pg.op("act", (lambda e, sb_=sb_, qn=qn: e.activation(out=PT[sb_][:, 0:qn], in_=cx.psb[sb_][:, 0:qn],
                                                                             func=AF.Exp, scale=scale)),
                              r=[("ps", sb_)], w=[("PT", sb_)])
                        pg.op("pe", (lambda e, sb_=sb_, kt=kt, ob=ob, qn=qn, ki=ki, nk=len(kts): e.matmul(
                            cx.psb[ob][0:65, 0:qn], V[:, kt, :], PT[sb_][:, 0:qn], start=(ki == 0), stop=(ki == nk - 1))),
                            r=[("PT", sb_)] + (vk if ki == 0 else []), w=[("ps", ob)])
                    pg.op("act", (lambda e, mp=mp, ob=ob, qn=qn: e.activation(out=osb[mp][:, 0:qn], in_=cx.psb[ob][0:65, 0:qn],
                                                                             func=AF.Copy)),
                          r=[("ps", ob)], w=[("osb", mp)])
                    for tt in range(qn // 128):
                        tb = 5 + (tt % 2)
                        pg.op("pe", (lambda e, mp=mp, tt=tt, tb=tb: e.matmul(
                            cx.psb[tb][:, 0:65], osb[mp][0:65, tt * 128:(tt + 1) * 128], cx.ident_f[0:65, 0:65],
                            start=True, stop=True)), r=[("osb", mp), ("ident_f",)], w=[("ps", tb)])
                        pg.op("dve", (lambda e, tb=tb: e.reciprocal(stt[:, 0:1], cx.psb[tb][:, 64:65])),
                              r=[("ps", tb)], w=[("stt", 0)])
                        pg.op("dve", (lambda e, tb=tb, mp=mp, tt=tt: e.tensor_scalar(
                            on[mp][:, tt, :], cx.psb[tb][:, 0:64], stt[:, 0:1], None, op0=ALU.mult)),
                            r=[("ps", tb), ("stt", 0)], w=[("on", mp, tt)])
                for tt in range(qn // 128):
                    t = q0 // 128 + tt
                    pg.op("dve", (lambda e, tt=tt: e.scalar_tensor_tensor(
                        out=dd[:], in0=on[1][:, tt, :], scalar=ls[:, 5:6], in1=on[0][:, tt, :], op0=ALU.mult, op1=ALU.add)),
                        r=[("on", 0, tt), ("on", 1, tt), ("ls",)], w=[("dd",)])
                    pg.op("dve", (lambda e: e.memset(stt[:, 1:2], 0.0)), w=[("stt", 1)])
                    pg.op("act", (lambda e: e.activation(out=junk[:], in_=dd[:], func=AF.Square, accum_out=stt[:, 1:2])),
                          r=[("dd",)], w=[("junk",), ("stt", 1)])
                    pg.op("dve", (lambda e: e.tensor_scalar(stt[:, 2:3], stt[:, 1:2], 1.0 / 64, EPS, op0=ALU.mult, op1=ALU.add)),
                          r=[("stt", 1)], w=[("stt", 2)])
                    pg.op("act", (lambda e: e.activation(out=stt[:, 3:4], in_=stt[:, 2:3], func=AF.Sqrt)),
                          r=[("stt", 2)], w=[("stt", 3)])
                    pg.op("dve", (lambda e: e.reciprocal(stt[:, 4:5], stt[:, 3:4])), r=[("stt", 3)], w=[("stt", 4)])
                    pg.op("dve", (lambda e, t=t, hd=hd: e.scalar_tensor_tensor(
                        out=dsb[:, t, hd * 64:(hd + 1) * 64], in0=dd[:], scalar=stt[:, 4:5], in1=subw[:],
                        op0=ALU.mult, op1=ALU.mult)), r=[("dd",), ("stt", 4), ("subw",)], w=[("dsb", t)])
        for t in range(NT):
            pg.dma("sp", do_d[t * 128:(t + 1) * 128, :], dsb[:, t, :], r=[("dsb", t)], w=[("do", t)])
        pg.emit(cx.scratch)
    return nc


def build_RET():
    nc = new_prog()
    NTK = NKT * 128
    with ExitStack() as es:
        cx = setup_common(nc, es)
        pg = cx.pg
        um_d = din(nc, "umat", [128, 128])
        mm_d = din(nc, "mmat", [128, 128])
        ir_d = din(nc, "irow", [128, 128])
        pc_d = din(nc, "pcol", [128, 1])
        umat = es.enter_context(nc.sbuf_tensor("umat_s", [128, 128], F32))
        mmat = es.enter_context(nc.sbuf_tensor("mmat_s", [128, 128], F32))
        irow = es.enter_context(nc.sbuf_tensor("irow_s", [128, 128], F32))
        pcol = es.enter_context(nc.sbuf_tensor("pcol_s", [128, 1], F32))
        pg.dma("sp", umat[:], um_d, w=[("umat",)])
        pg.dma("sp", mmat[:], mm_d, w=[("mmat",)])
        pg.dma("sp", irow[:], ir_d, w=[("irow",)])
        pg.dma("sp", pcol[:], pc_d, w=[("pcol",)])
        qT = es.enter_context(nc.sbuf_tensor("r_qT", [32, NTK], BF16))
        kT = es.enter_context(nc.sbuf_tensor("r_kT", [32, NTK], BF16))
        kk = es.enter_context(nc.sbuf_tensor("r_kk", [128, NKT, 32], BF16))
        vv = es.enter_context(nc.sbuf_tensor("r_vv", [128, NKT, 64], BF16))
        osb = es.enter_context(nc.sbuf_tensor("r_osb", [128, NKT, 64], F32))
        lg = es.enter_context(nc.sbuf_tensor("r_lg", [128, 8], F32))
        DT = es.enter_context(nc.sbuf_tensor("r_DT", [128, 128], F32))
        qdec = es.enter_context(nc.sbuf_tensor("r_qdec", [32, 128], F32))
        ATm = [es.enter_context(nc.sbuf_tensor("r_ATm%d" % i, [128, 128], BF16)) for i in range(2)]
        qd = [es.enter_context(nc.sbuf_tensor("r_qd%d" % i, [32, 128], BF16)) for i in range(2)]
        kd = [es.enter_context(nc.sbuf_tensor("r_kd%d" % i, [128, 32], BF16)) for i in range(2)]
        S32 = es.enter_context(nc.sbuf_tensor("r_S32", [32, 64], F32))
        Sbf = [es.enter_context(nc.sbuf_tensor("r_Sbf%d" % i, [32, 64], BF16)) for i in range(2)]
        for u in range(2):
            qT_d = din(nc, "qT%d" % u, [32, NTK], BF16)
            kT_d = din(nc, "kT%d" % u, [32, NTK], BF16)
            kk_d = din(nc, "kk%d" % u, [128, NKT, 32], BF16)
            vv_d = din(nc, "vv%d" % u, [128, NKT, 64], BF16)
            lg_d = din(nc, "lg%d" % u, [128, 1])
            ro_d = dout(nc, "ro%d" % u, [128, NKT, 64])
            pg.dma("sp", qT[:], qT_d, w=[("qT",)])
            pg.dma("sp", kT[:], kT_d, w=[("kT",)])
            pg.dma("sp", kk[:], kk_d, w=[("kk",)])
            pg.dma("sp", vv[:], vv_d, w=[("vv",)])
            pg.dma("sp", lg[:, 0:1], lg_d, w=[("lg",)])
            pg.op("act", lambda e: e.activation(out=lg[:, 1:2], in_=lg[:, 0:1], func=AF.Exp, scale=-1.0), r=[("lg",)], w=[("lg",)])
            pg.op("dve", lambda e: e.tensor_scalar(lg[:, 2:3], lg[:, 1:2], 1.0, None, op0=ALU.add), r=[("lg",)], w=[("lg",)])
            pg.op("act", lambda e: e.activation(out=lg[:, 3:4], in_=lg[:, 2:3], func=AF.Ln), r=[("lg",)], w=[("lg",)])
            pg.op("dve", lambda e: e.tensor_scalar(lg[:, 4:5], lg[:, 3:4], -1.0, None, op0=ALU.mult), r=[("lg",)], w=[("lg",)])
            pg.op("act", lambda e: e.activation(out=DT[:], in_=umat[:], func=AF.Exp, scale=lg[:, 4:5]),
                  r=[("lg",), ("umat",)], w=[("DT",)])
            pg.op("dve", lambda e: e.tensor_tensor(out=DT[:], in0=DT[:], in1=mmat[:], op=ALU.mult), r=[("DT",), ("mmat",)], w=[("DT",)])
            pg.op("act", lambda e: e.activation(out=qdec[:], in_=irow[0:32, :], func=AF.Exp, scale=lg[0:32, 4:5]),
                  r=[("lg",), ("irow",)], w=[("qdec",)])
            pg.op("act", lambda e: e.activation(out=lg[:, 5:6], in_=pcol[:], func=AF.Exp, scale=lg[:, 4:5]),
                  r=[("lg",), ("pcol",)], w=[("lg",)])
            pg.op("act", lambda e: e.activation(out=lg[:, 6:7], in_=lg[:, 4:5], func=AF.Exp, scale=128.0),
                  r=[("lg",)], w=[("lg",)])
            pg.op("dve", lambda e: e.memset(S32[:], 0.0), w=[("S32",)])
            pg.op("dve", lambda e: e.memset(Sbf[0][:], 0.0), w=[("Sbf", 0)])
            for c in range(NKT):
                b = c % 2
                pg.op("pe", (lambda e, c=c, b=b: e.matmul(cx.psb[b][:, 0:128], kT[:, c * 128:(c + 1) * 128],
                                                          qT[:, c * 128:(c + 1) * 128], start=True, stop=True)),
                      r=[("kT",), ("qT",)], w=[("ps", b)])
                pg.op("dve", (lambda e, b=b: e.tensor_tensor(out=ATm[b][:], in0=cx.psb[b][:, 0:128], in1=DT[:], op=ALU.mult)),
                      r=[("ps", b), ("DT",)], w=[("ATm", b)])
                pg.op("pool", (lambda e, c=c, b=b: e.tensor_tensor(out=qd[b][:], in0=qT[:, c * 128:(c + 1) * 128], in1=qdec[:],
                                                                   op=ALU.mult)), r=[("qT",), ("qdec",)], w=[("qd", b)])
                pg.op("pe", (lambda e, c=c, b=b: e.matmul(cx.psb[2 + b][:, 0:64], ATm[b][:], vv[:, c, :], start=True, stop=False)),
                      r=[("ATm", b), ("vv",)], w=[("ps", 2 + b)])
                pg.op("pe", (lambda e, c=c, b=b: e.matmul(cx.psb[2 + b][:, 0:64], qd[b][:], Sbf[b][:], start=False, stop=True)),
                      r=[("qd", b), ("Sbf", b)], w=[("ps", 2 + b)])
                pg.op("act", (lambda e, c=c, b=b: e.activation(out=osb[:, c, :], in_=cx.psb[2 + b][:, 0:64], func=AF.Copy)),
                      r=[("ps", 2 + b)], w=[("osb", c)])
                pg.op("act", (lambda e, c=c, b=b: e.activation(out=kd[b][:], in_=kk[:, c, :], func=AF.Copy, scale=lg[:, 5:6])),
                      r=[("kk",), ("lg",)], w=[("kd", b)])
                pg.op("pe", (lambda e, c=c, b=b: e.matmul(cx.psb[4 + b][0:32, 0:64], kd[b][:], vv[:, c, :], start=True, stop=True)),
                      r=[("kd", b), ("vv",)], w=[("ps", 4 + b)])
                pg.op("dve", (lambda e, b=b: e.scalar_tensor_tensor(out=S32[:], in0=S32[:], scalar=lg[0:32, 6:7],
                                                                    in1=cx.psb[4 + b][0:32, 0:64], op0=ALU.mult, op1=ALU.add)),
                      r=[("S32",), ("ps", 4 + b), ("lg",)], w=[("S32",)])
                pg.op("act", (lambda e, b=b: e.activation(out=Sbf[1 - b][:], in_=S32[:], func=AF.Copy)),
                      r=[("S32",)], w=[("Sbf", 1 - b)])
            pg.dma("sp", ro_d, osb[:], r=[("osb", c) for c in range(NKT)], w=[("ro", u)])
        pg.emit(cx.scratch)
    return nc


def build_FFT():
    nc = new_prog()
    with ExitStack() as es:
        cx = setup_common(nc, es)
        pg = cx.pg
        uT_d = din(nc, "uT", [64, SEQ + CTX], BF16)
        names = ["ccs", "cm1", "sm1", "nsm1", "cm3", "sm3"]
        shp = {"ccs": [64, 64]}
        tb = {}
        for n_ in names:
            s_ = shp.get(n_, [128, 128])
            d_ = din(nc, n_, s_, BF16)
            tb[n_] = es.enter_context(nc.sbuf_tensor(n_ + "_s", s_, BF16))
            pg.dma("sp", tb[n_][:], d_, w=[(n_,)])
        twc_d = din(nc, "twc", [128, 4096])
        tws_d = din(nc, "tws", [128, 4096])
        c256_d = din(nc, "c256", [128, 2, 256], BF16)
        s256_d = din(nc, "s256", [128, 2, 256], BF16)
        y_d = dout(nc, "y", [128, 4096])
        yc_d = dout(nc, "yc", [128, 2, 32])
        uT = es.enter_context(nc.sbuf_tensor("uT_s", [64, SEQ + CTX], BF16))
        X = es.enter_context(nc.sbuf_tensor("X", [128, 128, 64], BF16))
        Zr = es.enter_context(nc.sbuf_tensor("Zr", [128, 4096], F32))
        Zi = es.enter_context(nc.sbuf_tensor("Zi", [128, 4096], F32))
        ta = es.enter_context(nc.sbuf_tensor("ta", [128, 4096], F32))
        tb2 = es.enter_context(nc.sbuf_tensor("tb2", [128, 4096], F32))
        Zpr = es.enter_context(nc.sbuf_tensor("Zpr", [128, 4096], BF16))
        Zpi = es.enter_context(nc.sbuf_tensor("Zpi", [128, 4096], BF16))
        twc = es.enter_context(nc.sbuf_tensor("twc_s", [128, 4096], F32))
        tws = es.enter_context(nc.sbuf_tensor("tws_s", [128, 4096], F32))
        c256 = es.enter_context(nc.sbuf_tensor("c256_s", [128, 2, 256], BF16))
        s256 = es.enter_context(nc.sbuf_tensor("s256_s", [128, 2, 256], BF16))
        Xc = es.enter_context(nc.sbuf_tensor("Xc", [128, 2, 64], BF16))
        ycs = es.enter_context(nc.sbuf_tensor("ycs", [128, 2, 32], F32))
        pg.dma("sp", uT[:], uT_d, w=[("uT",)])
        pg.dma("sp", twc[:], twc_d, w=[("twc",)])
        pg.dma("sp", tws[:], tws_d, w=[("tws",)])
        pg.dma("sp", c256[:], c256_d, w=[("c256",)])
        pg.dma("sp", s256[:], s256_d, w=[("s256",)])
        uv = uT[:, 0:SEQ].rearrange("c (a b) -> c a b", b=128)
        for g8 in range(16):
            bk = g8 % 2
            for k in range(8):
                t2 = g8 * 8 + k
                pg.op("pe", (lambda e, t2=t2, k=k, bk=bk: e.matmul(cx.psb[bk][:, k * 64:(k + 1) * 64], uv[:, :, t2], tb["ccs"][:],
                                                                  start=True, stop=True)),
                      r=[("uT",), ("ccs",)], w=[("ps", bk)])
            pg.op("act", (lambda e, g8=g8, bk=bk: e.activation(
                out=X[:, g8 * 8:(g8 + 1) * 8, :].rearrange("p a b -> p (a b)"), in_=cx.psb[bk][:, :], func=AF.Copy)),
                r=[("ps", bk)], w=[("X", g8)])
        xk = [("X", g8) for g8 in range(16)]
        for m4 in range(8):
            br, bi = 2 + 2 * (m4 % 2), 3 + 2 * (m4 % 2)
            for k in range(4):
                m = m4 * 4 + k
                xr, xi = X[:, :, m], X[:, :, 32 + m]
                pg.op("pe", (lambda e, xr=xr, k=k, br=br: e.matmul(cx.psb[br][:, k * 128:(k + 1) * 128], xr, tb["cm1"][:],
                                                                  start=True, stop=False)),
                      r=xk + [("cm1",)], w=[("ps", br)])
                pg.op("pe", (lambda e, xi=xi, k=k, br=br: e.matmul(cx.psb[br][:, k * 128:(k + 1) * 128], xi, tb["sm1"][:],
                                                                  start=False, stop=True)),
                      r=[("sm1",)], w=[("ps", br)])
                pg.op("pe", (lambda e, xi=xi, k=k, bi=bi: e.matmul(cx.psb[bi][:, k * 128:(k + 1) * 128], xi, tb["cm1"][:],
                                                                  start=True, stop=False)),
                      r=[], w=[("ps", bi)])
                pg.op("pe", (lambda e, xr=xr, k=k, bi=bi: e.matmul(cx.psb[bi][:, k * 128:(k + 1) * 128], xr, tb["nsm1"][:],
                                                                  start=False, stop=True)),
                      r=[("nsm1",)], w=[("ps", bi)])
            pg.op("act", (lambda e, m4=m4, br=br: e.activation(out=Zr[:, m4 * 512:(m4 + 1) * 512], in_=cx.psb[br][:, :], func=AF.Copy)),
                  r=[("ps", br)], w=[("Zr", m4)])
            pg.op("act", (lambda e, m4=m4, bi=bi: e.activation(out=Zi[:, m4 * 512:(m4 + 1) * 512], in_=cx.psb[bi][:, :], func=AF.Copy)),
                  r=[("ps", bi)], w=[("Zi", m4)])
        zrk = [("Zr", m4) for m4 in range(8)]
        zik = [("Zi", m4) for m4 in range(8)]
        pg.op("dve", lambda e: e.tensor_tensor(out=ta[:], in0=Zr[:], in1=twc[:], op=ALU.mult), r=zrk + [("twc",)], w=[("ta",)])
        pg.op("pool", lambda e: e.tensor_tensor(out=tb2[:], in0=Zi[:], in1=tws[:], op=ALU.mult), r=zik + [("tws",)], w=[("tb2",)])
        pg.op("dve", lambda e: e.tensor_tensor(out=Zpr[:], in0=ta[:], in1=tb2[:], op=ALU.add), r=[("ta",), ("tb2",)], w=[("Zpr",)])
        pg.op("dve", lambda e: e.tensor_tensor(out=ta[:], in0=Zi[:], in1=twc[:], op=ALU.mult), r=zik + [("twc",), ("ta",)], w=[("ta",)])
        pg.op("pool", lambda e: e.tensor_tensor(out=tb2[:], in0=Zr[:], in1=tws[:], op=ALU.mult), r=zrk + [("tws",), ("tb2",)], w=[("tb2",)])
        pg.op("dve", lambda e: e.tensor_tensor(out=Zpi[:], in0=ta[:], in1=tb2[:], op=ALU.subtract), r=[("ta",), ("tb2",)], w=[("Zpi",)])
        for blk in range(8):
            bk = 6 + blk % 2
            pg.op("pe", (lambda e, blk=blk, bk=bk: e.matmul(cx.psb[bk][:, :], tb["cm3"][:], Zpr[:, blk * 512:(blk + 1) * 512],
                                                           start=True, stop=False)), r=[("Zpr",), ("cm3",)], w=[("ps", bk)])
            pg.op("pe", (lambda e, blk=blk, bk=bk: e.matmul(cx.psb[bk][:, :], tb["sm3"][:], Zpi[:, blk * 512:(blk + 1) * 512],
                                                           start=False, stop=True)), r=[("Zpi",), ("sm3",)], w=[("ps", bk)])
            pg.op("act", (lambda e, blk=blk, bk=bk: e.activation(out=Zr[:, blk * 512:(blk + 1) * 512], in_=cx.psb[bk][:, :], func=AF.Copy)),
                  r=[("ps", bk), ("ta",), ("tb2",)], w=[("Zr", blk)])
        pg.dma("sp", y_d, Zr[:], r=zrk, w=[("y",)])
        for tc in range(2):
            pg.op("pe", (lambda e, tc=tc: e.matmul(cx.psb[tc][:, 0:64], uT[:, SEQ + tc * 128:SEQ + (tc + 1) * 128], tb["ccs"][:],
                                                  start=True, stop=True)), r=[("uT",), ("ccs",)], w=[("ps", tc)])
            pg.op("act", (lambda e, tc=tc: e.activation(out=Xc[:, tc, :], in_=cx.psb[tc][:, 0:64], func=AF.Copy)),
                  r=[("ps", tc)], w=[("Xc", tc)])
        for nt in range(2):
            bk = 2 + nt
            for tc in range(2):
                pg.op("pe", (lambda e, nt=nt, tc=tc, bk=bk: e.matmul(cx.psb[bk][:, 0:32], c256[:, tc, nt * 128:(nt + 1) * 128],
                                                                    Xc[:, tc, 0:32], start=(tc == 0), stop=False)),
                      r=[("Xc", 0), ("Xc", 1), ("c256",)], w=[("ps", bk)])
                pg.op("pe", (lambda e, nt=nt, tc=tc, bk=bk: e.matmul(cx.psb[bk][:, 0:32], s256[:, tc, nt * 128:(nt + 1) * 128],
                                                                    Xc[:, tc, 32:64], start=False, stop=(tc == 1))),
                      r=[("s256",)], w=[("ps", bk)])
            pg.op("act", (lambda e, nt=nt, bk=bk: e.activation(out=ycs[:, nt, :], in_=cx.psb[bk][:, 0:32], func=AF.Copy)),
                  r=[("ps", bk)], w=[("ycs", nt)])
        pg.dma("sp", yc_d, ycs[:], r=[("ycs", 0), ("ycs", 1)], w=[("yc",)])
        pg.emit(cx.scratch)
    return nc


def build_C():
    nc = new_prog()
    with ExitStack() as es:
        cx = setup_common(nc, es)
        pg = cx.pg
        xin = din(nc, "xin", [TOK, D])
        xs = dout(nc, "xs", [TOK, D])
        xf = dout(nc, "xfin", [OWN, D])
        w1 = din(nc, "w1", [D, DFF])
        w3 = din(nc, "w3", [D, DFF])
        w2 = din(nc, "w2", [DFF, D])
        wgt_d = din(nc, "wgt", [D, 3072])
        wbf_d = din(nc, "wbf", [256, D])
        wbr_d = din(nc, "wbr", [384, D])
        wbd_d = din(nc, "wbd", [384, D])
        wo_d = din(nc, "wo", [D, D])
        foT_d = din(nc, "foT", [2, 128, TOK])
        doT_d = din(nc, "doT", [3, 128, TOK])
        rf_d = din(nc, "rf", [TOK, 384])
        rb_d = din(nc, "rb", [TOK, 384])
        rg_d = din(nc, "rg", [TOK, 384], BF16)
        fg_d = din(nc, "fgb", [128, D])
        load_mod(cx, es)
        HT = es.enter_context(nc.sbuf_tensor("HT", [128, 8, TOK], BF16))
        for t in range(NT):
            pg.dma("sp", xs[t * 128:(t + 1) * 128, :], xin[t * 128:(t + 1) * 128, :], w=[("xs", t)])
        with ExitStack() as e2:
            norm_to_HT(cx, e2, xs, 1, HT)
            sb = lambda n_, s_, d_: e2.enter_context(nc.sbuf_tensor(n_, s_, d_))
            wgt = sb("c_wgt", [128, 8, 3072], BF16)
            wb = [sb("c_wbf", [128, 2, D], BF16), sb("c_wbr", [128, 3, D], BF16), sb("c_wbd", [128, 3, D], BF16)]
            wo = sb("c_wo", [128, 8, D], BF16)
            foT = [sb("c_foT%d" % i, [128, 2, 128], BF16) for i in range(2)]
            doT = [sb("c_doT%d" % i, [128, 3, 128], BF16) for i in range(2)]
            g5 = [sb("c_g5_%d" % i, [128, D], F32) for i in range(2)]
            rf = [sb("c_rf%d" % i, [128, 384], F32) for i in range(2)]
            rb = [sb("c_rb%d" % i, [128, 384], F32) for i in range(2)]
            rg = [sb("c_rg%d" % i, [128, 384], BF16) for i in range(2)]
            sg = [sb("c_sg%d" % i, [128, 384], F32) for i in range(2)]
            rsq = sb("c_rsq", [128, 384], F32)
            rst = [sb("c_rst%d" % i, [128, 24], F32) for i in range(2)]
            rob = [sb("c_rob%d" % i, [128, 384], BF16) for i in range(2)]
            rT = [sb("c_rT%d" % i, [128, 3, 128], BF16) for i in range(2)]
            sig = [sb("c_sig%d" % i, [128, D], F32) for i in range(1)]
            prod = [sb("c_prod%d" % i, [128, D], F32) for i in range(1)]
            mixed = [sb("c_mixed%d" % i, [128, D], F32) for i in range(1)]
            mixb = [sb("c_mixb%d" % i, [128, D], BF16) for i in range(2)]
            mixT = [sb("c_mixT%d" % i, [128, 8, 128], BF16) for i in range(2)]
            xt = [sb("c_xt%d" % i, [128, D], F32) for i in range(2)]
            yt = [sb("c_yt%d" % i, [128, D], F32) for i in range(1)]
            for nb in range(6):
                pg.dma("pool", wgt[:, :, nb * 512:(nb + 1) * 512],
                       wgt_d[:, nb * 512:(nb + 1) * 512].rearrange("(c p) n -> p c n", p=128), w=[("wgt", nb)])
            for i, (wd, kc) in enumerate([(wbf_d, 2), (wbr_d, 3), (wbd_d, 3)]):
                pg.dma("pool", wb[i][:], wd.rearrange("(c p) n -> p c n", p=128), w=[("wb", i)])
            for hh in range(2):
                pg.dma("pool", wo[:, hh * 4:(hh + 1) * 4, :],
                       wo_d[hh * 512:(hh + 1) * 512, :].rearrange("(c p) n -> p c n", p=128), w=[("wo", hh)])
            for ty in range(2):
                pg.dma("sp", g5[ty][:], cx.mrows_d[ty, 5 * 1024:6 * 1024].partition_broadcast(128), w=[("g5", ty)])
            wgk = [("wgt", nb) for nb in range(6)]
            for t in range(NT):
                b = t % 2
                ty = typ_of(t)
                tsl = slice(t * 128, (t + 1) * 128)
                pg.dma("pool", foT[b][:], foT_d[:, :, tsl].rearrange("k p t -> p k t"), w=[("foT", b)])
                pg.dma("pool", doT[b][:], doT_d[:, :, tsl].rearrange("k p t -> p k t"), w=[("doT", b)])
                pg.dma("sp", rf[b][:], rf_d[tsl, :], w=[("rf", b)])
                pg.dma("sp", rb[b][:], rb_d[tsl, :], w=[("rb", b)])
                pg.dma("sp", rg[b][:], rg_d[tsl, :], w=[("rg", b)])
                pg.op("act", (lambda e, b=b: e.activation(out=sg[b][:], in_=rg[b][:], func=AF.Silu)), r=[("rg", b)], w=[("sg", b)])
                pg.op("dve", (lambda e, b=b: e.tensor_tensor(out=rf[b][:], in0=rf[b][:], in1=rb[b][:], op=ALU.add)),
                      r=[("rf", b), ("rb", b)], w=[("rf", b)])
                pg.op("pool", (lambda e, b=b: e.tensor_tensor(out=rsq[:], in0=rf[b][:], in1=rf[b][:], op=ALU.mult)),
                      r=[("rf", b)], w=[("rsq",)])
                pg.op("dve", (lambda e, b=b: e.tensor_reduce(out=rst[b][:, 0:6], in_=rsq[:].rearrange("p (h d) -> p h d", d=64),
                                                             axis=AX.X, op=ALU.add)), r=[("rsq",)], w=[("rst", b)])
                pg.op("dve", (lambda e, b=b: e.tensor_scalar(rst[b][:, 6:12], rst[b][:, 0:6], 1.0 / 64, EPS, op0=ALU.mult, op1=ALU.add)),
                      r=[("rst", b)], w=[("rst", b)])
                pg.op("act", (lambda e, b=b: e.activation(out=rst[b][:, 12:18], in_=rst[b][:, 6:12], func=AF.Sqrt)),
                      r=[("rst", b)], w=[("rst", b)])
                pg.op("dve", (lambda e, b=b: e.reciprocal(rst[b][:, 18:24], rst[b][:, 12:18])), r=[("rst", b)], w=[("rst", b)])
                for h in range(6):
                    pg.op("dve", (lambda e, b=b, h=h: e.scalar_tensor_tensor(
                        out=rob[b][:, h * 64:(h + 1) * 64], in0=rf[b][:, h * 64:(h + 1) * 64], scalar=rst[b][:, 18 + h:19 + h],
                        in1=sg[b][:, h * 64:(h + 1) * 64], op0=ALU.mult, op1=ALU.mult)),
                        r=[("rf", b), ("rst", b), ("sg", b)], w=[("rob", b, h)])
                pst = cx.psb[7][:, :].bitcast(BF16)
                for k in range(3):
                    pg.op("pe", (lambda e, k=k, b=b, pst=pst: e.transpose(pst[:, k * 128:(k + 1) * 128],
                                                                         rob[b][:, k * 128:(k + 1) * 128], cx.ident_bf[:])),
                          r=[("rob", b, 2 * k), ("rob", b, 2 * k + 1), ("ident_bf",)], w=[("ps", 7)])
                pg.op("act", (lambda e, b=b, pst=pst: e.activation(out=rT[b][:].rearrange("p a b -> p (a b)"), in_=pst[:, 0:384],
                                                                   func=AF.Copy)), r=[("ps", 7)], w=[("rT", b)])
                for br in range(3):
                    kc = [2, 3, 3][br]
                    gb = (0, 1)
                    pb_ = (2, 3) if br % 2 == 0 else (4, 5)
                    for hc in range(2):
                        for c in range(8):
                            pg.op("pe", (lambda e, c=c, hc=hc, br=br, tsl=tsl: e.matmul(
                                cx.psb[gb[hc]][:, :], HT[:, c, tsl], wgt[:, c, br * 1024 + hc * 512: br * 1024 + (hc + 1) * 512],
                                start=(c == 0), stop=(c == 7))),
                                r=(wgk + [("HT", cc, t) for cc in range(8)]) if c == 0 else [], w=[("ps", gb[hc])])
                    for hc in range(2):
                        for k in range(kc):
                            if br == 0:
                                lh = foT[b][:, k, :]
                                rk_ = [("foT", b)]
                            elif br == 1:
                                lh = rT[b][:, k, :]
                                rk_ = [("rT", b)]
                            else:
                                lh = doT[b][:, k, :]
                                rk_ = [("doT", b)]
                            pg.op("pe", (lambda e, lh=lh, k=k, hc=hc, br=br, kc=kc, pbk=pb_[hc]: e.matmul(
                                cx.psb[pbk][:, :], lh, wb[br][:, k, hc * 512:(hc + 1) * 512], start=(k == 0), stop=(k == kc - 1))),
                                r=rk_ + [("wb", br)], w=[("ps", pb_[hc])])
                    i2 = 0
                    for hc in range(2):
                        pg.op("act", (lambda e, i2=i2, hc=hc: e.activation(out=sig[i2][:, hc * 512:(hc + 1) * 512],
                                                                           in_=cx.psb[gb[hc]][:, :], func=AF.Sigmoid)),
                              r=[("ps", gb[hc])], w=[("sig", i2, hc)])
                        dst = mixed[0] if br == 0 else prod[i2]
                        dk = ("mixed", 0, hc) if br == 0 else ("prod", i2, hc)
                        pg.op("dve", (lambda e, i2=i2, hc=hc, dst=dst, pbk=pb_[hc]: e.tensor_tensor(
                            out=dst[:, hc * 512:(hc + 1) * 512], in0=sig[i2][:, hc * 512:(hc + 1) * 512], in1=cx.psb[pbk][:, :],
                            op=ALU.mult)), r=[("sig", i2, hc), ("ps", pb_[hc])], w=[dk])
                        if br > 0:
                            pg.op("pool", (lambda e, b=b, i2=i2, hc=hc: e.tensor_tensor(
                                out=mixed[0][:, hc * 512:(hc + 1) * 512], in0=mixed[0][:, hc * 512:(hc + 1) * 512],
                                in1=prod[i2][:, hc * 512:(hc + 1) * 512], op=ALU.add)),
                                r=[("mixed", 0, hc), ("prod", i2, hc)], w=[("mixed", 0, hc)])
                pg.op("act", (lambda e, b=b: e.activation(out=mixb[b][:], in_=mixed[0][:], func=AF.Copy)),
                      r=[("mixed", 0, 0), ("mixed", 0, 1)], w=[("mixb", b)])
                pst6 = cx.psb[6][:, :].bitcast(BF16)
                for c in range(8):
                    pg.op("pe", (lambda e, c=c, b=b, pst6=pst6: e.transpose(pst6[:, c * 128:(c + 1) * 128],
                                                                           mixb[b][:, c * 128:(c + 1) * 128], cx.ident_bf[:])),
                          r=[("mixb", b), ("ident_bf",)], w=[("ps", 6)])
                pg.op("act", (lambda e, b=b, pst6=pst6: e.activation(out=mixT[b][:].rearrange("p a b -> p (a b)"), in_=pst6[:, :],
                                                                     func=AF.Copy)), r=[("ps", 6)], w=[("mixT", b)])
                yb = (2, 3) if t % 2 == 1 else (4, 5)
                for hc in range(2):
                    for c in range(8):
                        pg.op("pe", (lambda e, c=c, hc=hc, b=b, ybk=yb[hc]: e.matmul(
                            cx.psb[ybk][:, :], mixT[b][:, c, :], wo[:, c, hc * 512:(hc + 1) * 512], start=(c == 0), stop=(c == 7))),
                            r=[("mixT", b), ("wo", 0), ("wo", 1)] if c == 0 else [], w=[("ps", yb[hc])])
                pg.dma("sp", xt[b][:], xs[tsl, :], r=[("xs", t)], w=[("c_xt", b)])
                for hc in range(2):
                    pg.op("dve", (lambda e, hc=hc, b=b, ty=ty, ybk=yb[hc]: e.tensor_tensor(
                        out=yt[0][:, hc * 512:(hc + 1) * 512], in0=cx.psb[ybk][:, :], in1=g5[ty][:, hc * 512:(hc + 1) * 512],
                        op=ALU.mult)), r=[("ps", yb[hc]), ("g5", ty)], w=[("c_yt", 0, hc)])
                pg.op("pool", (lambda e, b=b: e.tensor_tensor(out=xt[b][:], in0=xt[b][:], in1=yt[0][:], op=ALU.add)),
                      r=[("c_yt", 0, 0), ("c_yt", 0, 1), ("c_xt", b)], w=[("c_xt", b)])
                pg.dma("sp", xs[tsl, :], xt[b][:], r=[("c_xt", b)], w=[("xs", t)])
            pg.emit(cx.scratch)
        ffn(cx, xs, w1, w3, w2, 2, HT, "f2")
        with ExitStack() as e3:
            sb = lambda n_, s_, d_: e3.enter_context(nc.sbuf_tensor(n_, s_, d_))
            fg = sb("z_fg", [128, D], F32)
            xt = [sb("z_xt%d" % i, [128, D], F32) for i in range(2)]
            junk = sb("z_junk", [128, D], BF16)
            st = [sb("z_st%d" % i, [128, 4], F32) for i in range(2)]
            pg.dma("sp", fg[:], fg_d, w=[("fg",)])
            for t in range(NTO):
                b = t % 2
                tsl = slice(t * 128, (t + 1) * 128)
                pg.dma("sp", xt[b][:], xs[tsl, :], r=[("xs", t)], w=[("z_xt", b)])
                pg.op("dve", (lambda e, b=b: e.memset(st[b][:], 0.0)), w=[("z_st", b)])
                pg.op("act", (lambda e, b=b: e.activation(out=junk[:], in_=xt[b][:], func=AF.Square, accum_out=st[b][:, 0:1])),
                      r=[("z_xt", b)], w=[("z_junk",), ("z_st", b)])
                pg.op("dve", (lambda e, b=b: e.tensor_scalar(st[b][:, 1:2], st[b][:, 0:1], 1.0 / D, EPS, op0=ALU.mult, op1=ALU.add)),
                      r=[("z_st", b)], w=[("z_st", b)])
                pg.op("act", (lambda e, b=b: e.activation(out=st[b][:, 2:3], in_=st[b][:, 1:2], func=AF.Sqrt)),
                      r=[("z_st", b)], w=[("z_st", b)])
                pg.op("dve", (lambda e, b=b: e.reciprocal(st[b][:, 3:4], st[b][:, 2:3])), r=[("z_st", b)], w=[("z_st", b)])
                pg.op("dve", (lambda e, b=b: e.scalar_tensor_tensor(out=xt[b][:], in0=xt[b][:], scalar=st[b][:, 3:4], in1=fg[:],
                                                                    op0=ALU.mult, op1=ALU.mult)),
                      r=[("z_xt", b), ("z_st", b), ("fg",)], w=[("z_xt", b)])
                pg.dma("sp", xf[tsl, :], xt[b][:], r=[("z_xt", b)], w=[("xf", t)])
            pg.emit(cx.scratch)
    return nc


_PROGS = {}


def _run(name, maps):
    if name not in _PROGS:
        _PROGS[name] = {"ada": build_ada, "A": build_A, "ATT": build_ATT, "RET": build_RET,
                        "FFT": build_FFT, "C": build_C}[name]()
    ident = np.eye(128, dtype=np.float32)
    for m in maps:
        m["ident_in"] = ident
        for k in list(m.keys()):
            m[k] = np.ascontiguousarray(m[k])
    res = run_bass_kernel_spmd(_PROGS[name], maps, core_ids=list(range(NCORES)))
    return res.results


def _rope_tables():
    f32 = np.float32
    tabs = []
    inv_ret = (f32(10000.0) ** (-np.arange(0, 32, 2, dtype=f32) / f32(32))).astype(f32)
    inv_ax = (f32(10000.0) ** (-np.arange(0, 16, 2, dtype=f32) / f32(16))).astype(f32)
    pos = np.arange(SEQ, dtype=f32)
    a_ret = (pos[:, None] * inv_ret[None, :]).astype(f32)
    a_row = (np.floor(pos / GRID_W).astype(f32)[:, None] * inv_ax[None, :]).astype(f32)
    a_col = ((pos % GRID_W).astype(f32)[:, None] * inv_ax[None, :]).astype(f32)

    def cs(a):
        c = np.cos(a).astype(f32)
        s = np.sin(a).astype(f32)
        return np.concatenate([c, c], 1), np.concatenate([-s, s], 1)

    cr, sr = cs(a_ret)
    c1, s1 = cs(a_row)
    c2, s2 = cs(a_col)
    ca = np.concatenate([c1, c2], 1)
    sa = np.concatenate([s1, s2], 1)
    sc = f32(32.0 ** -0.5)
    cosL = np.concatenate([np.tile(cr, (1, 6)), np.tile(cr, (1, 6)) * sc, np.tile(ca, (1, 12)), np.tile(ca, (1, 12))], 1)
    sinL = np.concatenate([np.tile(sr, (1, 6)), np.tile(sr, (1, 6)) * sc, np.tile(sa, (1, 12)), np.tile(sa, (1, 12))], 1)
    cosC = np.ones((CTX, 1152), f32)
    cosC[:, 192:384] = sc
    sinC = np.zeros((CTX, 1152), f32)
    return cosL.astype(f32), sinL.astype(f32), cosC, sinC


def _fft_consts(hf):
    f64 = np.float64
    a = np.arange(128, dtype=f64)
    ang = 2 * np.pi * np.outer(a, a) / 128.0
    c = np.arange(64, dtype=f64)
    m = 32 * hf + np.arange(32, dtype=f64)
    angc = 2 * np.pi * np.outer(c, m) / 64.0
    ccs = np.concatenate([np.cos(angc), -np.sin(angc)], 1) / 8.0
    phi = 2 * np.pi * np.outer(a, a) / float(SEQ)
    twc = np.tile(np.cos(phi), (1, 32))
    tws = np.tile(np.sin(phi), (1, 32))
    t = np.arange(256, dtype=f64)
    a256 = 2 * np.pi * np.outer(t, t) / 256.0
    c256 = (np.cos(a256) / 16.0).reshape(2, 128, 256).transpose(1, 0, 2)
    s256 = (np.sin(a256) / 16.0).reshape(2, 128, 256).transpose(1, 0, 2)
    return dict(ccs=bf(ccs), cm1=bf(np.cos(ang) / 128.0), sm1=bf(np.sin(ang) / 128.0), nsm1=bf(-np.sin(ang) / 128.0),
                cm3=bf(np.cos(ang)), sm3=bf(np.sin(ang)), twc=twc.astype(np.float32), tws=tws.astype(np.float32),
                c256=bf(c256), s256=bf(s256))


def kernel(x, c, ctx, c_ctx, w_ada, b_ada, norm_g, ffn_w1, ffn_w3, ffn_w2, w_in, ret_decay_logit,
           diff_lambda, diff_subln, w_branch_f, w_branch_r, w_branch_d, w_out, final_g, depth=DEPTH):
    f32 = np.float32
    R = NCORES
    cc = np.stack([np.asarray(c)[0], np.asarray(c_ctx)], 0)
    ccl = cc.reshape(2, 8, 128).transpose(2, 1, 0)
    maps = []
    for r in range(R):
        l, h = r // 2, r % 2
        maps.append(dict(cc=ccl, wa=w_ada[l][:, h * 4608:(h + 1) * 4608],
                         ba=np.broadcast_to(b_ada[l][h * 4608:(h + 1) * 4608], (2, 4608))))
    res = _run("ada", maps)
    m_all = np.stack([np.concatenate([res[2 * l]["mo"], res[2 * l + 1]["mo"]], 1) for l in range(4)], 0)
    cosL, sinL, cosC, sinC = _rope_tables()
    jj = np.arange(128)
    umat = np.maximum(jj[None, :] - jj[:, None], 0).astype(f32)
    mmat = (jj[None, :] >= jj[:, None]).astype(f32)
    irow = np.broadcast_to((jj + 1).astype(f32)[None, :], (128, 128))
    pcol = (127 - jj).astype(f32)[:, None]
    fftc = [_fft_consts(0), _fft_consts(1)]
    x_lat = np.asarray(x)[0]
    x_ctx = np.asarray(ctx)[0]
    xfin = None
    for l in range(depth):
        lam_init = 0.8 - 0.6 * math.exp(-0.3 * l)
        m = m_all[l]
        mods = dict(mcols=m.reshape(2, 72, 128).transpose(0, 2, 1), mrows=m, gcols=norm_g[l].reshape(24, 128).T)
        maps = []
        for r in range(R):
            d_ = dict(mods)
            d_.update(xin=np.concatenate([x_lat[r * OWN:(r + 1) * OWN], x_ctx], 0), w1=ffn_w1[l, 0], w3=ffn_w3[l, 0],
                      w2=ffn_w2[l, 0], win=w_in[l][:, :NPA],
                      ropec=np.concatenate([cosL[r * OWN:(r + 1) * OWN], cosC], 0),
                      ropes=np.concatenate([sinL[r * OWN:(r + 1) * OWN], sinC], 0))
            maps.append(d_)
        resA = _run("A", maps)
        xsA = [resA[r]["xs"] for r in range(R)]
        pal = [resA[r]["pall"] for r in range(R)]
        P_lat = np.concatenate([p[:OWN] for p in pal], 0)
        P_ctx = pal[0][OWN:]
        P_all = np.concatenate([P_lat, P_ctx], 0)
        kT = P_all[:, C_DK:C_DK + 384].reshape(SEQ + CTX, 6, 64).transpose(1, 2, 0)
        vv = P_all[:, C_DV:C_DV + 384].reshape(NKT, 128, 6, 64).transpose(2, 1, 0, 3)
        lamv = np.broadcast_to(np.asarray(diff_lambda[l]).reshape(1, 128), (128, 128))
        cst = np.broadcast_to(np.array([lam_init, 1.0 - lam_init, 0, 0], f32)[None, :], (128, 4))
        sub = np.broadcast_to(np.asarray(diff_subln[l])[None, :], (128, 64))
        maps = []
        for r in range(R):
            q = pal[r][:, C_DQ:C_DQ + 384].reshape(TOK, 6, 64).transpose(1, 2, 0)
            maps.append(dict(qT=q, kT=kT, vv=vv, lamv=lamv, cst=cst, subln=sub))
        resT = _run("ATT", maps)
        units = []
        for u in range(12):
            dr, h = u // 6, u % 6
            if dr == 0:
                seq = np.concatenate([P_ctx, P_lat], 0)
            else:
                seq = np.concatenate([P_ctx[::-1], P_lat[::-1]], 0)
            q = seq[:, C_RQ + h * 32:C_RQ + (h + 1) * 32]
            k = seq[:, C_RK + h * 32:C_RK + (h + 1) * 32]
            v = seq[:, C_RV + h * 64:C_RV + (h + 1) * 64]
            units.append(dict(qT=q.T, kT=k.T, kk=k.reshape(NKT, 128, 32).transpose(1, 0, 2),
                              vv=v.reshape(NKT, 128, 64).transpose(1, 0, 2),
                              lg=np.broadcast_to(np.asarray(ret_decay_logit[l, dr, h], f32).reshape(1, 1), (128, 1))))
        maps = []
        for r in range(R):
            d_ = dict(umat=umat, mmat=mmat, irow=irow, pcol=pcol)
            for i in range(2):
                u = (2 * r + i) % 12
                for k_, v_ in units[u].items():
                    d_["%s%d" % (k_, i)] = v_
            maps.append(d_)
        resR = _run("RET", maps)
        rdir = [np.zeros((SEQ + CTX, 384), f32), np.zeros((SEQ + CTX, 384), f32)]
        for u in range(12):
            dr, h = u // 6, u % 6
            ro = resR[u // 2]["ro%d" % (u % 2)]
            o = ro.transpose(1, 0, 2).reshape(SEQ + CTX, 64)
            if dr == 1:
                o = np.concatenate([o[:CTX][::-1], o[CTX:][::-1]], 0)
            rdir[dr][:, h * 64:(h + 1) * 64] = o
        maps = []
        for r in range(R):
            g, hf = r // 2, r % 2
            d_ = dict(fftc[hf])
            d_["uT"] = P_all[:, g * 64:(g + 1) * 64].T
            maps.append(d_)
        resF = _run("FFT", maps)
        F_lat = np.zeros((SEQ, 256), f32)
        F_ctx = np.zeros((CTX, 256), f32)
        for r in range(R):
            g, hf = r // 2, r % 2
            y = resF[r]["y"].reshape(128, 32, 128)
            F_lat[:, g * 64 + 32 * hf:g * 64 + 32 * hf + 32] = y.transpose(0, 2, 1).reshape(SEQ, 32)
            yc = resF[r]["yc"]
            F_ctx[:, g * 64 + 32 * hf:g * 64 + 32 * hf + 32] = yc.transpose(1, 0, 2).reshape(CTX, 32)
        fgb = np.broadcast_to(np.asarray(final_g)[None, :], (128, D))
        maps = []
        for r in range(R):
            sl = slice(r * OWN, (r + 1) * OWN)
            Ft = np.concatenate([F_lat[sl], F_ctx], 0)
            Dt = resT[r]["dout"]
            d_ = dict(mods)
            d_.update(xin=xsA[r], w1=ffn_w1[l, 1], w3=ffn_w3[l, 1], w2=ffn_w2[l, 1], wgt=w_in[l][:, NPA:],
                      wbf=w_branch_f[l], wbr=w_branch_r[l], wbd=w_branch_d[l], wo=w_out[l],
                      foT=Ft.T.reshape(2, 128, TOK), doT=Dt.T.reshape(3, 128, TOK),
                      rf=np.concatenate([rdir[0][CTX:][sl], rdir[0][:CTX]], 0),
                      rb=np.concatenate([rdir[1][CTX:][sl], rdir[1][:CTX]], 0),
                      rg=pal[r][:, C_RG:C_RG + 384], fgb=fgb)
            maps.append(d_)
        resC = _run("C", maps)
        x_lat = np.concatenate([resC[r]["xs"][:OWN] for r in range(R)], 0)
        x_ctx = resC[0]["xs"][OWN:]
        xfin = np.concatenate([resC[r]["xfin"] for r in range(R)], 0)
    return xfin[None].astype(np.float32)
```

```python
import math
import re
from contextlib import ExitStack

import numpy as np
import ml_dtypes

import concourse.bass as bass
import concourse.mybir as mybir
from concourse.bass_utils import run_bass_kernel_spmd

F32 = mybir.dt.float32
BF16 = mybir.dt.bfloat16
AF = mybir.ActivationFunctionType
ALU = mybir.AluOpType
AX = mybir.AxisListType

NCORES = 8
D = 1024
SEQ = 16384
DEPTH = 4
GRID_W = 64
CTX = 256
DFF = 2816
NJ = DFF // 128
OWN = SEQ // NCORES
TOK = OWN + CTX
NT = TOK // 128
NTO = OWN // 128
EPS = 1e-6
F_W = 256
RQW = 192
RVW = 384
DQW = 384
DVW = 384
C_UF = 0
C_RQ = 256
C_RK = 448
C_RV = 640
C_RG = 1024
C_DQ = 1408
C_DK = 1792
C_DV = 2176
C_GL = 2560
D_IN = 5632
NKT = (SEQ + CTX) // 128

KD = 6
ENGS = ["pe", "act", "dve", "pool", "sp"]


class Prog:
    def __init__(self, nc, es):
        self.nc = nc
        self.es = es
        self.ops = []
        self.sem = {}
        for e in ["pe", "act", "dve", "pool"]:
            self.sem[("eng", e)] = es.enter_context(nc.semaphore("s_" + e))
        for q in ["sp", "pool", "act"]:
            for s in range(KD):
                self.sem[("dma", q, s)] = es.enter_context(nc.semaphore("d_%s%d" % (q, s)))
        self.sem[("cc",)] = es.enter_context(nc.semaphore("s_cc"))
        self.ccnt = 0
        self.dyn_src = None
        self.dynval = {}
        self.cnt = {e: 0 for e in ENGS}
        self.dcnt = {q: 0 for q in ["sp", "pool", "act"]}
        self.dlast = {q: [] for q in ["sp", "pool", "act"]}
        self.last_tok = {}
        self.nphase = 0

    def op(self, eng, fn, r=(), w=()):
        self.ops.append(dict(eng=eng, fn=fn, r=tuple(r), w=tuple(w), dma=False))

    def dma(self, q, out, in_, r=(), w=(), **kw):
        self.ops.append(dict(eng=q, fn=(lambda e: e.dma_start(out=out, in_=in_, **kw)),
                             r=tuple(r), w=tuple(w), dma=True))

    def dmaf(self, q, fn, r=(), w=()):
        def _f(e, q=q):
            ins = fn(e, self.dynval[q])
            names = set()
            for nm0 in set(re.findall(r"R\[(\w*tmp_\d+)\]", ins.concise())):
                pre, num = nm0.rsplit("_", 1)
                for k in range(0, 5):
                    names.add("%s_%d" % (pre, int(num) - k))
            if not hasattr(self, "pending_free"):
                self.pending_free = {}
            self.pending_free.setdefault(q, []).extend(sorted(names))
            return ins
        self.ops.append(dict(eng=q, fn=_f, r=tuple(r), w=tuple(w), dma=True, dyn=True))

    def coll(self, ins, outs, r=(), w=()):
        self.ops.append(dict(eng="pool", fn=(lambda e: e.collective_compute(
            "AllGather", ALU.bypass, replica_groups=[list(range(NCORES))], ins=[ins], outs=[outs])),
            r=tuple(r) + (("cc_order",),), w=tuple(w) + (("cc_order",),), dma=True, coll=True))

    def emit(self, scratch):
        nc = self.nc
        ops = self.ops
        sc = scratch
        self.op("pe", lambda e: e.matmul(sc["ps"][0:1, 0:1], sc["one_bf"][0:1, 0:1], sc["one_bf"][0:1, 0:1],
                                         start=True, stop=True), r=[], w=[("bar", "pe")] + sc["pskey"])
        self.op("act", lambda e: e.activation(out=sc["s_act"][0:1, 0:1], in_=sc["one_f"][0:1, 0:1], func=AF.Copy),
                w=[("bar", "act")])
        self.op("dve", lambda e: e.memset(sc["s_dve"][0:1, 0:1], 0.0), w=[("bar", "dve")])
        self.op("pool", lambda e: e.memset(sc["s_pool"][0:1, 0:1], 0.0), w=[("bar", "pool")])
        n = len(ops)
        last_w = {}
        rd_eng = {}
        rd_dma = {}
        deps = [None] * n
        for i, o in enumerate(ops):
            d = set()
            for k in o["r"]:
                if k in last_w:
                    d.add(last_w[k])
            for k in o["w"]:
                if k in last_w:
                    d.add(last_w[k])
                for j in rd_eng.get(k, {}).values():
                    d.add(j)
                for j in rd_dma.get(k, ()):
                    d.add(j)
            d.discard(i)
            deps[i] = d
            for k in o["w"]:
                last_w[k] = i
                rd_eng[k] = {}
                rd_dma[k] = []
            for k in o["r"]:
                if k in o["w"]:
                    continue
                if o["dma"]:
                    rd_dma.setdefault(k, []).append(i)
                else:
                    rd_eng.setdefault(k, {})[o["eng"]] = i
        qhist = {q: [] for q in self.dcnt}
        dma_prev_tok = [None] * n
        for i, o in enumerate(ops):
            if o["dma"] and not o.get("coll"):
                q = o["eng"]
                k = self.dcnt[q] + len(qhist[q])
                o["didx"] = k
                if len(qhist[q]) >= KD:
                    deps[i].add(qhist[q][-KD])
                elif k >= KD:
                    dma_prev_tok[i] = (("dma", q, k % KD), 16 * (k // KD))
                qhist[q].append(i)
        need = [False] * n
        for i, o in enumerate(ops):
            for j in deps[i]:
                pj = ops[j]
                if pj["dma"]:
                    continue
                if pj["eng"] == "pe" and o["eng"] == "pe" and not o["dma"]:
                    continue
                need[j] = True
        for i in range(n - 4, n):
            need[i] = True
        token = [None] * n
        for i, o in enumerate(ops):
            if o.get("coll"):
                self.ccnt += 1
                token[i] = (("cc",), self.ccnt)
            elif o["dma"]:
                k = o["didx"]
                token[i] = (("dma", o["eng"], k % KD), 16 * (k // KD + 1))
            elif need[i]:
                self.cnt[o["eng"]] += 1
                token[i] = (("eng", o["eng"]), self.cnt[o["eng"]])
        for q in self.dcnt:
            self.dcnt[q] += len(qhist[q])
        final_toks = [token[i] for i in range(n - 4, n)]
        for q in qhist:
            for i in qhist[q][-KD:]:
                final_toks.append(token[i])
        if self.ccnt > 0:
            final_toks.append((("cc",), self.ccnt))
        use_dyn = any(o.get("dyn") for o in ops)
        sem = self.sem
        stream = {e: [] for e in ENGS}
        for i, o in enumerate(ops):
            stream[o["eng"]].append(i)

        trace = {en: [] for en in ENGS}

        def run(e, ename):
            seen = {}
            tr = trace[ename]
            pend = getattr(self, "pending_free", {}).pop(ename, [])
            if pend:
                RH = type(e.zero_reg)
                for nm in pend:
                    try:
                        e.free_register(RH(nm, e.engine))
                    except Exception:
                        pass
            if ename in ("sp", "pool", "act") and any(ops[i_].get("dyn") for i_ in stream[ename]) \
                    and ename not in self.dynval:
                vals = {}
                for k_ in {"sp": (0, 1, 3), "act": (2, 3, 4), "pool": (0,)}[ename]:
                    reg = e.alloc_register("dyn_%s_%d" % (ename, k_))
                    e.reg_load(reg, self.dyn_src[0:1, k_:k_ + 1])
                    vals[k_] = e.snap(reg, min_val=0, max_val=[112, 416, 608, 640, 192][k_], guaranteed_mod_val=0,
                                      out_of_modulus=[16, 32, 32, 128, 64][k_])
                self.dynval[ename] = vals

            def wait(tk):
                if tk is None:
                    return
                key, val = tk
                if seen.get(key, 0) >= val:
                    return
                e.wait_ge(sem[key], val)
                tr.append(("w", key, val))
                seen[key] = val

            for i in stream[ename]:
                o = ops[i]
                wait(dma_prev_tok[i])
                for j in sorted(deps[i]):
                    pj = ops[j]
                    if (not pj["dma"]) and pj["eng"] == "pe" and ename == "pe" and not o["dma"]:
                        continue
                    wait(token[j])
                ins = o["fn"](e)
                if token[i] is not None:
                    amt = 16 if (o["dma"] and not o.get("coll")) else 1
                    ins.then_inc(sem[token[i][0]], amt)
                    tr.append(("i", token[i][0], amt))
            for tk in final_toks:
                wait(tk)

        with nc.Block() as block:
            @block.tensor
            def _(e):
                run(e, "pe")

            @block.scalar
            def _(e):
                run(e, "act")

            @block.vector
            def _(e):
                run(e, "dve")

            @block.gpsimd
            def _(e):
                run(e, "pool")

            @block.sync
            def _(e):
                run(e, "sp")
        self.check(trace)
        self.ops = []
        self.nphase += 1

    def check(self, trace):
        if not hasattr(self, "simsem"):
            self.simsem = {}
        ss = self.simsem
        ptr = {en: 0 for en in ENGS}
        prog = True
        while prog:
            prog = False
            for en in ENGS:
                tr = trace[en]
                while ptr[en] < len(tr):
                    kind, key, val = tr[ptr[en]]
                    if kind == "i":
                        ss[key] = ss.get(key, 0) + val
                    elif ss.get(key, 0) < val:
                        break
                    ptr[en] += 1
                    prog = True
        for en in ENGS:
            if ptr[en] < len(trace[en]):
                raise RuntimeError("DEADLOCK phase %d engine %s at %d/%d: %s (sem=%s)" % (
                    self.nphase, en, ptr[en], len(trace[en]), trace[en][ptr[en]], ss.get(trace[en][ptr[en]][1])))


def bf(a):
    return np.ascontiguousarray(a).astype(ml_dtypes.bfloat16)


class Ctx:
    pass


def setup_common(nc, es):
    cx = Ctx()
    cx.nc = nc
    cx.es = es
    cx.pg = Prog(nc, es)
    cx.psb2 = [es.enter_context(nc.psum_tensor("psb%d" % i, [128, 1024], F32)) for i in range(4)]
    cx.psb = [cx.psb2[i // 2][:, (i % 2) * 512:(i % 2 + 1) * 512] for i in range(8)]
    cx.ident_bf = es.enter_context(nc.sbuf_tensor("ident_bf", [128, 128], BF16))
    cx.ident_f = es.enter_context(nc.sbuf_tensor("ident_f", [128, 128], F32))
    cx.one_bf = es.enter_context(nc.sbuf_tensor("one_bf", [128, 128], BF16))
    cx.one_f = es.enter_context(nc.sbuf_tensor("one_f", [128, 128], F32))
    cx.s_act = es.enter_context(nc.sbuf_tensor("s_act", [128, 8], F32))
    cx.s_dve = es.enter_context(nc.sbuf_tensor("s_dve", [128, 8], F32))
    cx.s_pool = es.enter_context(nc.sbuf_tensor("s_pool", [128, 8], F32))
    cx.scratch = dict(ps=cx.psb[7], pskey=[("ps", 7)], one_bf=cx.one_bf, one_f=cx.one_f,
                      s_act=cx.s_act, s_dve=cx.s_dve, s_pool=cx.s_pool)
    cx.ident_d = nc.dram_tensor("ident_in", [128, 128], F32, kind="ExternalInput").ap()
    pg = cx.pg
    pg.dma("sp", cx.ident_f[:], cx.ident_d, w=[("ident_f",)])
    pg.dma("pool", cx.ident_bf[:], cx.ident_d, w=[("ident_bf",)])
    pg.op("dve", lambda e: e.memset(cx.one_bf[:], 1.0), w=[("one_bf",)])
    pg.op("dve", lambda e: e.memset(cx.one_f[:], 1.0), w=[("one_f",)])
    return cx


def build_ada():
    nc = bass.Bass("TRN2", target_bir_lowering=False)
    NCOL = 4608
    with ExitStack() as es:
        cx = setup_common(nc, es)
        pg = cx.pg
        cc = nc.dram_tensor("cc", [128, 8, 2], F32, kind="ExternalInput").ap()
        wa = nc.dram_tensor("wa", [1024, NCOL], F32, kind="ExternalInput").ap()
        ba = nc.dram_tensor("ba", [2, NCOL], F32, kind="ExternalInput").ap()
        mo = nc.dram_tensor("mo", [2, NCOL], F32, kind="ExternalOutput").ap()
        ccs = es.enter_context(nc.sbuf_tensor("ccs", [128, 8, 2], F32))
        scs = es.enter_context(nc.sbuf_tensor("scs", [128, 8, 2], F32))
        bas = es.enter_context(nc.sbuf_tensor("bas", [2, NCOL], F32))
        mos = es.enter_context(nc.sbuf_tensor("mos", [2, NCOL], F32))
        wsb = [es.enter_context(nc.sbuf_tensor("wsb%d" % i, [128, 8, 512], F32)) for i in range(2)]
        pg.dma("sp", ccs[:], cc, w=[("ccs",)])
        pg.dma("sp", bas[:], ba, w=[("bas",)])
        pg.op("act", lambda e: e.activation(out=scs[:], in_=ccs[:], func=AF.Silu), r=[("ccs",)], w=[("scs",)])
        for cb in range(NCOL // 512):
            b = cb % 2
            pg.dma("sp", wsb[b][:], wa[:, cb * 512:(cb + 1) * 512].rearrange("(c p) n -> p c n", p=128),
                   w=[("wsb", b)])
            for c in range(8):
                pg.op("pe", (lambda e, c=c, b=b: e.matmul(cx.psb[b][0:2, :], scs[:, c, :], wsb[b][:, c, :],
                                                          start=(c == 0), stop=(c == 7))),
                      r=[("scs",), ("wsb", b)], w=[("ps", b)])
            pg.op("dve", (lambda e, cb=cb, b=b: e.tensor_tensor(out=mos[:, cb * 512:(cb + 1) * 512],
                                                                in0=cx.psb[b][0:2, :],
                                                                in1=bas[:, cb * 512:(cb + 1) * 512], op=ALU.add)),
                  r=[("ps", b), ("bas",)], w=[("mos", cb)])
        pg.dma("sp", mo, mos[:], r=[("mos", cb) for cb in range(NCOL // 512)], w=[("mo",)])
        pg.emit(cx.scratch)
    return nc


def load_mod(cx, es, mcols_d=None, mrows_d=None, gcols_d=None, sfx=""):
    nc, pg = cx.nc, cx.pg
    kw = {}
    if mcols_d is None:
        cx.mcols_d = nc.dram_tensor("mcols", [2, 128, 72], F32, kind="ExternalInput").ap()
        cx.mrows_d = nc.dram_tensor("mrows", [2, 9216], F32, kind="ExternalInput").ap()
        cx.gcols_d = nc.dram_tensor("gcols", [128, 24], F32, kind="ExternalInput").ap()
    else:
        cx.mcols_d, cx.mrows_d, cx.gcols_d = mcols_d, mrows_d, gcols_d
        kw = dict(allow_slow_non_contiguous=True)
    cx.mcol = [es.enter_context(nc.sbuf_tensor("mcol%d%s" % (t, sfx), [128, 72], F32)) for t in range(2)]
    cx.gcol = es.enter_context(nc.sbuf_tensor("gcol" + sfx, [128, 24], F32))
    cx.Gc = [es.enter_context(nc.sbuf_tensor("Gc%d%s" % (t, sfx), [128, 24], F32)) for t in range(2)]
    for t in range(2):
        if kw:
            for kg in range(8):
                pg.dma("sp", cx.mcol[t][:, kg * 9:(kg + 1) * 9], cx.mcols_d[t][:, kg * 9:(kg + 1) * 9],
                       r=[("m_all",)], w=[("mcol", t)], **kw)
        else:
            pg.dma("sp", cx.mcol[t][:], cx.mcols_d[t], r=[("m_all",)], w=[("mcol", t)], **kw)
    pg.dma("sp", cx.gcol[:], cx.gcols_d, w=[("gcol",)])
    for t in range(2):
        for s in range(3):
            pg.op("dve", (lambda e, t=t, s=s: e.scalar_tensor_tensor(
                out=cx.Gc[t][:, s * 8:(s + 1) * 8], in0=cx.mcol[t][:, (3 * s + 1) * 8:(3 * s + 2) * 8], scalar=1.0,
                in1=cx.gcol[:, s * 8:(s + 1) * 8], op0=ALU.add, op1=ALU.mult)),
                r=[("mcol", t), ("gcol",)], w=[("Gc", t, s)])


def typ_of(t):
    return 0 if t < NTO else 1


def norm_to_HT(cx, es, xs, s, HT, ntiles=NT, gfinal=None):
    nc, pg = cx.nc, cx.pg
    xt = [es.enter_context(nc.sbuf_tensor("n_xt%d_p%d" % (i, cx.pg.nphase), [128, 1024], F32)) for i in range(2)]
    xn = [es.enter_context(nc.sbuf_tensor("n_xn%d_p%d" % (i, cx.pg.nphase), [128, 1024], BF16)) for i in range(2)]
    junk = es.enter_context(nc.sbuf_tensor("n_junk_p%d" % cx.pg.nphase, [128, 1024], BF16))
    st = [es.enter_context(nc.sbuf_tensor("n_st%d_p%d" % (i, cx.pg.nphase), [128, 4], F32)) for i in range(2)]
    for t in range(ntiles):
        b = t % 2
        ty = typ_of(t)
        pg.dma("sp", xt[b][:], xs[t * 128:(t + 1) * 128, :], r=[("xs", t)], w=[("n_xt", b)])
        pg.op("dve", (lambda e, b=b: e.memset(st[b][:], 0.0)), w=[("n_st", b)])
        pg.op("act", (lambda e, b=b: e.activation(out=junk[:], in_=xt[b][:], func=AF.Square,
                                                  accum_out=st[b][:, 0:1])),
              r=[("n_xt", b)], w=[("n_junk",), ("n_st", b)])
        pg.op("dve", (lambda e, b=b: e.tensor_scalar(st[b][:, 1:2], st[b][:, 0:1], 1.0 / D, EPS,
                                                     op0=ALU.mult, op1=ALU.add)),
              r=[("n_st", b)], w=[("n_st", b)])
        pg.op("act", (lambda e, b=b: e.activation(out=st[b][:, 2:3], in_=st[b][:, 1:2], func=AF.Sqrt)),
              r=[("n_st", b)], w=[("n_st", b)])
        pg.op("dve", (lambda e, b=b: e.reciprocal(st[b][:, 3:4], st[b][:, 2:3])),
              r=[("n_st", b)], w=[("n_st", b)])
        pg.op("dve", (lambda e, b=b: e.tensor_scalar(xn[b][:], xt[b][:], st[b][:, 3:4], None, op0=ALU.mult)),
              r=[("n_st", b), ("n_xt", b)], w=[("n_xn", b)])
        pb = b
        pst = cx.psb[pb][:, :].bitcast(BF16)
        for c in range(8):
            pg.op("pe", (lambda e, c=c, b=b, pst=pst: e.transpose(pst[:, c * 128:(c + 1) * 128],
                                                                 xn[b][:, c * 128:(c + 1) * 128], cx.ident_bf[:])),
                  r=[("n_xn", b), ("ident_bf",)], w=[("ps", pb)])
        for c in range(8):
            eng = "act"
            if eng == "act":
                fn = (lambda e, c=c, t=t, ty=ty, pst=pst: e.activation(
                    out=HT[:, c, t * 128:(t + 1) * 128], in_=pst[:, c * 128:(c + 1) * 128], func=AF.Identity,
                    scale=cx.Gc[ty][:, s * 8 + c:s * 8 + c + 1],
                    bias=cx.mcol[ty][:, 3 * s * 8 + c:3 * s * 8 + c + 1]))
            else:
                fn = (lambda e, c=c, t=t, ty=ty, pst=pst: e.tensor_scalar(
                    HT[:, c, t * 128:(t + 1) * 128], pst[:, c * 128:(c + 1) * 128],
                    cx.Gc[ty][:, s * 8 + c:s * 8 + c + 1],
                    cx.mcol[ty][:, 3 * s * 8 + c:3 * s * 8 + c + 1], op0=ALU.mult, op1=ALU.add))
            pg.op(eng, fn, r=[("ps", pb), ("Gc", ty, s), ("mcol", ty)], w=[("HT", c, t)])


def ffn(cx, xs, w1, w3, w2, s, HT, tag):
    nc, pg = cx.nc, cx.pg
    with ExitStack() as es:
        norm_to_HT(cx, es, xs, s, HT)
        GT = es.enter_context(nc.sbuf_tensor("f_GT_p%d" % cx.pg.nphase, [128, NJ, 1152], BF16))
        W2 = es.enter_context(nc.sbuf_tensor("f_W2_p%d" % cx.pg.nphase, [128, NJ, 1024], BF16))
        W13 = [es.enter_context(nc.sbuf_tensor("f_W13_%d_p%d" % (i, cx.pg.nphase), [128, 2, 8, 128], BF16)) for i in range(2)]
        sil = [es.enter_context(nc.sbuf_tensor("f_sil%d_p%d" % (i, cx.pg.nphase), [128, 512], F32)) for i in range(2)]
        gbc = [es.enter_context(nc.sbuf_tensor("f_gbc%d_p%d" % (i, cx.pg.nphase), [128, 1024], F32)) for i in range(2)]
        xt = [es.enter_context(nc.sbuf_tensor("f_xt%d_p%d" % (i, cx.pg.nphase), [128, 1024], F32)) for i in range(2)]
        yt = [es.enter_context(nc.sbuf_tensor("f_yt%d_p%d" % (i, cx.pg.nphase), [128, 1024], F32)) for i in range(2)]
        for ty in range(2):
            pg.dma("sp", gbc[ty][:], cx.mrows_d[ty, (3 * s + 2) * 1024:(3 * s + 3) * 1024].partition_broadcast(128),
                   w=[("f_gbc", ty)])
            pg.op("act", (lambda e, ty=ty: e.mul(gbc[ty][:], gbc[ty][:], 0.5)), r=[("f_gbc", ty)], w=[("f_gbc", ty)])
        for hj in range(2):
            pg.dma("pool", W2[:, hj * 11:(hj + 1) * 11, :],
                   w2[hj * 1408:(hj + 1) * 1408, :].rearrange("(j p) n -> p j n", p=128), w=[("f_W2", hj)])
        it = 0
        for hf in range(2):
            tok0 = hf * 1152
            groups = [(0, 512), (512, 512), (1024, 128)]
            for j in range(NJ):
                b = (hf * NJ + j) % 2
                pg.dma("pool", W13[b][:, 0], w1[:, j * 128:(j + 1) * 128].rearrange("(c p) n -> p c n", p=128),
                       w=[("f_W13", b, 0)])
                pg.dma("pool", W13[b][:, 1], w3[:, j * 128:(j + 1) * 128].rearrange("(c p) n -> p c n", p=128),
                       w=[("f_W13", b, 1)])
                for (g0, gn) in groups:
                    i2 = it % 2
                    it += 1
                    pa, pb_ = 2 + 2 * i2, 3 + 2 * i2
                    htr = [("HT", c, tt) for c in range(8) for tt in range((tok0 + g0) // 128, (tok0 + g0 + gn) // 128)]
                    for c in range(8):
                        pg.op("pe", (lambda e, c=c, b=b, pa=pa, g0=g0, gn=gn, tok0=tok0: e.matmul(
                            cx.psb[pa][:, 0:gn], W13[b][:, 0, c, :], HT[:, c, tok0 + g0:tok0 + g0 + gn],
                            start=(c == 0), stop=(c == 7))), r=[("f_W13", b, 0)] + (htr if c == 0 else []), w=[("ps", pa)])
                    for c in range(8):
                        pg.op("pe", (lambda e, c=c, b=b, pb_=pb_, g0=g0, gn=gn, tok0=tok0: e.matmul(
                            cx.psb[pb_][:, 0:gn], W13[b][:, 1, c, :], HT[:, c, tok0 + g0:tok0 + g0 + gn],
                            start=(c == 0), stop=(c == 7))), r=[("f_W13", b, 1)], w=[("ps", pb_)])
                    pg.op("act", (lambda e, i2=i2, pa=pa, gn=gn: e.activation(out=sil[i2][:, 0:gn], in_=cx.psb[pa][:, 0:gn],
                                                                              func=AF.Silu)),
                          r=[("ps", pa)], w=[("f_sil", i2)])
                    pg.op("dve", (lambda e, i2=i2, pb_=pb_, g0=g0, gn=gn, j=j: e.tensor_tensor(
                        out=GT[:, j, g0:g0 + gn], in0=sil[i2][:, 0:gn], in1=cx.psb[pb_][:, 0:gn], op=ALU.mult)),
                        r=[("f_sil", i2), ("ps", pb_)], w=[("f_GT", j, g0)])
            for tl in range(9):
                t = hf * 9 + tl
                ty = typ_of(t)
                b = t % 2
                banks = (6, 7) if b == 0 else (0, 1)
                gtr = [("f_GT", j, (tl * 128) // 512 * 512) for j in range(NJ)]
                for hc in range(2):
                    for j in range(NJ):
                        pg.op("pe", (lambda e, hc=hc, j=j, tl=tl, bk=banks[hc]: e.matmul(
                            cx.psb[bk][:, :], GT[:, j, tl * 128:(tl + 1) * 128], W2[:, j, hc * 512:(hc + 1) * 512],
                            start=(j == 0), stop=(j == NJ - 1))),
                            r=(gtr + [("f_W2", 0), ("f_W2", 1)]) if j == 0 else [], w=[("ps", banks[hc])])
                pg.dma("sp", xt[b][:], xs[t * 128:(t + 1) * 128, :], r=[("xs", t)], w=[("f_xt", b)])
                for hc in range(2):
                    pg.op("dve", (lambda e, hc=hc, b=b, ty=ty, bk=banks[hc]: e.tensor_tensor(
                        out=yt[b][:, hc * 512:(hc + 1) * 512], in0=cx.psb[bk][:, :],
                        in1=gbc[ty][:, hc * 512:(hc + 1) * 512], op=ALU.mult)),
                        r=[("ps", banks[hc]), ("f_gbc", ty)], w=[("f_yt", b, hc)])
                pg.op("pool", (lambda e, b=b: e.tensor_tensor(out=xt[b][:], in0=xt[b][:], in1=yt[b][:], op=ALU.add)),
                      r=[("f_yt", b, 0), ("f_yt", b, 1), ("f_xt", b)], w=[("f_xt", b)])
                pg.dma("sp", xs[t * 128:(t + 1) * 128, :], xt[b][:], r=[("f_xt", b)], w=[("xs", t)])
        pg.emit(cx.scratch)


def new_prog(inputs_only=False):
    nc = bass.Bass("TRN2", target_bir_lowering=False)
    return nc


def din(nc, name, shape, dt=F32):
    return nc.dram_tensor(name, list(shape), dt, kind="ExternalInput").ap()


def dout(nc, name, shape, dt=F32):
    return nc.dram_tensor(name, list(shape), dt, kind="ExternalOutput").ap()


NPA = 2560
ROPE_SPANS = [(256, 384, 0), (1408, 768, 384)]
ROPE_REGIONS = [(256, 192, 16, 0), (448, 192, 16, 192), (1408, 384, 8, 384), (1792, 384, 8, 768)]


def build_A():
    nc = new_prog()
    with ExitStack() as es:
        cx = setup_common(nc, es)
        pg = cx.pg
        xin = din(nc, "xin", [TOK, D])
        xs = dout(nc, "xs", [TOK, D])
        w1 = din(nc, "w1", [D, DFF])
        w3 = din(nc, "w3", [D, DFF])
        w2 = din(nc, "w2", [DFF, D])
        win = din(nc, "win", [D, NPA])
        rc = din(nc, "ropec", [TOK, 1152])
        rs = din(nc, "ropes", [TOK, 1152])
        pall = dout(nc, "pall", [TOK, NPA], BF16)
        load_mod(cx, es)
        HT = es.enter_context(nc.sbuf_tensor("HT", [128, 8, TOK], BF16))
        for t in range(NT):
            pg.dma("sp", xs[t * 128:(t + 1) * 128, :], xin[t * 128:(t + 1) * 128, :], w=[("xs", t)])
        ffn(cx, xs, w1, w3, w2, 0, HT, "f1")
        with ExitStack() as e2:
            norm_to_HT(cx, e2, xs, 1, HT)
            Win = e2.enter_context(nc.sbuf_tensor("a_Win", [128, 8, NPA], BF16))
            pt = [e2.enter_context(nc.sbuf_tensor("a_pt%d" % i, [128, NPA], F32)) for i in range(2)]
            psw = [e2.enter_context(nc.sbuf_tensor("a_psw%d" % i, [128, 1152], F32)) for i in range(2)]
            ct = [e2.enter_context(nc.sbuf_tensor("a_ct%d" % i, [128, 1152], F32)) for i in range(2)]
            stb = [e2.enter_context(nc.sbuf_tensor("a_st%d" % i, [128, 1152], F32)) for i in range(2)]
            pb = [e2.enter_context(nc.sbuf_tensor("a_pb%d" % i, [128, NPA], BF16)) for i in range(2)]
            for nb in range(5):
                pg.dma("pool", Win[:, :, nb * 512:(nb + 1) * 512],
                       win[:, nb * 512:(nb + 1) * 512].rearrange("(c p) n -> p c n", p=128), w=[("a_Win", nb)])
            for t in range(NT):
                b = t % 2
                pg.dma("sp", ct[b][:], rc[t * 128:(t + 1) * 128, :], w=[("a_ct", b)])
                pg.dma("sp", stb[b][:], rs[t * 128:(t + 1) * 128, :], w=[("a_st", b)])
                for nb in range(5):
                    bk = 2 + nb
                    for c in range(8):
                        pg.op("pe", (lambda e, c=c, t=t, nb=nb, bk=bk: e.matmul(
                            cx.psb[bk][:, :], HT[:, c, t * 128:(t + 1) * 128], Win[:, c, nb * 512:(nb + 1) * 512],
                            start=(c == 0), stop=(c == 7))),
                            r=[("a_Win", nb)] + ([("HT", cc, t) for cc in range(8)] if c == 0 else []), w=[("ps", bk)])
                    pg.op("act", (lambda e, b=b, nb=nb, bk=bk: e.activation(out=pt[b][:, nb * 512:(nb + 1) * 512],
                                                                           in_=cx.psb[bk][:, :], func=AF.Copy)),
                          r=[("ps", bk)], w=[("a_pt", b, nb)])
                ptk = [("a_pt", b, nb) for nb in range(5)]
                for (c0, w_, hf, toff) in ROPE_REGIONS:
                    g = w_ // (2 * hf)
                    src = pt[b][:, c0:c0 + w_].rearrange("p (g h k) -> p g h k", h=2, k=hf)
                    dst = psw[b][:, toff:toff + w_].rearrange("p (g h k) -> p g h k", h=2, k=hf)
                    for h in range(2):
                        pg.op("pool", (lambda e, src=src, dst=dst, h=h: e.tensor_copy(dst[:, :, h, :], src[:, :, 1 - h, :])),
                              r=ptk, w=[("a_psw", b, toff, h)])
                pswk = [("a_psw", b, toff, h) for (_, _, _, toff) in ROPE_REGIONS for h in range(2)]
                for (c0, w_, toff) in ROPE_SPANS:
                    pg.op("dve", (lambda e, b=b, c0=c0, w_=w_, toff=toff: e.tensor_tensor(
                        out=pt[b][:, c0:c0 + w_], in0=pt[b][:, c0:c0 + w_], in1=ct[b][:, toff:toff + w_], op=ALU.mult)),
                        r=ptk + pswk + [("a_ct", b)], w=ptk)
                    pg.op("pool", (lambda e, b=b, w_=w_, toff=toff: e.tensor_tensor(
                        out=psw[b][:, toff:toff + w_], in0=psw[b][:, toff:toff + w_], in1=stb[b][:, toff:toff + w_],
                        op=ALU.mult)), r=pswk + [("a_st", b)], w=pswk)
                    pg.op("dve", (lambda e, b=b, c0=c0, w_=w_, toff=toff: e.tensor_tensor(
                        out=pt[b][:, c0:c0 + w_], in0=pt[b][:, c0:c0 + w_], in1=psw[b][:, toff:toff + w_], op=ALU.add)),
                        r=ptk + pswk, w=ptk)
                pg.op("act", (lambda e, b=b: e.activation(out=pb[b][:], in_=pt[b][:], func=AF.Copy)),
                      r=ptk, w=[("a_pb", b)])
                pg.dma("sp", pall[t * 128:(t + 1) * 128, :], pb[b][:], r=[("a_pb", b)], w=[("pall", t)])
            pg.emit(cx.scratch)
    return nc


def build_ATT():
    nc = new_prog()
    with ExitStack() as es:
        cx = setup_common(nc, es)
        pg = cx.pg
        qT_d = din(nc, "qT", [6, 64, TOK], BF16)
        kT_d = din(nc, "kT", [6, 64, NKT * 128], BF16)
        v_d = din(nc, "vv", [6, 128, NKT, 64], BF16)
        lamv_d = din(nc, "lamv", [128, 128])
        cst_d = din(nc, "cst", [128, 4])
        sub_d = din(nc, "subln", [128, 64])
        do_d = dout(nc, "dout", [TOK, 384])
        QT = es.enter_context(nc.sbuf_tensor("QT", [64, 6, TOK], BF16))
        KT = es.enter_context(nc.sbuf_tensor("KT", [64, NKT * 128], BF16))
        V = es.enter_context(nc.sbuf_tensor("V", [128, NKT, 65], BF16))
        PT = [es.enter_context(nc.sbuf_tensor("PT%d" % i, [128, 512], BF16)) for i in range(3)]
        osb = [es.enter_context(nc.sbuf_tensor("osb%d" % i, [65, 512], F32)) for i in range(2)]
        on = [es.enter_context(nc.sbuf_tensor("on%d" % i, [128, 4, 64], F32)) for i in range(2)]
        dd = es.enter_context(nc.sbuf_tensor("dd", [128, 64], F32))
        junk = es.enter_context(nc.sbuf_tensor("junk", [128, 64], F32))
        stt = es.enter_context(nc.sbuf_tensor("stt", [128, 8], F32))
        lamv = es.enter_context(nc.sbuf_tensor("lamv_s", [128, 128], F32))
        lt = es.enter_context(nc.sbuf_tensor("lt", [128, 64], F32))
        ls = es.enter_context(nc.sbuf_tensor("ls", [128, 8], F32))
        cst = es.enter_context(nc.sbuf_tensor("cst_s", [128, 4], F32))
        subw = es.enter_context(nc.sbuf_tensor("subw", [128, 64], F32))
        dsb = es.enter_context(nc.sbuf_tensor("dsb", [128, NT, 384], F32))
        pg.dma("sp", lamv[:], lamv_d, w=[("lamv",)])
        pg.dma("sp", cst[:], cst_d, w=[("cst",)])
        pg.dma("sp", subw[:], sub_d, w=[("subw",)])
        for h in range(6):
            pg.dma("sp", QT[:, h, :], qT_d[h], w=[("QT", h)])
        pg.op("dve", lambda e: e.memset(V[:, :, 64:65], 1.0), w=[("Vone",)])
        pg.op("dve", lambda e: e.memset(ls[:], 0.0), w=[("ls",)])
        pg.op("dve", lambda e: e.tensor_tensor(out=lt[:, 0:32], in0=lamv[:, 0:32], in1=lamv[:, 32:64], op=ALU.mult),
              r=[("lamv",)], w=[("lt", 0)])
        pg.op("dve", lambda e: e.tensor_tensor(out=lt[:, 32:64], in0=lamv[:, 64:96], in1=lamv[:, 96:128], op=ALU.mult),
              r=[("lamv",)], w=[("lt", 1)])
        pg.op("dve", lambda e: e.tensor_reduce(out=ls[:, 0:1], in_=lt[:, 0:32], axis=AX.X, op=ALU.add),
              r=[("lt", 0), ("ls",)], w=[("ls",)])
        pg.op("dve", lambda e: e.tensor_reduce(out=ls[:, 1:2], in_=lt[:, 32:64], axis=AX.X, op=ALU.add),
              r=[("lt", 1), ("ls",)], w=[("ls",)])
        pg.op("act", lambda e: e.activation(out=ls[:, 2:4], in_=ls[:, 0:2], func=AF.Exp), r=[("ls",)], w=[("ls",)])
        pg.op("dve", lambda e: e.tensor_tensor(out=ls[:, 4:5], in0=ls[:, 3:4], in1=ls[:, 2:3], op=ALU.subtract),
              r=[("ls",)], w=[("ls",)])
        pg.op("dve", lambda e: e.tensor_tensor(out=ls[:, 5:6], in0=ls[:, 4:5], in1=cst[:, 0:1], op=ALU.subtract),
              r=[("ls",), ("cst",)], w=[("ls",)])
        pg.op("dve", lambda e: e.tensor_scalar(subw[:], subw[:], cst[:, 1:2], None, op0=ALU.mult),
              r=[("subw",), ("cst",)], w=[("subw",)])
        qgroups = [(0, 512), (512, 512), (1024, 512), (1536, 512), (2048, 256)]
        scale = 32.0 ** -0.5
        si = 0
        for hd in range(6):
            pg.dma("sp", KT[:, :], kT_d[hd], w=[("KT",)])
            for nb in range(5):
                pg.dma("sp", V[:, nb * 26:(nb + 1) * 26, 0:64], v_d[hd, :, nb * 26:(nb + 1) * 26, :],
                       r=[("Vone",)], w=[("V", nb)])
            vk = [("V", nb) for nb in range(5)]
            for qi, (q0, qn) in enumerate(qgroups):
                kts = list(range(NKT)) if qi < 4 else [NKT - 2, NKT - 1]
                for mp in range(2):
                    ob = 3 + mp
                    for ki, kt in enumerate(kts):
                        sb_ = si % 3
                        si += 1
                        pg.op("pe", (lambda e, sb_=sb_, mp=mp, kt=kt, hd=hd, q0=q0, qn=qn: e.matmul(
                            cx.psb[sb_][:, 0:qn], KT[32 * mp:32 * mp + 32, kt * 128:(kt + 1) * 128],
                            QT[32 * mp:32 * mp + 32, hd, q0:q0 + qn], start=True, stop=True)),
                            r=[("KT",), ("QT", hd)], w=[("ps", sb_)])
                        pg.op("act", (lambda e, sb_=sb_, qn=qn: e.activation(out=PT[sb_][:, 0:qn], in_=cx.psb[sb_][:, 0:qn],
                                                                             func=AF.Exp, scale=scale)),
                              r=[("ps", sb_)], w=[("PT", sb_)])
                        pg.op("pe", (lambda e, sb_=sb_, kt=kt, ob=ob, qn=qn, ki=ki, nk=len(kts): e.matmul(
                            cx.psb[ob][0:65, 0:qn], V[:, kt, :], PT[sb_][:, 0:qn], start=(ki == 0), stop=(ki == nk - 1))),
                            r=[("PT", sb_)] + (vk if ki == 0 else []), w=[("ps", ob)])
                    pg.op("act", (lambda e, mp=mp, ob=ob, qn=qn: e.activation(out=osb[mp][:, 0:qn], in_=cx.psb[ob][0:65, 0:qn],
                                                                             func=AF.Copy)),
                          r=[("ps", ob)], w=[("osb", mp)])
                    for tt in range(qn // 128):
                        tb = 5 + (tt % 2)
                        pg.op("pe", (lambda e, mp=mp, tt=tt, tb=tb: e.matmul(
                            cx.psb[tb][:, 0:65], osb[mp][0:65, tt * 128:(tt + 1) * 128], cx.ident_f[0:65, 0:65],
                            start=True, stop=True)), r=[("osb", mp), ("ident_f",)], w=[("ps", tb)])
                        pg.op("dve", (lambda e, tb=tb: e.reciprocal(stt[:, 0:1], cx.psb[tb][:, 64:65])),
                              r=[("ps", tb)], w=[("stt", 0)])
                        pg.op("dve", (lambda e, tb=tb, mp=mp, tt=tt: e.tensor_scalar(
                            on[mp][:, tt, :], cx.psb[tb][:, 0:64], stt[:, 0:1], None, op0=ALU.mult)),
                            r=[("ps", tb), ("stt", 0)], w=[("on", mp, tt)])
                for tt in range(qn // 128):
                    t = q0 // 128 + tt
                    pg.op("dve", (lambda e, tt=tt: e.scalar_tensor_tensor(
                        out=dd[:], in0=on[1][:, tt, :], scalar=ls[:, 5:6], in1=on[0][:, tt, :], op0=ALU.mult, op1=ALU.add)),
                        r=[("on", 0, tt), ("on", 1, tt), ("ls",)], w=[("dd",)])
                    pg.op("dve", (lambda e: e.memset(stt[:, 1:2], 0.0)), w=[("stt", 1)])
                    pg.op("act", (lambda e: e.activation(out=junk[:], in_=dd[:], func=AF.Square, accum_out=stt[:, 1:2])),
                          r=[("dd",)], w=[("junk",), ("stt", 1)])
                    pg.op("dve", (lambda e: e.tensor_scalar(stt[:, 2:3], stt[:, 1:2], 1.0 / 64, EPS, op0=ALU.mult, op1=ALU.add)),
                          r=[("stt", 1)], w=[("stt", 2)])
                    pg.op("act", (lambda e: e.activation(out=stt[:, 3:4], in_=stt[:, 2:3], func=AF.Sqrt)),
                          r=[("stt", 2)], w=[("stt", 3)])
                    pg.op("dve", (lambda e: e.reciprocal(stt[:, 4:5], stt[:, 3:4])), r=[("stt", 3)], w=[("stt", 4)])
                    pg.op("dve", (lambda e, t=t, hd=hd: e.scalar_tensor_tensor(
                        out=dsb[:, t, hd * 64:(hd + 1) * 64], in0=dd[:], scalar=stt[:, 4:5], in1=subw[:],
                        op0=ALU.mult, op1=ALU.mult)), r=[("dd",), ("stt", 4), ("subw",)], w=[("dsb", t)])
        for t in range(NT):
            pg.dma("sp", do_d[t * 128:(t + 1) * 128, :], dsb[:, t, :], r=[("dsb", t)], w=[("do", t)])
        pg.emit(cx.scratch)
    return nc


def build_RET():
    nc = new_prog()
    NTK = NKT * 128
    with ExitStack() as es:
        cx = setup_common(nc, es)
        pg = cx.pg
        um_d = din(nc, "umat", [128, 128])
        mm_d = din(nc, "mmat", [128, 128])
        ir_d = din(nc, "irow", [128, 128])
        pc_d = din(nc, "pcol", [128, 1])
        umat = es.enter_context(nc.sbuf_tensor("umat_s", [128, 128], F32))
        mmat = es.enter_context(nc.sbuf_tensor("mmat_s", [128, 128], F32))
        irow = es.enter_context(nc.sbuf_tensor("irow_s", [128, 128], F32))
        pcol = es.enter_context(nc.sbuf_tensor("pcol_s", [128, 1], F32))
        pg.dma("sp", umat[:], um_d, w=[("umat",)])
        pg.dma("sp", mmat[:], mm_d, w=[("mmat",)])
        pg.dma("sp", irow[:], ir_d, w=[("irow",)])
        pg.dma("sp", pcol[:], pc_d, w=[("pcol",)])
        qT = es.enter_context(nc.sbuf_tensor("r_qT", [32, NTK], BF16))
        kT = es.enter_context(nc.sbuf_tensor("r_kT", [32, NTK], BF16))
        kk = es.enter_context(nc.sbuf_tensor("r_kk", [128, NKT, 32], BF16))
        vv = es.enter_context(nc.sbuf_tensor("r_vv", [128, NKT, 64], BF16))
        osb = es.enter_context(nc.sbuf_tensor("r_osb", [128, NKT, 64], F32))
        lg = es.enter_context(nc.sbuf_tensor("r_lg", [128, 8], F32))
        DT = es.enter_context(nc.sbuf_tensor("r_DT", [128, 128], F32))
        qdec = es.enter_context(nc.sbuf_tensor("r_qdec", [32, 128], F32))
        ATm = [es.enter_context(nc.sbuf_tensor("r_ATm%d" % i, [128, 128], BF16)) for i in range(2)]
        qd = [es.enter_context(nc.sbuf_tensor("r_qd%d" % i, [32, 128], BF16)) for i in range(2)]
        kd = [es.enter_context(nc.sbuf_tensor("r_kd%d" % i, [128, 32], BF16)) for i in range(2)]
        S32 = es.enter_context(nc.sbuf_tensor("r_S32", [32, 64], F32))
        Sbf = [es.enter_context(nc.sbuf_tensor("r_Sbf%d" % i, [32, 64], BF16)) for i in range(2)]
        for u in range(2):
            qT_d = din(nc, "qT%d" % u, [32, NTK], BF16)
            kT_d = din(nc, "kT%d" % u, [32, NTK], BF16)
            kk_d = din(nc, "kk%d" % u, [128, NKT, 32], BF16)
            vv_d = din(nc, "vv%d" % u, [128, NKT, 64], BF16)
            lg_d = din(nc, "lg%d" % u, [128, 1])
            ro_d = dout(nc, "ro%d" % u, [128, NKT, 64])
            pg.dma("sp", qT[:], qT_d, w=[("qT",)])
            pg.dma("sp", kT[:], kT_d, w=[("kT",)])
            pg.dma("sp", kk[:], kk_d, w=[("kk",)])
            pg.dma("sp", vv[:], vv_d, w=[("vv",)])
            pg.dma("sp", lg[:, 0:1], lg_d, w=[("lg",)])
            pg.op("act", lambda e: e.activation(out=lg[:, 1:2], in_=lg[:, 0:1], func=AF.Exp, scale=-1.0), r=[("lg",)], w=[("lg",)])
            pg.op("dve", lambda e: e.tensor_scalar(lg[:, 2:3], lg[:, 1:2], 1.0, None, op0=ALU.add), r=[("lg",)], w=[("lg",)])
            pg.op("act", lambda e: e.activation(out=lg[:, 3:4], in_=lg[:, 2:3], func=AF.Ln), r=[("lg",)], w=[("lg",)])
            pg.op("dve", lambda e: e.tensor_scalar(lg[:, 4:5], lg[:, 3:4], -1.0, None, op0=ALU.mult), r=[("lg",)], w=[("lg",)])
            pg.op("act", lambda e: e.activation(out=DT[:], in_=umat[:], func=AF.Exp, scale=lg[:, 4:5]),
                  r=[("lg",), ("umat",)], w=[("DT",)])
            pg.op("dve", lambda e: e.tensor_tensor(out=DT[:], in0=DT[:], in1=mmat[:], op=ALU.mult), r=[("DT",), ("mmat",)], w=[("DT",)])
            pg.op("act", lambda e: e.activation(out=qdec[:], in_=irow[0:32, :], func=AF.Exp, scale=lg[0:32, 4:5]),
                  r=[("lg",), ("irow",)], w=[("qdec",)])
            pg.op("act", lambda e: e.activation(out=lg[:, 5:6], in_=pcol[:], func=AF.Exp, scale=lg[:, 4:5]),
                  r=[("lg",), ("pcol",)], w=[("lg",)])
            pg.op("act", lambda e: e.activation(out=lg[:, 6:7], in_=lg[:, 4:5], func=AF.Exp, scale=128.0),
                  r=[("lg",)], w=[("lg",)])
            pg.op("dve", lambda e: e.memset(S32[:], 0.0), w=[("S32",)])
            pg.op("dve", lambda e: e.memset(Sbf[0][:], 0.0), w=[("Sbf", 0)])
            for c in range(NKT):
                b = c % 2
                pg.op("pe", (lambda e, c=c, b=b: e.matmul(cx.psb[b][:, 0:128], kT[:, c * 128:(c + 1) * 128],
                                                          qT[:, c * 128:(c + 1) * 128], start=True, stop=True)),
                      r=[("kT",), ("qT",)], w=[("ps", b)])
                pg.op("dve", (lambda e, b=b: e.tensor_tensor(out=ATm[b][:], in0=cx.psb[b][:, 0:128], in1=DT[:], op=ALU.mult)),
                      r=[("ps", b), ("DT",)], w=[("ATm", b)])
                pg.op("pool", (lambda e, c=c, b=b: e.tensor_tensor(out=qd[b][:], in0=qT[:, c * 128:(c + 1) * 128], in1=qdec[:],
                                                                   op=ALU.mult)), r=[("qT",), ("qdec",)], w=[("qd", b)])
                pg.op("pe", (lambda e, c=c, b=b: e.matmul(cx.psb[2 + b][:, 0:64], ATm[b][:], vv[:, c, :], start=True, stop=False)),
                      r=[("ATm", b), ("vv",)], w=[("ps", 2 + b)])
                pg.op("pe", (lambda e, c=c, b=b: e.matmul(cx.psb[2 + b][:, 0:64], qd[b][:], Sbf[b][:], start=False, stop=True)),
                      r=[("qd", b), ("Sbf", b)], w=[("ps", 2 + b)])
                pg.op("act", (lambda e, c=c, b=b: e.activation(out=osb[:, c, :], in_=cx.psb[2 + b][:, 0:64], func=AF.Copy)),
                      r=[("ps", 2 + b)], w=[("osb", c)])
                pg.op("act", (lambda e, c=c, b=b: e.activation(out=kd[b][:], in_=kk[:, c, :], func=AF.Copy, scale=lg[:, 5:6])),
                      r=[("kk",), ("lg",)], w=[("kd", b)])
                pg.op("pe", (lambda e, c=c, b=b: e.matmul(cx.psb[4 + b][0:32, 0:64], kd[b][:], vv[:, c, :], start=True, stop=True)),
                      r=[("kd", b), ("vv",)], w=[("ps", 4 + b)])
                pg.op("dve", (lambda e, b=b: e.scalar_tensor_tensor(out=S32[:], in0=S32[:], scalar=lg[0:32, 6:7],
                                                                    in1=cx.psb[4 + b][0:32, 0:64], op0=ALU.mult, op1=ALU.add)),
                      r=[("S32",), ("ps", 4 + b), ("lg",)], w=[("S32",)])
                pg.op("act", (lambda e, b=b: e.activation(out=Sbf[1 - b][:], in_=S32[:], func=AF.Copy)),
                      r=[("S32",)], w=[("Sbf", 1 - b)])
            pg.dma("sp", ro_d, osb[:], r=[("osb", c) for c in range(NKT)], w=[("ro", u)])
        pg.emit(cx.scratch)
    return nc


def build_FFT():
    nc = new_prog()
    with ExitStack() as es:
        cx = setup_common(nc, es)
        pg = cx.pg
        uT_d = din(nc, "uT", [64, SEQ + CTX], BF16)
        names = ["ccs", "cm1", "sm1", "nsm1", "cm3", "sm3"]
        shp = {"ccs": [64, 64]}
        tb = {}
        for n_ in names:
            s_ = shp.get(n_, [128, 128])
            d_ = din(nc, n_, s_, BF16)
            tb[n_] = es.enter_context(nc.sbuf_tensor(n_ + "_s", s_, BF16))
            pg.dma("sp", tb[n_][:], d_, w=[(n_,)])
        twc_d = din(nc, "twc", [128, 4096])
        tws_d = din(nc, "tws", [128, 4096])
        c256_d = din(nc, "c256", [128, 2, 256], BF16)
        s256_d = din(nc, "s256", [128, 2, 256], BF16)
        y_d = dout(nc, "y", [128, 4096])
        yc_d = dout(nc, "yc", [128, 2, 32])
        uT = es.enter_context(nc.sbuf_tensor("uT_s", [64, SEQ + CTX], BF16))
        X = es.enter_context(nc.sbuf_tensor("X", [128, 128, 64], BF16))
        Zr = es.enter_context(nc.sbuf_tensor("Zr", [128, 4096], F32))
        Zi = es.enter_context(nc.sbuf_tensor("Zi", [128, 4096], F32))
        ta = es.enter_context(nc.sbuf_tensor("ta", [128, 4096], F32))
        tb2 = es.enter_context(nc.sbuf_tensor("tb2", [128, 4096], F32))
        Zpr = es.enter_context(nc.sbuf_tensor("Zpr", [128, 4096], BF16))
        Zpi = es.enter_context(nc.sbuf_tensor("Zpi", [128, 4096], BF16))
        twc = es.enter_context(nc.sbuf_tensor("twc_s", [128, 4096], F32))
        tws = es.enter_context(nc.sbuf_tensor("tws_s", [128, 4096], F32))
        c256 = es.enter_context(nc.sbuf_tensor("c256_s", [128, 2, 256], BF16))
        s256 = es.enter_context(nc.sbuf_tensor("s256_s", [128, 2, 256], BF16))
        Xc = es.enter_context(nc.sbuf_tensor("Xc", [128, 2, 64], BF16))
        ycs = es.enter_context(nc.sbuf_tensor("ycs", [128, 2, 32], F32))
        pg.dma("sp", uT[:], uT_d, w=[("uT",)])
        pg.dma("sp", twc[:], twc_d, w=[("twc",)])
        pg.dma("sp", tws[:], tws_d, w=[("tws",)])
        pg.dma("sp", c256[:], c256_d, w=[("c256",)])
        pg.dma("sp", s256[:], s256_d, w=[("s256",)])
        uv = uT[:, 0:SEQ].rearrange("c (a b) -> c a b", b=128)
        for g8 in range(16):
            bk = g8 % 2
            for k in range(8):
                t2 = g8 * 8 + k
                pg.op("pe", (lambda e, t2=t2, k=k, bk=bk: e.matmul(cx.psb[bk][:, k * 64:(k + 1) * 64], uv[:, :, t2], tb["ccs"][:],
                                                                  start=True, stop=True)),
                      r=[("uT",), ("ccs",)], w=[("ps", bk)])
            pg.op("act", (lambda e, g8=g8, bk=bk: e.activation(
                out=X[:, g8 * 8:(g8 + 1) * 8, :].rearrange("p a b -> p (a b)"), in_=cx.psb[bk][:, :], func=AF.Copy)),
                r=[("ps", bk)], w=[("X", g8)])
        xk = [("X", g8) for g8 in range(16)]
        for m4 in range(8):
            br, bi = 2 + 2 * (m4 % 2), 3 + 2 * (m4 % 2)
            for k in range(4):
                m = m4 * 4 + k
                xr, xi = X[:, :, m], X[:, :, 32 + m]
                pg.op("pe", (lambda e, xr=xr, k=k, br=br: e.matmul(cx.psb[br][:, k * 128:(k + 1) * 128], xr, tb["cm1"][:],
                                                                  start=True, stop=False)),
                      r=xk + [("cm1",)], w=[("ps", br)])
                pg.op("pe", (lambda e, xi=xi, k=k, br=br: e.matmul(cx.psb[br][:, k * 128:(k + 1) * 128], xi, tb["sm1"][:],
                                                                  start=False, stop=True)),
                      r=[("sm1",)], w=[("ps", br)])
                pg.op("pe", (lambda e, xi=xi, k=k, bi=bi: e.matmul(cx.psb[bi][:, k * 128:(k + 1) * 128], xi, tb["cm1"][:],
                                                                  start=True, stop=False)),
                      r=[], w=[("ps", bi)])
                pg.op("pe", (lambda e, xr=xr, k=k, bi=bi: e.matmul(cx.psb[bi][:, k * 128:(k + 1) * 128], xr, tb["nsm1"][:],
                                                                  start=False, stop=True)),
                      r=[("nsm1",)], w=[("ps", bi)])
            pg.op("act", (lambda e, m4=m4, br=br: e.activation(out=Zr[:, m4 * 512:(m4 + 1) * 512], in_=cx.psb[br][:, :], func=AF.Copy)),
                  r=[("ps", br)], w=[("Zr", m4)])
            pg.op("act", (lambda e, m4=m4, bi=bi: e.activation(out=Zi[:, m4 * 512:(m4 + 1) * 512], in_=cx.psb[bi][:, :], func=AF.Copy)),
                  r=[("ps", bi)], w=[("Zi", m4)])
        zrk = [("Zr", m4) for m4 in range(8)]
        zik = [("Zi", m4) for m4 in range(8)]
        pg.op("dve", lambda e: e.tensor_tensor(out=ta[:], in0=Zr[:], in1=twc[:], op=ALU.mult), r=zrk + [("twc",)], w=[("ta",)])
        pg.op("pool", lambda e: e.tensor_tensor(out=tb2[:], in0=Zi[:], in1=tws[:], op=ALU.mult), r=zik + [("tws",)], w=[("tb2",)])
        pg.op("dve", lambda e: e.tensor_tensor(out=Zpr[:], in0=ta[:], in1=tb2[:], op=ALU.add), r=[("ta",), ("tb2",)], w=[("Zpr",)])
        pg.op("dve", lambda e: e.tensor_tensor(out=ta[:], in0=Zi[:], in1=twc[:], op=ALU.mult), r=zik + [("twc",), ("ta",)], w=[("ta",)])
        pg.op("pool", lambda e: e.tensor_tensor(out=tb2[:], in0=Zr[:], in1=tws[:], op=ALU.mult), r=zrk + [("tws",), ("tb2",)], w=[("tb2",)])
        pg.op("dve", lambda e: e.tensor_tensor(out=Zpi[:], in0=ta[:], in1=tb2[:], op=ALU.subtract), r=[("ta",), ("tb2",)], w=[("Zpi",)])
        for blk in range(8):
            bk = 6 + blk % 2
            pg.op("pe", (lambda e, blk=blk, bk=bk: e.matmul(cx.psb[bk][:, :], tb["cm3"][:], Zpr[:, blk * 512:(blk + 1) * 512],
                                                           start=True, stop=False)), r=[("Zpr",), ("cm3",)], w=[("ps", bk)])
            pg.op("pe", (lambda e, blk=blk, bk=bk: e.matmul(cx.psb[bk][:, :], tb["sm3"][:], Zpi[:, blk * 512:(blk + 1) * 512],
                                                           start=False, stop=True)), r=[("Zpi",), ("sm3",)], w=[("ps", bk)])
            pg.op("act", (lambda e, blk=blk, bk=bk: e.activation(out=Zr[:, blk * 512:(blk + 1) * 512], in_=cx.psb[bk][:, :], func=AF.Copy)),
                  r=[("ps", bk), ("ta",), ("tb2",)], w=[("Zr", blk)])
        pg.dma("sp", y_d, Zr[:], r=zrk, w=[("y",)])
        for tc in range(2):
            pg.op("pe", (lambda e, tc=tc: e.matmul(cx.psb[tc][:, 0:64], uT[:, SEQ + tc * 128:SEQ + (tc + 1) * 128], tb["ccs"][:],
                                                  start=True, stop=True)), r=[("uT",), ("ccs",)], w=[("ps", tc)])
            pg.op("act", (lambda e, tc=tc: e.activation(out=Xc[:, tc, :], in_=cx.psb[tc][:, 0:64], func=AF.Copy)),
                  r=[("ps", tc)], w=[("Xc", tc)])
        for nt in range(2):
            bk = 2 + nt
            for tc in range(2):
                pg.op("pe", (lambda e, nt=nt, tc=tc, bk=bk: e.matmul(cx.psb[bk][:, 0:32], c256[:, tc, nt * 128:(nt + 1) * 128],
                                                                    Xc[:, tc, 0:32], start=(tc == 0), stop=False)),
                      r=[("Xc", 0), ("Xc", 1), ("c256",)], w=[("ps", bk)])
                pg.op("pe", (lambda e, nt=nt, tc=tc, bk=bk: e.matmul(cx.psb[bk][:, 0:32], s256[:, tc, nt * 128:(nt + 1) * 128],
                                                                    Xc[:, tc, 32:64], start=False, stop=(tc == 1))),
                      r=[("s256",)], w=[("ps", bk)])
            pg.op("act", (lambda e, nt=nt, bk=bk: e.activation(out=ycs[:, nt, :], in_=cx.psb[bk][:, 0:32], func=AF.Copy)),
                  r=[("ps", bk)], w=[("ycs", nt)])
        pg.dma("sp", yc_d, ycs[:], r=[("ycs", 0), ("ycs", 1)], w=[("yc",)])
        pg.emit(cx.scratch)
    return nc


def build_C():
    nc = new_prog()
    with ExitStack() as es:
        cx = setup_common(nc, es)
        pg = cx.pg
        xin = din(nc, "xin", [TOK, D])
        xs = dout(nc, "xs", [TOK, D])
        xf = dout(nc, "xfin", [OWN, D])
        w1 = din(nc, "w1", [D, DFF])
        w3 = din(nc, "w3", [D, DFF])
        w2 = din(nc, "w2", [DFF, D])
        wgt_d = din(nc, "wgt", [D, 3072])
        wbf_d = din(nc, "wbf", [256, D])
        wbr_d = din(nc, "wbr", [384, D])
        wbd_d = din(nc, "wbd", [384, D])
        wo_d = din(nc, "wo", [D, D])
        foT_d = din(nc, "foT", [2, 128, TOK])
        doT_d = din(nc, "doT", [3, 128, TOK])
        rf_d = din(nc, "rf", [TOK, 384])
        rb_d = din(nc, "rb", [TOK, 384])
        rg_d = din(nc, "rg", [TOK, 384], BF16)
        fg_d = din(nc, "fgb", [128, D])
        load_mod(cx, es)
        HT = es.enter_context(nc.sbuf_tensor("HT", [128, 8, TOK], BF16))
        for t in range(NT):
            pg.dma("sp", xs[t * 128:(t + 1) * 128, :], xin[t * 128:(t + 1) * 128, :], w=[("xs", t)])
        with ExitStack() as e2:
            norm_to_HT(cx, e2, xs, 1, HT)
            sb = lambda n_, s_, d_: e2.enter_context(nc.sbuf_tensor(n_, s_, d_))
            wgt = sb("c_wgt", [128, 8, 3072], BF16)
            wb = [sb("c_wbf", [128, 2, D], BF16), sb("c_wbr", [128, 3, D], BF16), sb("c_wbd", [128, 3, D], BF16)]
            wo = sb("c_wo", [128, 8, D], BF16)
            foT = [sb("c_foT%d" % i, [128, 2, 128], BF16) for i in range(2)]
            doT = [sb("c_doT%d" % i, [128, 3, 128], BF16) for i in range(2)]
            g5 = [sb("c_g5_%d" % i, [128, D], F32) for i in range(2)]
            rf = [sb("c_rf%d" % i, [128, 384], F32) for i in range(2)]
            rb = [sb("c_rb%d" % i, [128, 384], F32) for i in range(2)]
            rg = [sb("c_rg%d" % i, [128, 384], BF16) for i in range(2)]
            sg = [sb("c_sg%d" % i, [128, 384], F32) for i in range(2)]
            rsq = sb("c_rsq", [128, 384], F32)
            rst = [sb("c_rst%d" % i, [128, 24], F32) for i in range(2)]
            rob = [sb("c_rob%d" % i, [128, 384], BF16) for i in range(2)]
            rT = [sb("c_rT%d" % i, [128, 3, 128], BF16) for i in range(2)]
            sig = [sb("c_sig%d" % i, [128, D], F32) for i in range(1)]
            prod = [sb("c_prod%d" % i, [128, D], F32) for i in range(1)]
            mixed = [sb("c_mixed%d" % i, [128, D], F32) for i in range(1)]
            mixb = [sb("c_mixb%d" % i, [128, D], BF16) for i in range(2)]
            mixT = [sb("c_mixT%d" % i, [128, 8, 128], BF16) for i in range(2)]
            xt = [sb("c_xt%d" % i, [128, D], F32) for i in range(2)]
            yt = [sb("c_yt%d" % i, [128, D], F32) for i in range(1)]
            for nb in range(6):
                pg.dma("pool", wgt[:, :, nb * 512:(nb + 1) * 512],
                       wgt_d[:, nb * 512:(nb + 1) * 512].rearrange("(c p) n -> p c n", p=128), w=[("wgt", nb)])
            for i, (wd, kc) in enumerate([(wbf_d, 2), (wbr_d, 3), (wbd_d, 3)]):
                pg.dma("pool", wb[i][:], wd.rearrange("(c p) n -> p c n", p=128), w=[("wb", i)])
            for hh in range(2):
                pg.dma("pool", wo[:, hh * 4:(hh + 1) * 4, :],
                       wo_d[hh * 512:(hh + 1) * 512, :].rearrange("(c p) n -> p c n", p=128), w=[("wo", hh)])
            for ty in range(2):
                pg.dma("sp", g5[ty][:], cx.mrows_d[ty, 5 * 1024:6 * 1024].partition_broadcast(128), w=[("g5", ty)])
            wgk = [("wgt", nb) for nb in range(6)]
            for t in range(NT):
                b = t % 2
                ty = typ_of(t)
                tsl = slice(t * 128, (t + 1) * 128)
                pg.dma("pool", foT[b][:], foT_d[:, :, tsl].rearrange("k p t -> p k t"), w=[("foT", b)])
                pg.dma("pool", doT[b][:], doT_d[:, :, tsl].rearrange("k p t -> p k t"), w=[("doT", b)])
                pg.dma("sp", rf[b][:], rf_d[tsl, :], w=[("rf", b)])
                pg.dma("sp", rb[b][:], rb_d[tsl, :], w=[("rb", b)])
                pg.dma("sp", rg[b][:], rg_d[tsl, :], w=[("rg", b)])
                pg.op("act", (lambda e, b=b: e.activation(out=sg[b][:], in_=rg[b][:], func=AF.Silu)), r=[("rg", b)], w=[("sg", b)])
                pg.op("dve", (lambda e, b=b: e.tensor_tensor(out=rf[b][:], in0=rf[b][:], in1=rb[b][:], op=ALU.add)),
                      r=[("rf", b), ("rb", b)], w=[("rf", b)])
                pg.op("pool", (lambda e, b=b: e.tensor_tensor(out=rsq[:], in0=rf[b][:], in1=rf[b][:], op=ALU.mult)),
                      r=[("rf", b)], w=[("rsq",)])
                pg.op("dve", (lambda e, b=b: e.tensor_reduce(out=rst[b][:, 0:6], in_=rsq[:].rearrange("p (h d) -> p h d", d=64),
                                                             axis=AX.X, op=ALU.add)), r=[("rsq",)], w=[("rst", b)])
                pg.op("dve", (lambda e, b=b: e.tensor_scalar(rst[b][:, 6:12], rst[b][:, 0:6], 1.0 / 64, EPS, op0=ALU.mult, op1=ALU.add)),
                      r=[("rst", b)], w=[("rst", b)])
                pg.op("act", (lambda e, b=b: e.activation(out=rst[b][:, 12:18], in_=rst[b][:, 6:12], func=AF.Sqrt)),
                      r=[("rst", b)], w=[("rst", b)])
                pg.op("dve", (lambda e, b=b: e.reciprocal(rst[b][:, 18:24], rst[b][:, 12:18])), r=[("rst", b)], w=[("rst", b)])
                for h in range(6):
                    pg.op("dve", (lambda e, b=b, h=h: e.scalar_tensor_tensor(
                        out=rob[b][:, h * 64:(h + 1) * 64], in0=rf[b][:, h * 64:(h + 1) * 64], scalar=rst[b][:, 18 + h:19 + h],
                        in1=sg[b][:, h * 64:(h + 1) * 64], op0=ALU.mult, op1=ALU.mult)),
                        r=[("rf", b), ("rst", b), ("sg", b)], w=[("rob", b, h)])
                pst = cx.psb[7][:, :].bitcast(BF16)
                for k in range(3):
                    pg.op("pe", (lambda e, k=k, b=b, pst=pst: e.transpose(pst[:, k * 128:(k + 1) * 128],
                                                                         rob[b][:, k * 128:(k + 1) * 128], cx.ident_bf[:])),
                          r=[("rob", b, 2 * k), ("rob", b, 2 * k + 1), ("ident_bf",)], w=[("ps", 7)])
                pg.op("act", (lambda e, b=b, pst=pst: e.activation(out=rT[b][:].rearrange("p a b -> p (a b)"), in_=pst[:, 0:384],
                                                                   func=AF.Copy)), r=[("ps", 7)], w=[("rT", b)])
                for br in range(3):
                    kc = [2, 3, 3][br]
                    gb = (0, 1)
                    pb_ = (2, 3) if br % 2 == 0 else (4, 5)
                    for hc in range(2):
                        for c in range(8):
                            pg.op("pe", (lambda e, c=c, hc=hc, br=br, tsl=tsl: e.matmul(
                                cx.psb[gb[hc]][:, :], HT[:, c, tsl], wgt[:, c, br * 1024 + hc * 512: br * 1024 + (hc + 1) * 512],
                                start=(c == 0), stop=(c == 7))),
                                r=(wgk + [("HT", cc, t) for cc in range(8)]) if c == 0 else [], w=[("ps", gb[hc])])
                    for hc in range(2):
                        for k in range(kc):
                            if br == 0:
                                lh = foT[b][:, k, :]
                                rk_ = [("foT", b)]
                            elif br == 1:
                                lh = rT[b][:, k, :]
                                rk_ = [("rT", b)]
                            else:
                                lh = doT[b][:, k, :]
                                rk_ = [("doT", b)]
                            pg.op("pe", (lambda e, lh=lh, k=k, hc=hc, br=br, kc=kc, pbk=pb_[hc]: e.matmul(
                                cx.psb[pbk][:, :], lh, wb[br][:, k, hc * 512:(hc + 1) * 512], start=(k == 0), stop=(k == kc - 1))),
                                r=rk_ + [("wb", br)], w=[("ps", pb_[hc])])
                    i2 = 0
                    for hc in range(2):
                        pg.op("act", (lambda e, i2=i2, hc=hc: e.activation(out=sig[i2][:, hc * 512:(hc + 1) * 512],
                                                                           in_=cx.psb[gb[hc]][:, :], func=AF.Sigmoid)),
                              r=[("ps", gb[hc])], w=[("sig", i2, hc)])
                        dst = mixed[0] if br == 0 else prod[i2]
                        dk = ("mixed", 0, hc) if br == 0 else ("prod", i2, hc)
                        pg.op("dve", (lambda e, i2=i2, hc=hc, dst=dst, pbk=pb_[hc]: e.tensor_tensor(
                            out=dst[:, hc * 512:(hc + 1) * 512], in0=sig[i2][:, hc * 512:(hc + 1) * 512], in1=cx.psb[pbk][:, :],
                            op=ALU.mult)), r=[("sig", i2, hc), ("ps", pb_[hc])], w=[dk])
                        if br > 0:
                            pg.op("pool", (lambda e, b=b, i2=i2, hc=hc: e.tensor_tensor(
                                out=mixed[0][:, hc * 512:(hc + 1) * 512], in0=mixed[0][:, hc * 512:(hc + 1) * 512],
                                in1=prod[i2][:, hc * 512:(hc + 1) * 512], op=ALU.add)),
                                r=[("mixed", 0, hc), ("prod", i2, hc)], w=[("mixed", 0, hc)])
                pg.op("act", (lambda e, b=b: e.activation(out=mixb[b][:], in_=mixed[0][:], func=AF.Copy)),
                      r=[("mixed", 0, 0), ("mixed", 0, 1)], w=[("mixb", b)])
                pst6 = cx.psb[6][:, :].bitcast(BF16)
                for c in range(8):
                    pg.op("pe", (lambda e, c=c, b=b, pst6=pst6: e.transpose(pst6[:, c * 128:(c + 1) * 128],
                                                                           mixb[b][:, c * 128:(c + 1) * 128], cx.ident_bf[:])),
                          r=[("mixb", b), ("ident_bf",)], w=[("ps", 6)])
                pg.op("act", (lambda e, b=b, pst6=pst6: e.activation(out=mixT[b][:].rearrange("p a b -> p (a b)"), in_=pst6[:, :],
                                                                     func=AF.Copy)), r=[("ps", 6)], w=[("mixT", b)])
                yb = (2, 3) if t % 2 == 1 else (4, 5)
                for hc in range(2):
                    for c in range(8):
                        pg.op("pe", (lambda e, c=c, hc=hc, b=b, ybk=yb[hc]: e.matmul(
                            cx.psb[ybk][:, :], mixT[b][:, c, :], wo[:, c, hc * 512:(hc + 1) * 512], start=(c == 0), stop=(c == 7))),
                            r=[("mixT", b), ("wo", 0), ("wo", 1)] if c == 0 else [], w=[("ps", yb[hc])])
                pg.dma("sp", xt[b][:], xs[tsl, :], r=[("xs", t)], w=[("c_xt", b)])
                for hc in range(2):
                    pg.op("dve", (lambda e, hc=hc, b=b, ty=ty, ybk=yb[hc]: e.tensor_tensor(
                        out=yt[0][:, hc * 512:(hc + 1) * 512], in0=cx.psb[ybk][:, :], in1=g5[ty][:, hc * 512:(hc + 1) * 512],
                        op=ALU.mult)), r=[("ps", yb[hc]), ("g5", ty)], w=[("c_yt", 0, hc)])
                pg.op("pool", (lambda e, b=b: e.tensor_tensor(out=xt[b][:], in0=xt[b][:], in1=yt[0][:], op=ALU.add)),
                      r=[("c_yt", 0, 0), ("c_yt", 0, 1), ("c_xt", b)], w=[("c_xt", b)])
                pg.dma("sp", xs[tsl, :], xt[b][:], r=[("c_xt", b)], w=[("xs", t)])
            pg.emit(cx.scratch)
        ffn(cx, xs, w1, w3, w2, 2, HT, "f2")
        with ExitStack() as e3:
            sb = lambda n_, s_, d_: e3.enter_context(nc.sbuf_tensor(n_, s_, d_))
            fg = sb("z_fg", [128, D], F32)
            xt = [sb("z_xt%d" % i, [128, D], F32) for i in range(2)]
            junk = sb("z_junk", [128, D], BF16)
            st = [sb("z_st%d" % i, [128, 4], F32) for i in range(2)]
            pg.dma("sp", fg[:], fg_d, w=[("fg",)])
            for t in range(NTO):
                b = t % 2
                tsl = slice(t * 128, (t + 1) * 128)
                pg.dma("sp", xt[b][:], xs[tsl, :], r=[("xs", t)], w=[("z_xt", b)])
                pg.op("dve", (lambda e, b=b: e.memset(st[b][:], 0.0)), w=[("z_st", b)])
                pg.op("act", (lambda e, b=b: e.activation(out=junk[:], in_=xt[b][:], func=AF.Square, accum_out=st[b][:, 0:1])),
                      r=[("z_xt", b)], w=[("z_junk",), ("z_st", b)])
                pg.op("dve", (lambda e, b=b: e.tensor_scalar(st[b][:, 1:2], st[b][:, 0:1], 1.0 / D, EPS, op0=ALU.mult, op1=ALU.add)),
                      r=[("z_st", b)], w=[("z_st", b)])
                pg.op("act", (lambda e, b=b: e.activation(out=st[b][:, 2:3], in_=st[b][:, 1:2], func=AF.Sqrt)),
                      r=[("z_st", b)], w=[("z_st", b)])
                pg.op("dve", (lambda e, b=b: e.reciprocal(st[b][:, 3:4], st[b][:, 2:3])), r=[("z_st", b)], w=[("z_st", b)])
                pg.op("dve", (lambda e, b=b: e.scalar_tensor_tensor(out=xt[b][:], in0=xt[b][:], scalar=st[b][:, 3:4], in1=fg[:],
                                                                    op0=ALU.mult, op1=ALU.mult)),
                      r=[("z_xt", b), ("z_st", b), ("fg",)], w=[("z_xt", b)])
                pg.dma("sp", xf[tsl, :], xt[b][:], r=[("z_xt", b)], w=[("xf", t)])
            pg.emit(cx.scratch)
    return nc


_PROGS = {}


def _run(name, maps):
    if name not in _PROGS:
        _PROGS[name] = {"ada": build_ada, "A": build_A, "ATT": build_ATT, "RET": build_RET,
                        "FFT": build_FFT, "C": build_C}[name]()
    ident = np.eye(128, dtype=np.float32)
    for m in maps:
        m["ident_in"] = ident
        for k in list(m.keys()):
            m[k] = np.ascontiguousarray(m[k])
    res = run_bass_kernel_spmd(_PROGS[name], maps, core_ids=list(range(NCORES)))
    return res.results


def _rope_tables():
    f32 = np.float32
    tabs = []
    inv_ret = (f32(10000.0) ** (-np.arange(0, 32, 2, dtype=f32) / f32(32))).astype(f32)
    inv_ax = (f32(10000.0) ** (-np.arange(0, 16, 2, dtype=f32) / f32(16))).astype(f32)
    pos = np.arange(SEQ, dtype=f32)
    a_ret = (pos[:, None] * inv_ret[None, :]).astype(f32)
    a_row = (np.floor(pos / GRID_W).astype(f32)[:, None] * inv_ax[None, :]).astype(f32)
    a_col = ((pos % GRID_W).astype(f32)[:, None] * inv_ax[None, :]).astype(f32)

    def cs(a):
        c = np.cos(a).astype(f32)
        s = np.sin(a).astype(f32)
        return np.concatenate([c, c], 1), np.concatenate([-s, s], 1)

    cr, sr = cs(a_ret)
    c1, s1 = cs(a_row)
    c2, s2 = cs(a_col)
    ca = np.concatenate([c1, c2], 1)
    sa = np.concatenate([s1, s2], 1)
    sc = f32(32.0 ** -0.5)
    cosL = np.concatenate([np.tile(cr, (1, 6)), np.tile(cr, (1, 6)) * sc, np.tile(ca, (1, 12)), np.tile(ca, (1, 12))], 1)
    sinL = np.concatenate([np.tile(sr, (1, 6)), np.tile(sr, (1, 6)) * sc, np.tile(sa, (1, 12)), np.tile(sa, (1, 12))], 1)
    cosC = np.ones((CTX, 1152), f32)
    cosC[:, 192:384] = sc
    sinC = np.zeros((CTX, 1152), f32)
    return cosL.astype(f32), sinL.astype(f32), cosC, sinC


def _fft_consts(hf):
    f64 = np.float64
    a = np.arange(128, dtype=f64)
    ang = 2 * np.pi * np.outer(a, a) / 128.0
    c = np.arange(64, dtype=f64)
    m = 32 * hf + np.arange(32, dtype=f64)
    angc = 2 * np.pi * np.outer(c, m) / 64.0
    ccs = np.concatenate([np.cos(angc), -np.sin(angc)], 1) / 8.0
    phi = 2 * np.pi * np.outer(a, a) / float(SEQ)
    twc = np.tile(np.cos(phi), (1, 32))
    tws = np.tile(np.sin(phi), (1, 32))
    t = np.arange(256, dtype=f64)
    a256 = 2 * np.pi * np.outer(t, t) / 256.0
    c256 = (np.cos(a256) / 16.0).reshape(2, 128, 256).transpose(1, 0, 2)
    s256 = (np.sin(a256) / 16.0).reshape(2, 128, 256).transpose(1, 0, 2)
    return dict(ccs=bf(ccs), cm1=bf(np.cos(ang) / 128.0), sm1=bf(np.sin(ang) / 128.0), nsm1=bf(-np.sin(ang) / 128.0),
                cm3=bf(np.cos(ang)), sm3=bf(np.sin(ang)), twc=twc.astype(np.float32), tws=tws.astype(np.float32),
                c256=bf(c256), s256=bf(s256))


def kernel_unfused(x, c, ctx, c_ctx, w_ada, b_ada, norm_g, ffn_w1, ffn_w3, ffn_w2, w_in, ret_decay_logit,
           diff_lambda, diff_subln, w_branch_f, w_branch_r, w_branch_d, w_out, final_g, depth=DEPTH):
    f32 = np.float32
    R = NCORES
    cc = np.stack([np.asarray(c)[0], np.asarray(c_ctx)], 0)
    ccl = cc.reshape(2, 8, 128).transpose(2, 1, 0)
    maps = []
    for r in range(R):
        l, h = r // 2, r % 2
        maps.append(dict(cc=ccl, wa=w_ada[l][:, h * 4608:(h + 1) * 4608],
                         ba=np.broadcast_to(b_ada[l][h * 4608:(h + 1) * 4608], (2, 4608))))
    res = _run("ada", maps)
    m_all = np.stack([np.concatenate([res[2 * l]["mo"], res[2 * l + 1]["mo"]], 1) for l in range(4)], 0)
    cosL, sinL, cosC, sinC = _rope_tables()
    jj = np.arange(128)
    umat = np.maximum(jj[None, :] - jj[:, None], 0).astype(f32)
    mmat = (jj[None, :] >= jj[:, None]).astype(f32)
    irow = np.broadcast_to((jj + 1).astype(f32)[None, :], (128, 128))
    pcol = (127 - jj).astype(f32)[:, None]
    fftc = [_fft_consts(0), _fft_consts(1)]
    x_lat = np.asarray(x)[0]
    x_ctx = np.asarray(ctx)[0]
    xfin = None
    for l in range(depth):
        lam_init = 0.8 - 0.6 * math.exp(-0.3 * l)
        m = m_all[l]
        mods = dict(mcols=m.reshape(2, 72, 128).transpose(0, 2, 1), mrows=m, gcols=norm_g[l].reshape(24, 128).T)
        maps = []
        for r in range(R):
            d_ = dict(mods)
            d_.update(xin=np.concatenate([x_lat[r * OWN:(r + 1) * OWN], x_ctx], 0), w1=ffn_w1[l, 0], w3=ffn_w3[l, 0],
                      w2=ffn_w2[l, 0], win=w_in[l][:, :NPA],
                      ropec=np.concatenate([cosL[r * OWN:(r + 1) * OWN], cosC], 0),
                      ropes=np.concatenate([sinL[r * OWN:(r + 1) * OWN], sinC], 0))
            maps.append(d_)
        resA = _run("A", maps)
        xsA = [resA[r]["xs"] for r in range(R)]
        pal = [resA[r]["pall"] for r in range(R)]
        P_lat = np.concatenate([p[:OWN] for p in pal], 0)
        P_ctx = pal[0][OWN:]
        P_all = np.concatenate([P_lat, P_ctx], 0)
        kT = P_all[:, C_DK:C_DK + 384].reshape(SEQ + CTX, 6, 64).transpose(1, 2, 0)
        vv = P_all[:, C_DV:C_DV + 384].reshape(NKT, 128, 6, 64).transpose(2, 1, 0, 3)
        lamv = np.broadcast_to(np.asarray(diff_lambda[l]).reshape(1, 128), (128, 128))
        cst = np.broadcast_to(np.array([lam_init, 1.0 - lam_init, 0, 0], f32)[None, :], (128, 4))
        sub = np.broadcast_to(np.asarray(diff_subln[l])[None, :], (128, 64))
        maps = []
        for r in range(R):
            q = pal[r][:, C_DQ:C_DQ + 384].reshape(TOK, 6, 64).transpose(1, 2, 0)
            maps.append(dict(qT=q, kT=kT, vv=vv, lamv=lamv, cst=cst, subln=sub))
        resT = _run("ATT", maps)
        units = []
        for u in range(12):
            dr, h = u // 6, u % 6
            if dr == 0:
                seq = np.concatenate([P_ctx, P_lat], 0)
            else:
                seq = np.concatenate([P_ctx[::-1], P_lat[::-1]], 0)
            q = seq[:, C_RQ + h * 32:C_RQ + (h + 1) * 32]
            k = seq[:, C_RK + h * 32:C_RK + (h + 1) * 32]
            v = seq[:, C_RV + h * 64:C_RV + (h + 1) * 64]
            units.append(dict(qT=q.T, kT=k.T, kk=k.reshape(NKT, 128, 32).transpose(1, 0, 2),
                              vv=v.reshape(NKT, 128, 64).transpose(1, 0, 2),
                              lg=np.broadcast_to(np.asarray(ret_decay_logit[l, dr, h], f32).reshape(1, 1), (128, 1))))
        maps = []
        for r in range(R):
            d_ = dict(umat=umat, mmat=mmat, irow=irow, pcol=pcol)
            for i in range(2):
                u = (2 * r + i) % 12
                for k_, v_ in units[u].items():
                    d_["%s%d" % (k_, i)] = v_
            maps.append(d_)
        resR = _run("RET", maps)
        rdir = [np.zeros((SEQ + CTX, 384), f32), np.zeros((SEQ + CTX, 384), f32)]
        for u in range(12):
            dr, h = u // 6, u % 6
            ro = resR[u // 2]["ro%d" % (u % 2)]
            o = ro.transpose(1, 0, 2).reshape(SEQ + CTX, 64)
            if dr == 1:
                o = np.concatenate([o[:CTX][::-1], o[CTX:][::-1]], 0)
            rdir[dr][:, h * 64:(h + 1) * 64] = o
        maps = []
        for r in range(R):
            g, hf = r // 2, r % 2
            d_ = dict(fftc[hf])
            d_["uT"] = P_all[:, g * 64:(g + 1) * 64].T
            maps.append(d_)
        resF = _run("FFT", maps)
        F_lat = np.zeros((SEQ, 256), f32)
        F_ctx = np.zeros((CTX, 256), f32)
        for r in range(R):
            g, hf = r // 2, r % 2
            y = resF[r]["y"].reshape(128, 32, 128)
            F_lat[:, g * 64 + 32 * hf:g * 64 + 32 * hf + 32] = y.transpose(0, 2, 1).reshape(SEQ, 32)
            yc = resF[r]["yc"]
            F_ctx[:, g * 64 + 32 * hf:g * 64 + 32 * hf + 32] = yc.transpose(1, 0, 2).reshape(CTX, 32)
        fgb = np.broadcast_to(np.asarray(final_g)[None, :], (128, D))
        maps = []
        for r in range(R):
            sl = slice(r * OWN, (r + 1) * OWN)
            Ft = np.concatenate([F_lat[sl], F_ctx], 0)
            Dt = resT[r]["dout"]
            d_ = dict(mods)
            d_.update(xin=xsA[r], w1=ffn_w1[l, 1], w3=ffn_w3[l, 1], w2=ffn_w2[l, 1], wgt=w_in[l][:, NPA:],
                      wbf=w_branch_f[l], wbr=w_branch_r[l], wbd=w_branch_d[l], wo=w_out[l],
                      foT=Ft.T.reshape(2, 128, TOK), doT=Dt.T.reshape(3, 128, TOK),
                      rf=np.concatenate([rdir[0][CTX:][sl], rdir[0][:CTX]], 0),
                      rb=np.concatenate([rdir[1][CTX:][sl], rdir[1][:CTX]], 0),
                      rg=pal[r][:, C_RG:C_RG + 384], fgb=fgb)
            maps.append(d_)
        resC = _run("C", maps)
        x_lat = np.concatenate([resC[r]["xs"][:OWN] for r in range(R)], 0)
        x_ctx = resC[0]["xs"][OWN:]
        xfin = np.concatenate([resC[r]["xfin"] for r in range(R)], 0)
    return xfin[None].astype(np.float32)


I32 = mybir.dt.int32
PIECE_ROWS = 3968
P2_ROWS = 2600
FM_BLOCKS = [0, 128, 256, 384, 512, 1408, 1536, 1664, 1792, 1920, 2048]


def build_fused(depth=DEPTH):
    nc = new_prog()
    with ExitStack() as es:
        cx = setup_common(nc, es)
        pg = cx.pg
        L4 = DEPTH
        x_in = din(nc, "x_in", [TOK, D])
        cc_d = din(nc, "cc", [128, 8, 2])
        wada = din(nc, "w_ada", [L4, D, 9216])
        bada = din(nc, "b_ada2", [L4, 2, 9216])
        gcols = din(nc, "gcols", [L4, 128, 24])
        w1a = din(nc, "ffn_w1", [L4, 2, D, DFF])
        w3a = din(nc, "ffn_w3", [L4, 2, D, DFF])
        w2a = din(nc, "ffn_w2", [L4, 2, DFF, D])
        wina = din(nc, "w_in", [L4, D, D_IN])
        rc = din(nc, "ropec", [TOK, 1152])
        rs = din(nc, "ropes", [TOK, 1152])
        lamv_d = din(nc, "lamv", [L4, 128, 128])
        cst_d = din(nc, "cst", [L4, 128, 4])
        sub_d = din(nc, "subln", [L4, 128, 64])
        lg_d = din(nc, "lgin", [L4, 2, 128, 1])
        um_d = din(nc, "umat", [2, 128, 128])
        mm_d = din(nc, "mmat", [2, 128, 128])
        ir_d = din(nc, "irow", [2, 128, 128])
        pc_d = din(nc, "pcol", [2, 128, 1])
        fnames = ["ccs", "cm1", "sm1", "nsm1", "cm3", "sm3"]
        fshp = {"ccs": [64, 64]}
        f_d = {n_: din(nc, n_, fshp.get(n_, [128, 128]), BF16) for n_ in fnames}
        twc_d = din(nc, "twc", [128, 4096])
        tws_d = din(nc, "tws", [128, 4096])
        c256_d = din(nc, "c256", [128, 2, 256], BF16)
        s256_d = din(nc, "s256", [128, 2, 256], BF16)
        wbf_a = din(nc, "wbf", [L4, 256, D])
        wbr_a = din(nc, "wbr", [L4, 384, D])
        wbd_a = din(nc, "wbd", [L4, 384, D])
        wo_a = din(nc, "wo", [L4, D, D])
        fg_d = din(nc, "fgb", [128, D])
        rank_d = din(nc, "rankc", [1, 8], I32)
        xf = dout(nc, "xfin", [OWN, D])
        m_all = nc.dram_tensor("m_all", [L4, 2, 9216], F32).ap()
        xs = nc.dram_tensor("xs_i", [TOK, D], F32).ap()
        pall = nc.dram_tensor("pall_i", [TOK, NPA], BF16).ap()
        dout_i = nc.dram_tensor("dout_i", [TOK, 384], F32).ap()
        piece_t = nc.dram_tensor("piece", [PIECE_ROWS, 1024], BF16)
        G_t = nc.dram_tensor("Gath", [NCORES * PIECE_ROWS, 1024], BF16)
        piece2_t = nc.dram_tensor("piece2", [P2_ROWS, 1024], F32)
        G2_t = nc.dram_tensor("Gath2", [NCORES * P2_ROWS, 1024], F32)
        pTc = nc.dram_tensor("pTc", [1408, CTX], BF16).ap()
        vpc = nc.dram_tensor("vpc", [6, 128, 2, 64], BF16).ap()
        kkc = nc.dram_tensor("kkc", [6, 128, 2, 32], BF16).ap()
        rvc = nc.dram_tensor("rvc", [6, 128, 2, 64], BF16).ap()
        piece = piece_t.ap()
        G = G_t.ap()
        piece2 = piece2_t.ap()
        G2 = G2_t.ap()
        pTo = piece[0:2816, :].rearrange("(f two) c -> f (two c)", two=2)
        pieceV_t = nc.dram_tensor("pieceV", [768, 1024], BF16)
        GVt = nc.dram_tensor("GathV", [NCORES * 768, 1024], BF16)
        vpo = pieceV_t.ap().rearrange("(h p) (j d) -> h p j d", h=6, d=64)
        kko = piece[2816:3200, :].rearrange("a (two c) -> (a two) c", two=2).rearrange(
            "(h p) (j d) -> h p j d", h=6, d=32)
        rvo = piece[3200:3968, :].rearrange("(h p) (j d) -> h p j d", h=6, d=64)
        Gr = G.rearrange("(r a) c -> r a c", r=NCORES)
        GT = Gr[:, 0:2816, :].rearrange("r (f two) c -> r f (two c)", two=2)
        GV = GVt.ap().rearrange("(r h p) (j d) -> r h p j d", r=NCORES, h=6, d=64)
        GK = Gr[:, 2816:3200, :].rearrange("r a (two c) -> r (a two) c", two=2).rearrange(
            "r (h p) (j d) -> r h p j d", h=6, d=32)
        GRV = Gr[:, 3200:3968, :].rearrange("r (h p) (j d) -> r h p j d", h=6, d=64)
        GK2 = Gr[:, 2816:3200, :].rearrange("r a (two c) -> r (a two) c", two=2)
        GRV2 = Gr[:, 3200:3968, :]
        RO = piece2[0:2080, :].rearrange("a c -> (a c)").rearrange("(u p n d) -> u p n d", u=2, p=128, n=NKT)
        Yp = piece2[2080:2592, :].rearrange("(a four) c -> a (four c)", four=4)
        YCp = piece2[2592:2600, :].rearrange("(m a) c -> m (a c)", m=32) if False else \
            piece2[2592:2600, :].rearrange("a c -> (a c)").rearrange("(m n) -> m n", m=32)
        G2r = G2.rearrange("(r a) c -> r a c", r=NCORES)
        GRO = G2r[:, 0:2080, :].rearrange("r a c -> r (a c)").rearrange("r (u p n d) -> r u p n d", u=2, p=128, n=NKT)
        GY = G2r[:, 2080:2592, :].rearrange("r (a four) c -> r a (four c)", four=4).rearrange(
            "r a (m n) -> r a m n", m=32)
        GYC = G2r[:, 2592:2600, :].rearrange("r a c -> r (a c)").rearrange("r (m t n) -> r m t n", m=32, t=2)
        HT = es.enter_context(nc.sbuf_tensor("HT", [128, 8, TOK], BF16))
        rank_s = es.enter_context(nc.sbuf_tensor("rank_s", [1, 8], I32))
        pg.dyn_src = rank_s
        pg.dma("sp", rank_s[:], rank_d, w=[("rank_s",)])
        with ExitStack() as e0:
            ccs_ = e0.enter_context(nc.sbuf_tensor("ccs_a", [128, 8, 2], F32))
            scs = e0.enter_context(nc.sbuf_tensor("scs_a", [128, 8, 2], F32))
            bas = [e0.enter_context(nc.sbuf_tensor("bas%d" % i, [2, 4608], F32)) for i in range(2)]
            mos = [e0.enter_context(nc.sbuf_tensor("mos%d" % i, [2, 4608], F32)) for i in range(2)]
            wsb = [e0.enter_context(nc.sbuf_tensor("wsb%d" % i, [128, 8, 512], F32)) for i in range(2)]
            pg.dma("sp", ccs_[:], cc_d, w=[("ccs_a",)])
            pg.op("act", lambda e: e.activation(out=scs[:], in_=ccs_[:], func=AF.Silu), r=[("ccs_a",)], w=[("scs",)])
            it = 0
            for l in range(depth):
                for hf in range(2):
                    hb = (l * 2 + hf) % 2
                    pg.dma("sp", bas[hb][:], bada[l, :, hf * 4608:(hf + 1) * 4608], w=[("bas", hb)])
                    for cb in range(9):
                        b = it % 2
                        it += 1
                        col0 = hf * 4608 + cb * 512
                        pg.dma("sp", wsb[b][:], wada[l, :, col0:col0 + 512].rearrange("(c p) n -> p c n", p=128),
                               w=[("wsb", b)])
                        for c in range(8):
                            pg.op("pe", (lambda e, c=c, b=b: e.matmul(cx.psb[b][0:2, :], scs[:, c, :], wsb[b][:, c, :],
                                                                      start=(c == 0), stop=(c == 7))),
                                  r=[("scs",), ("wsb", b)], w=[("ps", b)])
                        pg.op("dve", (lambda e, cb=cb, b=b, hb=hb: e.tensor_tensor(
                            out=mos[hb][:, cb * 512:(cb + 1) * 512], in0=cx.psb[b][0:2, :],
                            in1=bas[hb][:, cb * 512:(cb + 1) * 512], op=ALU.add)),
                            r=[("ps", b), ("bas", hb)], w=[("mos", hb, cb)])
                    pg.dma("sp", m_all[l, :, hf * 4608:(hf + 1) * 4608], mos[hb][:],
                           r=[("mos", hb, cb) for cb in range(9)], w=[("m_all",)])
            for t in range(NT):
                pg.dma("sp", xs[t * 128:(t + 1) * 128, :], x_in[t * 128:(t + 1) * 128, :], w=[("xs", t)])
            pg.emit(cx.scratch)

        for l in range(depth):
            sfx = "_L%d" % l
            el = ExitStack()
            load_mod(cx, el, mcols_d=m_all[l].rearrange("t (k p) -> t p k", p=128), mrows_d=m_all[l], gcols_d=gcols[l], sfx=sfx)
            ffn(cx, xs, w1a[l, 0], w3a[l, 0], w2a[l, 0], 0, HT, "f1")
            with ExitStack() as e2:
                norm_to_HT(cx, e2, xs, 1, HT)
                sb = lambda n_, s_, d_: e2.enter_context(nc.sbuf_tensor(n_ + sfx, s_, d_))
                Win = sb("a_Win", [128, 8, NPA], BF16)
                pt = [sb("a_pt%d" % i, [128, NPA], F32) for i in range(2)]
                psw = [sb("a_psw%d" % i, [128, 1152], F32) for i in range(2)]
                ct = [sb("a_ct%d" % i, [128, 1152], F32) for i in range(2)]
                stb = [sb("a_st%d" % i, [128, 1152], F32) for i in range(2)]
                pb = [sb("a_pb%d" % i, [128, NPA], BF16) for i in range(2)]
                stg = [sb("a_stg%d" % i, [128, 11, 128], BF16) for i in range(2)]
                for nb in range(5):
                    pg.dma("pool", Win[:, :, nb * 512:(nb + 1) * 512],
                           wina[l, :, nb * 512:(nb + 1) * 512].rearrange("(c p) n -> p c n", p=128), w=[("a_Win", nb)])
                for t in range(NT):
                    b = t % 2
                    pg.dma("sp", ct[b][:], rc[t * 128:(t + 1) * 128, :], w=[("a_ct", b)])
                    pg.dma("sp", stb[b][:], rs[t * 128:(t + 1) * 128, :], w=[("a_st", b)])
                    for nb in range(5):
                        bk = 2 + nb
                        for c in range(8):
                            pg.op("pe", (lambda e, c=c, t=t, nb=nb, bk=bk: e.matmul(
                                cx.psb[bk][:, :], HT[:, c, t * 128:(t + 1) * 128], Win[:, c, nb * 512:(nb + 1) * 512],
                                start=(c == 0), stop=(c == 7))),
                                r=[("a_Win", nb)] + ([("HT", cc, t) for cc in range(8)] if c == 0 else []), w=[("ps", bk)])
                        pg.op("act", (lambda e, b=b, nb=nb, bk=bk: e.activation(out=pt[b][:, nb * 512:(nb + 1) * 512],
                                                                               in_=cx.psb[bk][:, :], func=AF.Copy)),
                              r=[("ps", bk)], w=[("a_pt", b, nb)])
                    ptk = [("a_pt", b, nb) for nb in range(5)]
                    for (c0, w_, hf, toff) in ROPE_REGIONS:
                        src = pt[b][:, c0:c0 + w_].rearrange("p (g h k) -> p g h k", h=2, k=hf)
                        dst = psw[b][:, toff:toff + w_].rearrange("p (g h k) -> p g h k", h=2, k=hf)
                        for h in range(2):
                            pg.op("pool", (lambda e, src=src, dst=dst, h=h: e.tensor_copy(dst[:, :, h, :], src[:, :, 1 - h, :])),
                                  r=ptk, w=[("a_psw", b, toff, h)])
                    pswk = [("a_psw", b, toff, h) for (_, _, _, toff) in ROPE_REGIONS for h in range(2)]
                    for (c0, w_, toff) in ROPE_SPANS:
                        pg.op("dve", (lambda e, b=b, c0=c0, w_=w_, toff=toff: e.tensor_tensor(
                            out=pt[b][:, c0:c0 + w_], in0=pt[b][:, c0:c0 + w_], in1=ct[b][:, toff:toff + w_], op=ALU.mult)),
                            r=ptk + pswk + [("a_ct", b)], w=ptk)
                        pg.op("pool", (lambda e, b=b, w_=w_, toff=toff: e.tensor_tensor(
                            out=psw[b][:, toff:toff + w_], in0=psw[b][:, toff:toff + w_], in1=stb[b][:, toff:toff + w_],
                            op=ALU.mult)), r=pswk + [("a_st", b)], w=pswk)
                        pg.op("dve", (lambda e, b=b, c0=c0, w_=w_, toff=toff: e.tensor_tensor(
                            out=pt[b][:, c0:c0 + w_], in0=pt[b][:, c0:c0 + w_], in1=psw[b][:, toff:toff + w_], op=ALU.add)),
                            r=ptk + pswk, w=ptk)
                    pg.op("act", (lambda e, b=b: e.activation(out=pb[b][:], in_=pt[b][:], func=AF.Copy)),
                          r=ptk, w=[("a_pb", b)])
                    pg.dma("sp", pall[t * 128:(t + 1) * 128, :], pb[b][:], r=[("a_pb", b)], w=[("pall", t)])
                    ps7 = cx.psb[7][:, :].bitcast(BF16)
                    ps0 = cx.psb[0][:, :].bitcast(BF16)
                    for k, c0 in enumerate(FM_BLOCKS):
                        dstp = ps7[:, k * 128:(k + 1) * 128] if k < 8 else ps0[:, (k - 8) * 128:(k - 7) * 128]
                        pg.op("pe", (lambda e, b=b, c0=c0, dstp=dstp: e.transpose(dstp, pb[b][:, c0:c0 + 128], cx.ident_bf[:])),
                              r=[("a_pb", b), ("ident_bf",)], w=[("ps", 7 if k < 8 else 0)])
                    pg.op("act", (lambda e, b=b, ps7=ps7: e.activation(out=stg[b][:, 0:8, :].rearrange("p a c -> p (a c)"),
                                                                       in_=ps7[:, :], func=AF.Copy)),
                          r=[("ps", 7)], w=[("a_stg", b, 0)])
                    pg.op("act", (lambda e, b=b, ps0=ps0: e.activation(out=stg[b][:, 8:11, :].rearrange("p a c -> p (a c)"),
                                                                       in_=ps0[:, 0:384], func=AF.Copy)),
                          r=[("ps", 0)], w=[("a_stg", b, 1)])
                    sk = [("a_stg", b, 0), ("a_stg", b, 1)]
                    if t < NTO:
                        pg.dma("sp", pTo.rearrange("(k p) n -> p k n", p=128)[:, :, t * 128:(t + 1) * 128], stg[b][:],
                               r=sk, w=[("piece", "pT", t)])
                        j = t
                        vd, kd_, rd_ = vpo, kko, rvo
                    else:
                        j = t - NTO
                        pg.dma("sp", pTc.rearrange("(k p) n -> p k n", p=128)[:, :, j * 128:(j + 1) * 128], stg[b][:],
                               r=sk, w=[("pTc", j)])
                        vd, kd_, rd_ = vpc, kkc, rvc
                    pg.dma("sp", vd[:, :, j, :].rearrange("h p d -> p h d"),
                           pb[b][:, C_DV:C_DV + 384].rearrange("p (h d) -> p h d", d=64), r=[("a_pb", b)], w=[("piece", "v", t)])
                    pg.dma("sp", kd_[:, :, j, :].rearrange("h p d -> p h d"),
                           pb[b][:, C_RK:C_RK + 192].rearrange("p (h d) -> p h d", d=32), r=[("a_pb", b)], w=[("piece", "k", t)])
                    pg.dma("sp", rd_[:, :, j, :].rearrange("h p d -> p h d"),
                           pb[b][:, C_RV:C_RV + 384].rearrange("p (h d) -> p h d", d=64), r=[("a_pb", b)], w=[("piece", "rv", t)])
                pk = [("piece", kind, t) for kind in ("k", "rv") for t in range(NT)] + [("piece", "pT", t) for t in range(NTO)]
                pg.coll(piece_t.ap().opt(), G_t.ap().opt(), r=pk, w=[("G",)])
                pg.coll(pieceV_t.ap().opt(), GVt.ap().opt(), r=[("piece", "v", t) for t in range(NT)], w=[("GV",)])
                pg.emit(cx.scratch)
            with ExitStack() as e3:
                sb = lambda n_, s_, d_: e3.enter_context(nc.sbuf_tensor(n_ + sfx, s_, d_))
                QT = sb("QT", [64, 6, TOK], BF16)
                KT = sb("KT", [64, NKT * 128], BF16)
                V = sb("V", [128, NKT, 65], BF16)
                PT = [sb("PT%d" % i, [128, 2, 512], BF16) for i in range(2)]
                osb = [sb("osb%d" % i, [65, 512], F32) for i in range(2)]
                on = [sb("on%d" % i, [128, 4, 64], F32) for i in range(2)]
                dd = sb("dd", [128, 64], F32)
                junk = sb("junk", [128, 64], F32)
                stt = sb("stt", [128, 8], F32)
                lamv = sb("lamv_s", [128, 128], F32)
                lt = sb("lt", [128, 64], F32)
                ls = sb("ls", [128, 8], F32)
                cst = sb("cst_s", [128, 4], F32)
                subw = sb("subw", [128, 64], F32)
                dsb = sb("dsb", [128, NT, 384], F32)
                pg.dma("sp", lamv[:], lamv_d[l], w=[("lamv",)])
                pg.dma("sp", cst[:], cst_d[l], w=[("cst",)])
                pg.dma("sp", subw[:], sub_d[l], w=[("subw",)])
                for h in range(6):
                    pg.dma("sp", QT[:, h, 0:OWN], pTo[640 + h * 64:640 + (h + 1) * 64, :], w=[("QT", h, 0)])
                    pg.dma("sp", QT[:, h, OWN:TOK], pTc[640 + h * 64:640 + (h + 1) * 64, :], w=[("QT", h, 1)])
                pg.op("dve", lambda e: e.memset(V[:, :, 64:65], 1.0), w=[("Vone",)])
                pg.op("dve", lambda e: e.memset(ls[:], 0.0), w=[("ls",)])
                pg.op("dve", lambda e: e.tensor_tensor(out=lt[:, 0:32], in0=lamv[:, 0:32], in1=lamv[:, 32:64], op=ALU.mult),
                      r=[("lamv",)], w=[("lt", 0)])
                pg.op("dve", lambda e: e.tensor_tensor(out=lt[:, 32:64], in0=lamv[:, 64:96], in1=lamv[:, 96:128], op=ALU.mult),
                      r=[("lamv",)], w=[("lt", 1)])
                pg.op("dve", lambda e: e.tensor_reduce(out=ls[:, 0:1], in_=lt[:, 0:32], axis=AX.X, op=ALU.add),
                      r=[("lt", 0), ("ls",)], w=[("ls",)])
                pg.op("dve", lambda e: e.tensor_reduce(out=ls[:, 1:2], in_=lt[:, 32:64], axis=AX.X, op=ALU.add),
                      r=[("lt", 1), ("ls",)], w=[("ls",)])
                pg.op("act", lambda e: e.activation(out=ls[:, 2:4], in_=ls[:, 0:2], func=AF.Exp), r=[("ls",)], w=[("ls",)])
                pg.op("dve", lambda e: e.tensor_tensor(out=ls[:, 4:5], in0=ls[:, 3:4], in1=ls[:, 2:3], op=ALU.subtract),
                      r=[("ls",)], w=[("ls",)])
                pg.op("dve", lambda e: e.tensor_tensor(out=ls[:, 5:6], in0=ls[:, 4:5], in1=cst[:, 0:1], op=ALU.subtract),
                      r=[("ls",), ("cst",)], w=[("ls",)])
                pg.op("dve", lambda e: e.tensor_scalar(subw[:], subw[:], cst[:, 1:2], None, op0=ALU.mult),
                      r=[("subw",), ("cst",)], w=[("subw",)])
                qgroups = [(0, 512), (512, 512), (1024, 512), (1536, 512), (2048, 256)]
                scale = 32.0 ** -0.5
                si = 0
                for hd in range(6):
                    r0 = 1024 + hd * 64
                    pg.dma("sp", KT[:, 0:SEQ].rearrange("p (r n) -> p r n", r=NCORES),
                           GT[:, r0:r0 + 64, :].rearrange("r p n -> p r n"), r=[("G",)], w=[("KT", 0)])
                    pg.dma("sp", KT[:, SEQ:SEQ + CTX], pTc[r0:r0 + 64, :], w=[("KT", 1)])
                    for rr in range(NCORES):
                        pg.dma("sp", V[:, rr * 16:(rr + 1) * 16, 0:64], GV[rr, hd], r=[("Vone",), ("GV",)], w=[("V", rr)])
                    pg.dma("sp", V[:, 128:130, 0:64], vpc[hd], r=[("Vone",)], w=[("V", 8)])
                    vk = [("V", nb) for nb in range(9)]
                    ktk = [("KT", 0), ("KT", 1)]
                    for qi, (q0, qn) in enumerate(qgroups):
                        kts = list(range(NKT)) if qi < 4 else [NKT - 2, NKT - 1]
                        qk_ = [("QT", hd, 0 if qi < 4 else 1)]
                        for mp in range(2):
                            ob = 4 + mp
                            npair = len(kts) // 2
                            for pi in range(npair):
                                sj = si % 2
                                si += 1
                                for hh in range(2):
                                    kt = kts[2 * pi + hh]
                                    pg.op("pe", (lambda e, sj=sj, hh=hh, mp=mp, kt=kt, hd=hd, q0=q0, qn=qn: e.matmul(
                                        cx.psb2[sj][:, hh * 512:hh * 512 + qn], KT[32 * mp:32 * mp + 32, kt * 128:(kt + 1) * 128],
                                        QT[32 * mp:32 * mp + 32, hd, q0:q0 + qn], start=True, stop=True)),
                                        r=ktk + qk_, w=[("ps", 2 * sj + hh)])
                                pg.op("act", (lambda e, sj=sj, qn=qn: e.activation(
                                    out=PT[sj][:, :, 0:qn], in_=cx.psb2[sj][:, :].rearrange("p (a b) -> p a b", a=2)[:, :, 0:qn],
                                    func=AF.Exp, scale=scale)),
                                    r=[("ps", 2 * sj), ("ps", 2 * sj + 1)], w=[("PT", sj)])
                                for hh in range(2):
                                    kt = kts[2 * pi + hh]
                                    ki = 2 * pi + hh
                                    pg.op("pe", (lambda e, sj=sj, hh=hh, kt=kt, ob=ob, qn=qn, ki=ki, nk=len(kts): e.matmul(
                                        cx.psb[ob][0:65, 0:qn], V[:, kt, :], PT[sj][:, hh, 0:qn], start=(ki == 0), stop=(ki == nk - 1))),
                                        r=[("PT", sj)] + (vk if ki == 0 else []), w=[("ps", ob)])
                            pg.op("act", (lambda e, mp=mp, ob=ob, qn=qn: e.activation(out=osb[mp][:, 0:qn], in_=cx.psb[ob][0:65, 0:qn],
                                                                                     func=AF.Copy)),
                                  r=[("ps", ob)], w=[("osb", mp)])
                            for tt in range(qn // 128):
                                tb = 6 + (tt % 2)
                                pg.op("pe", (lambda e, mp=mp, tt=tt, tb=tb: e.matmul(
                                    cx.psb[tb][:, 0:65], osb[mp][0:65, tt * 128:(tt + 1) * 128], cx.ident_f[0:65, 0:65],
                                    start=True, stop=True)), r=[("osb", mp), ("ident_f",)], w=[("ps", tb)])
                                pg.op("dve", (lambda e, tb=tb: e.reciprocal(stt[:, 0:1], cx.psb[tb][:, 64:65])),
                                      r=[("ps", tb)], w=[("stt", 0)])
                                pg.op("dve", (lambda e, tb=tb, mp=mp, tt=tt: e.tensor_scalar(
                                    on[mp][:, tt, :], cx.psb[tb][:, 0:64], stt[:, 0:1], None, op0=ALU.mult)),
                                    r=[("ps", tb), ("stt", 0)], w=[("on", mp, tt)])
                        for tt in range(qn // 128):
                            t = q0 // 128 + tt
                            pg.op("dve", (lambda e, tt=tt: e.scalar_tensor_tensor(
                                out=dd[:], in0=on[1][:, tt, :], scalar=ls[:, 5:6], in1=on[0][:, tt, :], op0=ALU.mult, op1=ALU.add)),
                                r=[("on", 0, tt), ("on", 1, tt), ("ls",)], w=[("dd",)])
                            pg.op("dve", (lambda e: e.memset(stt[:, 1:2], 0.0)), w=[("stt", 1)])
                            pg.op("act", (lambda e: e.activation(out=junk[:], in_=dd[:], func=AF.Square, accum_out=stt[:, 1:2])),
                                  r=[("dd",)], w=[("junk",), ("stt", 1)])
                            pg.op("dve", (lambda e: e.tensor_scalar(stt[:, 2:3], stt[:, 1:2], 1.0 / 64, EPS, op0=ALU.mult, op1=ALU.add)),
                                  r=[("stt", 1)], w=[("stt", 2)])
                            pg.op("act", (lambda e: e.activation(out=stt[:, 3:4], in_=stt[:, 2:3], func=AF.Sqrt)),
                                  r=[("stt", 2)], w=[("stt", 3)])
                            pg.op("dve", (lambda e: e.reciprocal(stt[:, 4:5], stt[:, 3:4])), r=[("stt", 3)], w=[("stt", 4)])
                            pg.op("dve", (lambda e, t=t, hd=hd: e.scalar_tensor_tensor(
                                out=dsb[:, t, hd * 64:(hd + 1) * 64], in0=dd[:], scalar=stt[:, 4:5], in1=subw[:],
                                op0=ALU.mult, op1=ALU.mult)), r=[("dd",), ("stt", 4), ("subw",)], w=[("dsb", t)])
                for t in range(NT):
                    pg.dma("sp", dout_i[t * 128:(t + 1) * 128, :], dsb[:, t, :], r=[("dsb", t)], w=[("do", t)])
                pg.emit(cx.scratch)
            _fused_ret(cx, sfx, l, GT, GK2, GRV2, pTc, kkc, rvc, lg_d, um_d, mm_d, ir_d, pc_d, RO)
            _fused_fft(cx, sfx, GT, pTc, f_d, twc_d, tws_d, c256_d, s256_d, Yp, YCp, piece2_t, G2_t)
            _fused_c(cx, sfx, l, depth, xs, pall, dout_i, GRO, GY, GYC, HT, wina, wbf_a, wbr_a, wbd_a, wo_a,
                     w1a, w3a, w2a, fg_d, xf)
            el.close()
    return nc


def _fused_ret(cx, sfx, l, GT, GK, GRV, pTc, kkc, rvc, lg_d, um_d, mm_d, ir_d, pc_d, RO):
    nc, pg = cx.nc, cx.pg
    NTK = NKT * 128
    with ExitStack() as es:
        sb = lambda n_, s_, d_: es.enter_context(nc.sbuf_tensor(n_ + sfx, s_, d_))
        umat = sb("umat_s", [128, 128], F32)
        mmat = sb("mmat_s", [128, 128], F32)
        irow = sb("irow_s", [128, 128], F32)
        pcol = sb("pcol_s", [128, 1], F32)
        qT = sb("r_qT", [32, NTK], BF16)
        kT = sb("r_kT", [32, NTK], BF16)
        kk = sb("r_kk", [128, NKT, 32], BF16)
        vv = sb("r_vv", [128, NKT, 64], BF16)
        osb = sb("r_osb", [128, NKT, 64], F32)
        lg = sb("r_lg", [128, 8], F32)
        DT = sb("r_DT", [128, 128], F32)
        qdec = sb("r_qdec", [32, 128], F32)
        ATm = [sb("r_ATm%d" % i, [128, 128], BF16) for i in range(2)]
        qd = [sb("r_qd%d" % i, [32, 128], BF16) for i in range(2)]
        kd = [sb("r_kd%d" % i, [128, 32], BF16) for i in range(2)]
        S32 = sb("r_S32", [32, 64], F32)
        Sbf = [sb("r_Sbf%d" % i, [32, 64], BF16) for i in range(2)]
        for (dst, vi, key, qq) in ((qT, 1, "qT", "sp"), (kT, 2, "kT", "act")):
            pg.dmaf(qq, (lambda e, dv, dst=dst, vi=vi: e.dma_start(
                out=dst[:, 0:SEQ].rearrange("p (r n) -> p r n", r=NCORES),
                in_=GT[:, bass.ds(dv[vi], 32), :].rearrange("r p n -> p r n"))), r=[("G",)], w=[(key, 0)])
            pg.dmaf(qq, (lambda e, dv, dst=dst, vi=vi: e.dma_start(
                out=dst[:, SEQ:NTK], in_=pTc[bass.ds(dv[vi], 32), :])), w=[(key, 1)])
        pg.dmaf("sp", (lambda e, dv: e.dma_start(
            out=kk[:, 0:128, :].rearrange("p (r j) d -> p r (j d)", r=NCORES),
            in_=GK[:, bass.ds(dv[3], 128), :].rearrange("r p c -> p r c"))), r=[("G",)], w=[("kk", 0)])
        pg.dmaf("act", (lambda e, dv: e.dma_start(
            out=vv[:, 0:128, :].rearrange("p (r j) d -> p r (j d)", r=NCORES),
            in_=GRV[:, bass.ds(dv[3], 128), :].rearrange("r p c -> p r c"))), r=[("G",)], w=[("vv", 0)])
        pg.dmaf("sp", (lambda e, dv: e.dma_start(out=kk[:, 128:130, :].rearrange("p j d -> p (j d)"),
                                                in_=kkc.rearrange("h p j d -> (h p) (j d)")[bass.ds(dv[3], 128), :])), w=[("kk", 8)])
        pg.dmaf("act", (lambda e, dv: e.dma_start(out=vv[:, 128:130, :].rearrange("p j d -> p (j d)"),
                                                in_=rvc.rearrange("h p j d -> (h p) (j d)")[bass.ds(dv[3], 128), :])), w=[("vv", 8)])
        qk = [("qT", 0), ("qT", 1)]
        kk_ = [("kT", 0), ("kT", 1)]
        kkk = [("kk", 0), ("kk", 8)]
        vvk = [("vv", 0), ("vv", 8)]
        for u in range(2):
            pg.dma("sp", umat[:], um_d[u], w=[("umat",)])
            pg.dma("sp", mmat[:], mm_d[u], w=[("mmat",)])
            pg.dma("sp", irow[:], ir_d[u], w=[("irow",)])
            pg.dma("sp", pcol[:], pc_d[u], w=[("pcol",)])
            pg.dma("sp", lg[:, 0:1], lg_d[l, u], w=[("lg",)])
            pg.op("act", lambda e: e.activation(out=lg[:, 1:2], in_=lg[:, 0:1], func=AF.Exp, scale=-1.0), r=[("lg",)], w=[("lg",)])
            pg.op("dve", lambda e: e.tensor_scalar(lg[:, 2:3], lg[:, 1:2], 1.0, None, op0=ALU.add), r=[("lg",)], w=[("lg",)])
            pg.op("act", lambda e: e.activation(out=lg[:, 3:4], in_=lg[:, 2:3], func=AF.Ln), r=[("lg",)], w=[("lg",)])
            pg.op("dve", lambda e: e.tensor_scalar(lg[:, 4:5], lg[:, 3:4], -1.0, None, op0=ALU.mult), r=[("lg",)], w=[("lg",)])
            pg.op("act", lambda e: e.activation(out=DT[:], in_=umat[:], func=AF.Exp, scale=lg[:, 4:5]),
                  r=[("lg",), ("umat",)], w=[("DT",)])
            pg.op("dve", lambda e: e.tensor_tensor(out=DT[:], in0=DT[:], in1=mmat[:], op=ALU.mult), r=[("DT",), ("mmat",)], w=[("DT",)])
            pg.op("act", lambda e: e.activation(out=qdec[:], in_=irow[0:32, :], func=AF.Exp, scale=lg[0:32, 4:5]),
                  r=[("lg",), ("irow",)], w=[("qdec",)])
            pg.op("act", lambda e: e.activation(out=lg[:, 5:6], in_=pcol[:], func=AF.Exp, scale=lg[:, 4:5]),
                  r=[("lg",), ("pcol",)], w=[("lg",)])
            pg.op("act", lambda e: e.activation(out=lg[:, 6:7], in_=lg[:, 4:5], func=AF.Exp, scale=128.0),
                  r=[("lg",)], w=[("lg",)])
            pg.op("dve", lambda e: e.memset(S32[:], 0.0), w=[("S32",)])
            pg.op("dve", lambda e: e.memset(Sbf[0][:], 0.0), w=[("Sbf", 0)])
            order = [128, 129] + list(range(128)) if u == 0 else [129, 128] + list(range(127, -1, -1))
            for ci, c in enumerate(order):
                b = ci % 2
                pg.op("pe", (lambda e, c=c, b=b: e.matmul(cx.psb[b][:, 0:128], kT[:, c * 128:(c + 1) * 128],
                                                          qT[:, c * 128:(c + 1) * 128], start=True, stop=True)),
                      r=kk_ + qk, w=[("ps", b)])
                pg.op("dve", (lambda e, b=b: e.tensor_tensor(out=ATm[b][:], in0=cx.psb[b][:, 0:128], in1=DT[:], op=ALU.mult)),
                      r=[("ps", b), ("DT",)], w=[("ATm", b)])
                pg.op("pool", (lambda e, c=c, b=b: e.tensor_tensor(out=qd[b][:], in0=qT[:, c * 128:(c + 1) * 128], in1=qdec[:],
                                                                   op=ALU.mult)), r=qk + [("qdec",)], w=[("qd", b)])
                pg.op("pe", (lambda e, c=c, b=b: e.matmul(cx.psb[2 + b][:, 0:64], ATm[b][:], vv[:, c, :], start=True, stop=False)),
                      r=[("ATm", b)] + vvk, w=[("ps", 2 + b)])
                pg.op("pe", (lambda e, c=c, b=b: e.matmul(cx.psb[2 + b][:, 0:64], qd[b][:], Sbf[b][:], start=False, stop=True)),
                      r=[("qd", b), ("Sbf", b)], w=[("ps", 2 + b)])
                pg.op("act", (lambda e, c=c, b=b: e.activation(out=osb[:, c, :], in_=cx.psb[2 + b][:, 0:64], func=AF.Copy)),
                      r=[("ps", 2 + b)], w=[("osb", c)])
                pg.op("act", (lambda e, c=c, b=b: e.activation(out=kd[b][:], in_=kk[:, c, :], func=AF.Copy, scale=lg[:, 5:6])),
                      r=kkk + [("lg",)], w=[("kd", b)])
                pg.op("pe", (lambda e, c=c, b=b: e.matmul(cx.psb[4 + b][0:32, 0:64], kd[b][:], vv[:, c, :], start=True, stop=True)),
                      r=[("kd", b)] + vvk, w=[("ps", 4 + b)])
                pg.op("dve", (lambda e, b=b: e.scalar_tensor_tensor(out=S32[:], in0=S32[:], scalar=lg[0:32, 6:7],
                                                                    in1=cx.psb[4 + b][0:32, 0:64], op0=ALU.mult, op1=ALU.add)),
                      r=[("S32",), ("ps", 4 + b), ("lg",)], w=[("S32",)])
                pg.op("act", (lambda e, b=b: e.activation(out=Sbf[1 - b][:], in_=S32[:], func=AF.Copy)),
                      r=[("S32",)], w=[("Sbf", 1 - b)])
            pg.dma("sp", RO[u], osb[:], r=[("osb", c) for c in range(NKT)], w=[("RO", u)])
        pg.emit(cx.scratch)


def _fused_fft(cx, sfx, GT, pTc, f_d, twc_d, tws_d, c256_d, s256_d, Yp, YCp, piece2_t, G2_t):
    nc, pg = cx.nc, cx.pg
    with ExitStack() as es:
        sb = lambda n_, s_, d_: es.enter_context(nc.sbuf_tensor(n_ + sfx, s_, d_))
        tb = {}
        for n_, d_ in f_d.items():
            tb[n_] = sb(n_ + "_s", list(d_.shape), BF16)
            pg.dma("sp", tb[n_][:], d_, w=[(n_,)])
        uT = sb("uT_s", [64, SEQ + CTX], BF16)
        X = sb("X", [128, 128, 64], BF16)
        Zr = sb("Zr", [128, 4096], F32)
        Zi = sb("Zi", [128, 4096], F32)
        ta = sb("ta", [128, 4096], F32)
        tb2 = sb("tb2", [128, 4096], F32)
        Zpr = sb("Zpr", [128, 4096], BF16)
        Zpi = sb("Zpi", [128, 4096], BF16)
        twc = sb("twc_s", [128, 4096], F32)
        tws = sb("tws_s", [128, 4096], F32)
        c256 = sb("c256_s", [128, 2, 256], BF16)
        s256 = sb("s256_s", [128, 2, 256], BF16)
        Xc = sb("Xc", [128, 2, 64], BF16)
        ycs = sb("ycs", [32, 2, 128], F32)
        pg.dmaf("act", (lambda e, dv: e.dma_start(out=uT[:, 0:SEQ].rearrange("p (r n) -> p r n", r=NCORES),
                                                in_=GT[:, bass.ds(dv[4], 64), :].rearrange("r p n -> p r n"))),
                r=[("G",)], w=[("uT", 0)])
        pg.dmaf("act", (lambda e, dv: e.dma_start(out=uT[:, SEQ:SEQ + CTX], in_=pTc[bass.ds(dv[4], 64), :])), w=[("uT", 1)])
        pg.dma("sp", twc[:], twc_d, w=[("twc",)])
        pg.dma("sp", tws[:], tws_d, w=[("tws",)])
        pg.dma("sp", c256[:], c256_d, w=[("c256",)])
        pg.dma("sp", s256[:], s256_d, w=[("s256",)])
        uk = [("uT", 0), ("uT", 1)]
        uv = uT[:, 0:SEQ].rearrange("c (a b) -> c a b", b=128)
        for g8 in range(16):
            bk = g8 % 2
            for k in range(8):
                t2 = g8 * 8 + k
                pg.op("pe", (lambda e, t2=t2, k=k, bk=bk: e.matmul(cx.psb[bk][:, k * 64:(k + 1) * 64], uv[:, :, t2], tb["ccs"][:],
                                                                  start=True, stop=True)),
                      r=uk + [("ccs",)], w=[("ps", bk)])
            pg.op("act", (lambda e, g8=g8, bk=bk: e.activation(
                out=X[:, g8 * 8:(g8 + 1) * 8, :].rearrange("p a b -> p (a b)"), in_=cx.psb[bk][:, :], func=AF.Copy)),
                r=[("ps", bk)], w=[("X", g8)])
        xk = [("X", g8) for g8 in range(16)]
        for m4 in range(8):
            br, bi = 2 + 2 * (m4 % 2), 3 + 2 * (m4 % 2)
            for k in range(4):
                m = m4 * 4 + k
                xr, xi = X[:, :, m], X[:, :, 32 + m]
                pg.op("pe", (lambda e, xr=xr, k=k, br=br: e.matmul(cx.psb[br][:, k * 128:(k + 1) * 128], xr, tb["cm1"][:],
                                                                  start=True, stop=False)),
                      r=xk + [("cm1",)], w=[("ps", br)])
                pg.op("pe", (lambda e, xi=xi, k=k, br=br: e.matmul(cx.psb[br][:, k * 128:(k + 1) * 128], xi, tb["sm1"][:],
                                                                  start=False, stop=True)),
                      r=[("sm1",)], w=[("ps", br)])
                pg.op("pe", (lambda e, xi=xi, k=k, bi=bi: e.matmul(cx.psb[bi][:, k * 128:(k + 1) * 128], xi, tb["cm1"][:],
                                                                  start=True, stop=False)),
                      r=[], w=[("ps", bi)])
                pg.op("pe", (lambda e, xr=xr, k=k, bi=bi: e.matmul(cx.psb[bi][:, k * 128:(k + 1) * 128], xr, tb["nsm1"][:],
                                                                  start=False, stop=True)),
                      r=[("nsm1",)], w=[("ps", bi)])
            pg.op("act", (lambda e, m4=m4, br=br: e.activation(out=Zr[:, m4 * 512:(m4 + 1) * 512], in_=cx.psb[br][:, :], func=AF.Copy)),
                  r=[("ps", br)], w=[("Zr", m4)])
            pg.op("act", (lambda e, m4=m4, bi=bi: e.activation(out=Zi[:, m4 * 512:(m4 + 1) * 512], in_=cx.psb[bi][:, :], func=AF.Copy)),
                  r=[("ps", bi)], w=[("Zi", m4)])
        zrk = [("Zr", m4) for m4 in range(8)]
        zik = [("Zi", m4) for m4 in range(8)]
        pg.op("dve", lambda e: e.tensor_tensor(out=ta[:], in0=Zr[:], in1=twc[:], op=ALU.mult), r=zrk + [("twc",)], w=[("ta",)])
        pg.op("pool", lambda e: e.tensor_tensor(out=tb2[:], in0=Zi[:], in1=tws[:], op=ALU.mult), r=zik + [("tws",)], w=[("tb2",)])
        pg.op("dve", lambda e: e.tensor_tensor(out=Zpr[:], in0=ta[:], in1=tb2[:], op=ALU.add), r=[("ta",), ("tb2",)], w=[("Zpr",)])
        pg.op("dve", lambda e: e.tensor_tensor(out=ta[:], in0=Zi[:], in1=twc[:], op=ALU.mult), r=zik + [("twc",), ("ta",)], w=[("ta",)])
        pg.op("pool", lambda e: e.tensor_tensor(out=tb2[:], in0=Zr[:], in1=tws[:], op=ALU.mult), r=zrk + [("tws",), ("tb2",)], w=[("tb2",)])
        pg.op("dve", lambda e: e.tensor_tensor(out=Zpi[:], in0=ta[:], in1=tb2[:], op=ALU.subtract), r=[("ta",), ("tb2",)], w=[("Zpi",)])
        for blk in range(8):
            bk = 6 + blk % 2
            pg.op("pe", (lambda e, blk=blk, bk=bk: e.matmul(cx.psb[bk][:, :], tb["cm3"][:], Zpr[:, blk * 512:(blk + 1) * 512],
                                                           start=True, stop=False)), r=[("Zpr",), ("cm3",)], w=[("ps", bk)])
            pg.op("pe", (lambda e, blk=blk, bk=bk: e.matmul(cx.psb[bk][:, :], tb["sm3"][:], Zpi[:, blk * 512:(blk + 1) * 512],
                                                           start=False, stop=True)), r=[("Zpi",), ("sm3",)], w=[("ps", bk)])
            pg.op("act", (lambda e, blk=blk, bk=bk: e.activation(out=Zr[:, blk * 512:(blk + 1) * 512], in_=cx.psb[bk][:, :], func=AF.Copy)),
                  r=[("ps", bk), ("ta",), ("tb2",)], w=[("Zr", blk)])
        pg.dma("sp", Yp, Zr[:], r=zrk, w=[("y",)])
        for tc in range(2):
            pg.op("pe", (lambda e, tc=tc: e.matmul(cx.psb[tc][:, 0:64], uT[:, SEQ + tc * 128:SEQ + (tc + 1) * 128], tb["ccs"][:],
                                                  start=True, stop=True)), r=uk + [("ccs",)], w=[("ps", tc)])
            pg.op("act", (lambda e, tc=tc: e.activation(out=Xc[:, tc, :], in_=cx.psb[tc][:, 0:64], func=AF.Copy)),
                  r=[("ps", tc)], w=[("Xc", tc)])
        for nt in range(2):
            bk = 2 + nt
            for tc in range(2):
                pg.op("pe", (lambda e, nt=nt, tc=tc, bk=bk: e.matmul(cx.psb[bk][0:32, 0:128], Xc[:, tc, 0:32],
                                                                    c256[:, tc, nt * 128:(nt + 1) * 128], start=(tc == 0), stop=False)),
                      r=[("Xc", 0), ("Xc", 1), ("c256",)], w=[("ps", bk)])
                pg.op("pe", (lambda e, nt=nt, tc=tc, bk=bk: e.matmul(cx.psb[bk][0:32, 0:128], Xc[:, tc, 32:64],
                                                                    s256[:, tc, nt * 128:(nt + 1) * 128], start=False, stop=(tc == 1))),
                      r=[("s256",)], w=[("ps", bk)])
            pg.op("act", (lambda e, nt=nt, bk=bk: e.activation(out=ycs[:, nt, :], in_=cx.psb[bk][0:32, 0:128], func=AF.Copy)),
                  r=[("ps", bk)], w=[("ycs", nt)])
        pg.dma("sp", YCp.rearrange("m (t n) -> m t n", t=2), ycs[:], r=[("ycs", 0), ("ycs", 1)], w=[("yc",)])
        pg.coll(piece2_t.ap().opt(), G2_t.ap().opt(), r=[("y",), ("yc",)], w=[("G2",)])
        pg.emit(cx.scratch)


def _fused_c(cx, sfx, l, depth, xs, pall, dout_i, GRO, GY, GYC, HT, wina, wbf_a, wbr_a, wbd_a, wo_a, w1a, w3a, w2a, fg_d, xf):
    nc, pg = cx.nc, cx.pg
    with ExitStack() as e2:
        norm_to_HT(cx, e2, xs, 1, HT)
        sb = lambda n_, s_, d_: e2.enter_context(nc.sbuf_tensor(n_ + sfx, s_, d_))
        wgt = sb("c_wgt", [128, 8, 3072], BF16)
        wb = [sb("c_wbf", [128, 2, D], BF16), sb("c_wbr", [128, 3, D], BF16), sb("c_wbd", [128, 3, D], BF16)]
        wo = sb("c_wo", [128, 8, D], BF16)
        dob = [sb("c_dob%d" % i, [128, 384], BF16) for i in range(1)] * 2
        doT = [sb("c_doT%d" % i, [128, 3, 128], BF16) for i in range(2)]
        g5 = [sb("c_g5_%d" % i, [128, D], F32) for i in range(2)]
        rfa = sb("c_rfa", [128, 6, 8, 64], F32)
        rba = sb("c_rba", [128, 6, 8, 64], F32)
        foTa = sb("c_foTa", [128, 2, 16, 128], BF16)
        rg = [sb("c_rg%d" % i, [128, 384], BF16) for i in range(2)]
        sg = [sb("c_sg%d" % i, [128, 384], F32) for i in range(1)] * 2
        rsq = sb("c_rsq", [128, 384], F32)
        rst = [sb("c_rst%d" % i, [128, 24], F32) for i in range(2)]
        rob = [sb("c_rob%d" % i, [128, 384], BF16) for i in range(2)]
        rT = [sb("c_rT%d" % i, [128, 3, 128], BF16) for i in range(2)]
        sig = sb("c_sig", [128, D], F32)
        prod = sb("c_prod", [128, D], F32)
        mixed = sb("c_mixed", [128, D], F32)
        mixb = sb("c_mixb", [128, D], BF16)
        mixT = sb("c_mixT", [128, 8, 128], BF16)
        xt = [sb("c_xt%d" % i, [128, D], F32) for i in range(1)]
        yt = sb("c_yt", [128, D], F32)
        for nb in range(6):
            pg.dma("pool", wgt[:, :, nb * 512:(nb + 1) * 512],
                   wina[l, :, NPA + nb * 512:NPA + (nb + 1) * 512].rearrange("(c p) n -> p c n", p=128), w=[("wgt", nb)])
        for i, wd in enumerate([wbf_a[l], wbr_a[l], wbd_a[l]]):
            pg.dma("pool", wb[i][:], wd.rearrange("(c p) n -> p c n", p=128), w=[("wb", i)])
        for hh in range(2):
            pg.dma("pool", wo[:, hh * 4:(hh + 1) * 4, :],
                   wo_a[l, hh * 512:(hh + 1) * 512, :].rearrange("(c p) n -> p c n", p=128), w=[("wo", hh)])
        for ty in range(2):
            pg.dma("sp", g5[ty][:], cx.mrows_d[ty, 5 * 1024:6 * 1024].partition_broadcast(128), w=[("g5", ty)])
        wgk = [("wgt", nb) for nb in range(6)]
        for t in range(NT):
            b = t % 2
            ty = typ_of(t)
            tsl = slice(t * 128, (t + 1) * 128)
            own = t < NTO
            if t in (0, 8, 16):
                hb0 = t
                for dr, dsta, key in ((0, rfa, "rfa"), (1, rba, "rba")):
                    if t < NTO:
                        pg.dmaf("sp", (lambda e, dv, dr=dr, dsta=dsta, hb0=hb0: e.dma_start(
                            out=dsta[:, :, :, :].rearrange("p h t d -> p h (t d)"),
                            in_=GRO[0:6, dr, :, hb0:, :][:, :, bass.ds(dv[0], 8), :].rearrange("h p n d -> p h (n d)"))),
                            r=[("G2",)], w=[(key,)])
                    else:
                        pg.dma("sp", dsta[:, :, 0:2, :].rearrange("p h t d -> p h (t d)"),
                               GRO[0:6, dr, :, 128:130, :].rearrange("h p n d -> p h (n d)"), r=[("G2",)], w=[(key,)])
            if t in (0, 16):
                for rr in range(NCORES):
                    dstf = foTa[(rr % 4) * 32:(rr % 4) * 32 + 32, rr // 4, :, :]
                    if t < NTO:
                        pg.dmaf("pool", (lambda e, dv, rr=rr, dstf=dstf: e.dma_start(
                            out=dstf, in_=GY[rr, bass.ds(dv[0], 16), :, :].rearrange("a m n -> m a n"))),
                            r=[("G2",)], w=[("foTa", rr)])
                    else:
                        pg.dma("pool", dstf[:, 0:2, :], GYC[rr, :, :, :], r=[("G2",)], w=[("foTa", rr)])
            tl = t % 8
            fok = [("foTa", rr) for rr in range(NCORES)]
            pg.dma("pool", dob[b][:], dout_i[tsl, :], r=[("do", t)], w=[("dob", 0)])
            pg.dma("sp", rg[b][:], pall[tsl, C_RG:C_RG + 384], r=[("pall", t)], w=[("rg", b)])
            rfk = [("rfa",)]
            rbk = [("rba",)]
            rfv = rfa[:, :, tl, :]
            rbv = rba[:, :, tl, :]
            pg.op("act", (lambda e, b=b: e.activation(out=sg[b][:], in_=rg[b][:], func=AF.Silu)), r=[("rg", b)], w=[("sg", 0)])
            pg.op("dve", (lambda e, rfv=rfv, rbv=rbv: e.tensor_tensor(out=rfv, in0=rfv, in1=rbv, op=ALU.add)),
                  r=rfk + rbk, w=rfk)
            pg.op("pool", (lambda e, rfv=rfv: e.tensor_tensor(out=rsq[:].rearrange("p (h d) -> p h d", d=64), in0=rfv, in1=rfv,
                                                            op=ALU.mult)), r=rfk, w=[("rsq",)])
            pg.op("dve", (lambda e, b=b: e.tensor_reduce(out=rst[b][:, 0:6], in_=rsq[:].rearrange("p (h d) -> p h d", d=64),
                                                         axis=AX.X, op=ALU.add)), r=[("rsq",)], w=[("rst", b)])
            pg.op("dve", (lambda e, b=b: e.tensor_scalar(rst[b][:, 6:12], rst[b][:, 0:6], 1.0 / 64, EPS, op0=ALU.mult, op1=ALU.add)),
                  r=[("rst", b)], w=[("rst", b)])
            pg.op("act", (lambda e, b=b: e.activation(out=rst[b][:, 12:18], in_=rst[b][:, 6:12], func=AF.Sqrt)),
                  r=[("rst", b)], w=[("rst", b)])
            pg.op("dve", (lambda e, b=b: e.reciprocal(rst[b][:, 18:24], rst[b][:, 12:18])), r=[("rst", b)], w=[("rst", b)])
            for h in range(6):
                pg.op("dve", (lambda e, b=b, h=h, tl=tl: e.scalar_tensor_tensor(
                    out=rob[b][:, h * 64:(h + 1) * 64], in0=rfa[:, h, tl, :], scalar=rst[b][:, 18 + h:19 + h],
                    in1=sg[b][:, h * 64:(h + 1) * 64], op0=ALU.mult, op1=ALU.mult)),
                    r=rfk + [("rst", b), ("sg", 0)], w=[("rob", b, h)])
            pst = cx.psb[7][:, :].bitcast(BF16)
            for k in range(3):
                pg.op("pe", (lambda e, k=k, b=b, pst=pst: e.transpose(pst[:, k * 128:(k + 1) * 128],
                                                                     rob[b][:, k * 128:(k + 1) * 128], cx.ident_bf[:])),
                      r=[("rob", b, 2 * k), ("rob", b, 2 * k + 1), ("ident_bf",)], w=[("ps", 7)])
            for k in range(3):
                pg.op("pe", (lambda e, k=k, b=b, pst=pst: e.transpose(pst[:, 384 + k * 128:384 + (k + 1) * 128],
                                                                     dob[b][:, k * 128:(k + 1) * 128], cx.ident_bf[:])),
                      r=[("dob", 0), ("ident_bf",)], w=[("ps", 7)])
            pg.op("act", (lambda e, b=b, pst=pst: e.activation(out=rT[b][:].rearrange("p a b -> p (a b)"), in_=pst[:, 0:384],
                                                               func=AF.Copy)), r=[("ps", 7)], w=[("rT", b)])
            pg.op("act", (lambda e, b=b, pst=pst: e.activation(out=doT[b][:].rearrange("p a b -> p (a b)"), in_=pst[:, 384:768],
                                                               func=AF.Copy)), r=[("ps", 7)], w=[("doT", b)])
            for br in range(3):
                kc = [2, 3, 3][br]
                gb = (0, 1)
                pb_ = (2, 3) if br % 2 == 0 else (4, 5)
                for hc in range(2):
                    for c in range(8):
                        pg.op("pe", (lambda e, c=c, hc=hc, br=br, tsl=tsl: e.matmul(
                            cx.psb[gb[hc]][:, :], HT[:, c, tsl], wgt[:, c, br * 1024 + hc * 512: br * 1024 + (hc + 1) * 512],
                            start=(c == 0), stop=(c == 7))),
                            r=(wgk + [("HT", cc, t) for cc in range(8)]) if c == 0 else [], w=[("ps", gb[hc])])
                for hc in range(2):
                    for k in range(kc):
                        if br == 0:
                            lh = foTa[:, k, t % 16, :]
                            rk_ = fok
                        elif br == 1:
                            lh = rT[b][:, k, :]
                            rk_ = [("rT", b)]
                        else:
                            lh = doT[b][:, k, :]
                            rk_ = [("doT", b)]
                        pg.op("pe", (lambda e, lh=lh, k=k, hc=hc, br=br, kc=kc, pbk=pb_[hc]: e.matmul(
                            cx.psb[pbk][:, :], lh, wb[br][:, k, hc * 512:(hc + 1) * 512], start=(k == 0), stop=(k == kc - 1))),
                            r=rk_ + [("wb", br)], w=[("ps", pb_[hc])])
                for hc in range(2):
                    pg.op("act", (lambda e, hc=hc: e.activation(out=sig[:, hc * 512:(hc + 1) * 512],
                                                                in_=cx.psb[gb[hc]][:, :], func=AF.Sigmoid)),
                          r=[("ps", gb[hc])], w=[("sig", hc)])
                    dst = mixed if br == 0 else prod
                    dk = ("mixed", hc) if br == 0 else ("prod", hc)
                    pg.op("dve", (lambda e, hc=hc, dst=dst, pbk=pb_[hc]: e.tensor_tensor(
                        out=dst[:, hc * 512:(hc + 1) * 512], in0=sig[:, hc * 512:(hc + 1) * 512], in1=cx.psb[pbk][:, :],
                        op=ALU.mult)), r=[("sig", hc), ("ps", pb_[hc])], w=[dk])
                    if br > 0:
                        pg.op("pool", (lambda e, hc=hc: e.tensor_tensor(
                            out=mixed[:, hc * 512:(hc + 1) * 512], in0=mixed[:, hc * 512:(hc + 1) * 512],
                            in1=prod[:, hc * 512:(hc + 1) * 512], op=ALU.add)),
                            r=[("mixed", hc), ("prod", hc)], w=[("mixed", hc)])
            pg.op("act", (lambda e: e.activation(out=mixb[:], in_=mixed[:], func=AF.Copy)),
                  r=[("mixed", 0), ("mixed", 1)], w=[("mixb",)])
            pst6 = cx.psb[6][:, :].bitcast(BF16)
            for c in range(8):
                pg.op("pe", (lambda e, c=c, pst6=pst6: e.transpose(pst6[:, c * 128:(c + 1) * 128],
                                                                  mixb[:, c * 128:(c + 1) * 128], cx.ident_bf[:])),
                      r=[("mixb",), ("ident_bf",)], w=[("ps", 6)])
            pg.op("act", (lambda e, pst6=pst6: e.activation(out=mixT[:].rearrange("p a b -> p (a b)"), in_=pst6[:, :],
                                                            func=AF.Copy)), r=[("ps", 6)], w=[("mixT",)])
            yb = (2, 3) if t % 2 == 1 else (4, 5)
            for hc in range(2):
                for c in range(8):
                    pg.op("pe", (lambda e, c=c, hc=hc, ybk=yb[hc]: e.matmul(
                        cx.psb[ybk][:, :], mixT[:, c, :], wo[:, c, hc * 512:(hc + 1) * 512], start=(c == 0), stop=(c == 7))),
                        r=[("mixT",), ("wo", 0), ("wo", 1)] if c == 0 else [], w=[("ps", yb[hc])])
            pg.dma("sp", xt[0][:], xs[tsl, :], r=[("xs", t)], w=[("c_xt", 0)])
            for hc in range(2):
                pg.op("dve", (lambda e, hc=hc, ty=ty, ybk=yb[hc]: e.tensor_tensor(
                    out=yt[:, hc * 512:(hc + 1) * 512], in0=cx.psb[ybk][:, :], in1=g5[ty][:, hc * 512:(hc + 1) * 512],
                    op=ALU.mult)), r=[("ps", yb[hc]), ("g5", ty)], w=[("c_yt", hc)])
            pg.op("pool", (lambda e, b=b: e.tensor_tensor(out=xt[0][:], in0=xt[0][:], in1=yt[:], op=ALU.add)),
                  r=[("c_yt", 0), ("c_yt", 1), ("c_xt", 0)], w=[("c_xt", 0)])
            pg.dma("sp", xs[tsl, :], xt[0][:], r=[("c_xt", 0)], w=[("xs", t)])
        pg.emit(cx.scratch)
    ffn(cx, xs, w1a[l, 1], w3a[l, 1], w2a[l, 1], 2, HT, "f2")
    if l == depth - 1:
        with ExitStack() as e3:
            sb = lambda n_, s_, d_: e3.enter_context(nc.sbuf_tensor(n_, s_, d_))
            fg = sb("z_fg", [128, D], F32)
            xt = [sb("z_xt%d" % i, [128, D], F32) for i in range(2)]
            junk = sb("z_junk", [128, D], BF16)
            st = [sb("z_st%d" % i, [128, 4], F32) for i in range(2)]
            pg.dma("sp", fg[:], fg_d, w=[("fg",)])
            for t in range(NTO):
                b = t % 2
                tsl = slice(t * 128, (t + 1) * 128)
                pg.dma("sp", xt[b][:], xs[tsl, :], r=[("xs", t)], w=[("z_xt", b)])
                pg.op("dve", (lambda e, b=b: e.memset(st[b][:], 0.0)), w=[("z_st", b)])
                pg.op("act", (lambda e, b=b: e.activation(out=junk[:], in_=xt[b][:], func=AF.Square, accum_out=st[b][:, 0:1])),
                      r=[("z_xt", b)], w=[("z_junk",), ("z_st", b)])
                pg.op("dve", (lambda e, b=b: e.tensor_scalar(st[b][:, 1:2], st[b][:, 0:1], 1.0 / D, EPS, op0=ALU.mult, op1=ALU.add)),
                      r=[("z_st", b)], w=[("z_st", b)])
                pg.op("act", (lambda e, b=b: e.activation(out=st[b][:, 2:3], in_=st[b][:, 1:2], func=AF.Sqrt)),
                      r=[("z_st", b)], w=[("z_st", b)])
                pg.op("dve", (lambda e, b=b: e.reciprocal(st[b][:, 3:4], st[b][:, 2:3])), r=[("z_st", b)], w=[("z_st", b)])
                pg.op("dve", (lambda e, b=b: e.scalar_tensor_tensor(out=xt[b][:], in0=xt[b][:], scalar=st[b][:, 3:4], in1=fg[:],
                                                                    op0=ALU.mult, op1=ALU.mult)),
                      r=[("z_xt", b), ("z_st", b), ("fg",)], w=[("z_xt", b)])
                pg.dma("sp", xf[tsl, :], xt[b][:], r=[("z_xt", b)], w=[("xf", t)])
            pg.emit(cx.scratch)


_FUSED = {}


def kernel(x, c, ctx, c_ctx, w_ada, b_ada, norm_g, ffn_w1, ffn_w3, ffn_w2, w_in, ret_decay_logit,
           diff_lambda, diff_subln, w_branch_f, w_branch_r, w_branch_d, w_out, final_g, depth=DEPTH):
    f32 = np.float32
    R = NCORES
    if depth not in _FUSED:
        _FUSED[depth] = build_fused(depth)
    nc = _FUSED[depth]
    A = lambda a: np.ascontiguousarray(np.asarray(a))
    cc = np.stack([np.asarray(c)[0], np.asarray(c_ctx)], 0)
    ccl = A(cc.reshape(2, 8, 128).transpose(2, 1, 0))
    cosL, sinL, cosC, sinC = _rope_tables()
    jj = np.arange(128)
    um = np.stack([np.maximum(jj[None, :] - jj[:, None], 0), np.maximum(jj[:, None] - jj[None, :], 0)], 0).astype(f32)
    mm = np.stack([(jj[None, :] >= jj[:, None]), (jj[:, None] >= jj[None, :])], 0).astype(f32)
    ir = np.stack([np.broadcast_to((jj + 1)[None, :], (128, 128)), np.broadcast_to((128 - jj)[None, :], (128, 128))], 0).astype(f32)
    pc = np.stack([(127 - jj)[:, None], jj[:, None]], 0).astype(f32)
    fftc = [_fft_consts(0), _fft_consts(1)]
    lam_init = [0.8 - 0.6 * math.exp(-0.3 * l) for l in range(DEPTH)]
    cst = np.stack([np.broadcast_to(np.array([lam_init[l], 1.0 - lam_init[l], 0, 0], f32)[None, :], (128, 4))
                    for l in range(DEPTH)], 0)
    shared = dict(
        cc=ccl, w_ada=A(w_ada), b_ada2=A(np.broadcast_to(np.asarray(b_ada)[:, None, :], (DEPTH, 2, 9216))),
        gcols=A(np.asarray(norm_g).reshape(DEPTH, 24, 128).transpose(0, 2, 1)),
        ffn_w1=A(ffn_w1), ffn_w3=A(ffn_w3), ffn_w2=A(ffn_w2), w_in=A(w_in),
        lamv=A(np.broadcast_to(np.asarray(diff_lambda).reshape(DEPTH, 1, 128), (DEPTH, 128, 128))),
        cst=A(cst), subln=A(np.broadcast_to(np.asarray(diff_subln)[:, None, :], (DEPTH, 128, 64))),
        umat=um, mmat=mm, irow=A(ir), pcol=pc, wbf=A(w_branch_f), wbr=A(w_branch_r), wbd=A(w_branch_d), wo=A(w_out),
        fgb=A(np.broadcast_to(np.asarray(final_g)[None, :], (128, D))), ident_in=np.eye(128, dtype=f32))
    x_lat = np.asarray(x)[0]
    x_ctx = np.asarray(ctx)[0]
    rdl = np.asarray(ret_decay_logit)
    maps = []
    for r in range(R):
        d_ = dict(shared)
        hsel = r % 6
        d_.update(fftc[r % 2])
        d_["x_in"] = A(np.concatenate([x_lat[r * OWN:(r + 1) * OWN], x_ctx], 0))
        d_["ropec"] = A(np.concatenate([cosL[r * OWN:(r + 1) * OWN], cosC], 0))
        d_["ropes"] = A(np.concatenate([sinL[r * OWN:(r + 1) * OWN], sinC], 0))
        d_["lgin"] = A(np.broadcast_to(rdl[:, :, hsel].reshape(DEPTH, 2, 1, 1), (DEPTH, 2, 128, 1)))
        d_["rankc"] = np.array([[r * 16, hsel * 32 + 256, hsel * 32 + 448, hsel * 128, (r // 2) * 64, 0, 0, 0]], np.int32)
        maps.append(d_)
    res = run_bass_kernel_spmd(nc, maps, core_ids=list(range(R)))
    out = np.concatenate([res.results[r]["xfin"] for r in range(R)], 0)
    return out[None].astype(np.float32)
```
